# Optimizing a Trainium2 kernel written in Bass

```python
import jax, jax.numpy as jnp
from jax import lax
import numpy as np

D_MODEL = 4096
BATCH = 4
SEQ = 4096
DEPTH = 4

N_A_LAYERS = DEPTH // 2
N_B_LAYERS = DEPTH - N_A_LAYERS

SSM_GROUP = 16
SSM_N_GROUPS = D_MODEL // SSM_GROUP
SSM_STATE = 64
SSM_CHUNK = 128
DT_MIN = 0.001
DT_MAX = 0.1

MLA_HEADS = 64
Q_RANK = 1024
KV_RANK = 512
QK_NOPE = 128
QK_ROPE = 64
V_DIM = 128
ROPE_THETA = 10000.0
Q_BLOCK = 128

D_FF = 11008
CONV_W = 3
EPS = 1e-6

kernel_name = "yoco_s5_mla_convffn_trunk"


def rms_norm(x, g):
    xf = x.astype(jnp.float32)
    y = xf * lax.rsqrt(jnp.mean(xf * xf, axis=-1, keepdims=True) + EPS)
    return (y * g.astype(jnp.float32)).astype(x.dtype)


def rope(x, positions):
    half = QK_ROPE // 2
    inv_freq = ROPE_THETA ** (-jnp.arange(half, dtype=jnp.float32) / half)
    ang = positions.astype(jnp.float32)[..., None] * inv_freq
    ang = ang.reshape(ang.shape[:2] + (1,) * (x.ndim - 3) + (half,))
    cos, sin = jnp.cos(ang), jnp.sin(ang)
    x1 = x[..., :half].astype(jnp.float32)
    x2 = x[..., half:].astype(jnp.float32)
    out = jnp.concatenate([x1 * cos - x2 * sin, x2 * cos + x1 * sin], axis=-1)
    return out.astype(x.dtype)


def _ssm_combine(e1, e2):
    a1r, a1i, b1r, b1i = e1
    a2r, a2i, b2r, b2i = e2
    return (a2r * a1r - a2i * a1i,
            a2r * a1i + a2i * a1r,
            a2r * b1r - a2i * b1i + b2r,
            a2r * b1i + a2i * b1r + b2i)


def s5_mixer(u, A_re, A_im, log_dt, B_re, B_im, C_re, C_im, D_skip, w_glu, b_glu):
    f32 = jnp.float32
    Bsz, S, _ = u.shape
    G, P, L = SSM_N_GROUPS, SSM_STATE, SSM_CHUNK
    dt = jnp.exp(log_dt.astype(f32))[:, None]
    ar, ai = A_re.astype(f32), A_im.astype(f32)
    mag = jnp.exp(ar * dt)
    lam_re = mag * jnp.cos(ai * dt)
    lam_im = mag * jnp.sin(ai * dt)
    den = ar * ar + ai * ai
    f_re = ((lam_re - 1.0) * ar + lam_im * ai) / den
    f_im = (lam_im * ar - (lam_re - 1.0) * ai) / den
    br, bi = B_re.astype(f32), B_im.astype(f32)
    Bb_re = f_re[..., None] * br - f_im[..., None] * bi
    Bb_im = f_re[..., None] * bi + f_im[..., None] * br
    cr, ci = C_re.astype(f32), C_im.astype(f32)

    n_chunks = S // L
    u_c = u.reshape(Bsz, n_chunks, L, G, SSM_GROUP).transpose(1, 0, 2, 3, 4)
    lam_re_b = jnp.broadcast_to(lam_re, (Bsz, L, G, P))
    lam_im_b = jnp.broadcast_to(lam_im, (Bsz, L, G, P))

    def chunk_step(carry, uc):
        h_re, h_im = carry
        uc = uc.astype(f32)
        bu_re = jnp.einsum('blgc,gpc->blgp', uc, Bb_re)
        bu_im = jnp.einsum('blgc,gpc->blgp', uc, Bb_im)
        a_re, a_im, s_re, s_im = lax.associative_scan(
            _ssm_combine, (lam_re_b, lam_im_b, bu_re, bu_im), axis=1)
        s_re = s_re + a_re * h_re[:, None] - a_im * h_im[:, None]
        s_im = s_im + a_re * h_im[:, None] + a_im * h_re[:, None]
        y = (jnp.einsum('blgp,gcp->blgc', s_re, cr)
             - jnp.einsum('blgp,gcp->blgc', s_im, ci))
        return (s_re[:, -1], s_im[:, -1]), y.reshape(Bsz, L, D_MODEL)

    h0 = (jnp.zeros((Bsz, G, P), f32), jnp.zeros((Bsz, G, P), f32))
    _, ys = lax.scan(chunk_step, h0, u_c)
    y = ys.transpose(1, 0, 2, 3).reshape(Bsz, S, D_MODEL)
    y = y + D_skip.astype(f32) * u.astype(f32)
    y = jax.nn.gelu(y).astype(u.dtype)
    z = y @ w_glu + b_glu
    return z[..., :D_MODEL] * jax.nn.sigmoid(z[..., D_MODEL:])


def conv_ffn(h, w_in, conv_w, conv_b, w_out):
    gu = h @ w_in
    gate, up = gu[..., :D_FF], gu[..., D_FF:]
    gate = lax.conv_general_dilated(
        gate, conv_w[:, None, :], window_strides=(1,), padding=[(CONV_W - 1, 0)],
        dimension_numbers=('NWC', 'WIO', 'NWC'), feature_group_count=D_FF) + conv_b
    return (jax.nn.silu(gate) * up) @ w_out


def mla_shared_kv(x_s, g_in, w_kv_a, g_kv, positions):
    h = rms_norm(x_s, g_in)
    kv = h @ w_kv_a
    c_kv = rms_norm(kv[..., :KV_RANK], g_kv)
    k_rope = rope(kv[..., KV_RANK:], positions)
    return c_kv, k_rope


def mla_layer(h, c_kv, k_rope, w_kv_b, positions, w_q_a, g_q, w_q_b, w_o):
    Bsz, S, _ = h.shape
    c_q = rms_norm(h @ w_q_a, g_q)
    wkv = w_kv_b.reshape(KV_RANK, MLA_HEADS, QK_NOPE + V_DIM)
    w_uk, w_uv = wkv[..., :QK_NOPE], wkv[..., QK_NOPE:]
    n_blk = S // Q_BLOCK
    cq_blk = c_q.reshape(Bsz, n_blk, Q_BLOCK, Q_RANK).transpose(1, 0, 2, 3)
    pos_blk = positions.reshape(Bsz, n_blk, Q_BLOCK).transpose(1, 0, 2)
    key_idx = jnp.arange(S)
    scale = (QK_NOPE + QK_ROPE) ** -0.5

    def attend(args):
        blk, cq, pos = args
        q = (cq @ w_q_b).reshape(Bsz, Q_BLOCK, MLA_HEADS, QK_NOPE + QK_ROPE)
        q_nope = q[..., :QK_NOPE]
        q_pe = rope(q[..., QK_NOPE:], pos)
        q_lat = jnp.einsum('bqhd,chd->bqhc', q_nope, w_uk)
        s = (jnp.einsum('bqhc,bkc->bhqk', q_lat, c_kv, preferred_element_type=jnp.float32)
             + jnp.einsum('bqhr,bkr->bhqk', q_pe, k_rope, preferred_element_type=jnp.float32))
        q_idx = blk * Q_BLOCK + jnp.arange(Q_BLOCK)
        mask = key_idx[None, :] <= q_idx[:, None]
        s = jnp.where(mask, s * scale, -jnp.inf)
        p = jax.nn.softmax(s, axis=-1).astype(c_kv.dtype)
        o_lat = jnp.einsum('bhqk,bkc->bqhc', p, c_kv)
        o = jnp.einsum('bqhc,chd->bqhd', o_lat, w_uv).reshape(Bsz, Q_BLOCK, MLA_HEADS * V_DIM)
        return o @ w_o

    out = lax.map(attend, (jnp.arange(n_blk), cq_blk, pos_blk))
    return out.transpose(1, 0, 2, 3).reshape(Bsz, S, D_MODEL)


def setup_inputs(seed: int = 0) -> dict:
    key = jax.random.key(seed)
    ks = jax.random.split(key, 32)
    f32 = jnp.float32
    nrm = lambda k, shape, s: jax.random.normal(k, shape, f32) * s
    G, P, C = SSM_N_GROUPS, SSM_STATE, SSM_GROUP
    x = jax.random.normal(ks[0], (BATCH, SEQ, D_MODEL), f32)
    offsets = jax.random.randint(ks[1], (BATCH, 1), 0, 1024, dtype=jnp.int32)
    positions = offsets + jnp.arange(SEQ, dtype=jnp.int32)[None, :]
    ssm_A_re = -0.5 + nrm(ks[5], (N_A_LAYERS, G, P), 0.01)
    ssm_A_im = (jnp.pi * jnp.arange(P, dtype=f32))[None, None, :] + nrm(ks[6], (N_A_LAYERS, G, P), 0.01)
    ssm_log_dt = jax.random.uniform(ks[7], (N_A_LAYERS, G), f32, np.log(DT_MIN), np.log(DT_MAX))
    return {
        "x": x,
        "positions": positions,
        "ln_mix": 1.0 + nrm(ks[2], (DEPTH, D_MODEL), 0.02),
        "ln_ffn": 1.0 + nrm(ks[3], (DEPTH, D_MODEL), 0.02),
        "ln_final": 1.0 + nrm(ks[4], (D_MODEL,), 0.02),
        "ssm_A_re": ssm_A_re,
        "ssm_A_im": ssm_A_im,
        "ssm_log_dt": ssm_log_dt,
        "ssm_B_re": nrm(ks[8], (N_A_LAYERS, G, P, C), (2.0 * C) ** -0.5),
        "ssm_B_im": nrm(ks[9], (N_A_LAYERS, G, P, C), (2.0 * C) ** -0.5),
        "ssm_C_re": nrm(ks[10], (N_A_LAYERS, G, C, P), P ** -0.5),
        "ssm_C_im": nrm(ks[11], (N_A_LAYERS, G, C, P), P ** -0.5),
        "ssm_D": nrm(ks[12], (N_A_LAYERS, D_MODEL), 1.0),
        "ssm_w_glu": nrm(ks[13], (N_A_LAYERS, D_MODEL, 2 * D_MODEL), D_MODEL ** -0.5),
        "ssm_b_glu": nrm(ks[14], (N_A_LAYERS, 2 * D_MODEL), 0.02),
        "kv_in_norm": 1.0 + nrm(ks[15], (D_MODEL,), 0.02),
        "w_kv_a": nrm(ks[16], (D_MODEL, KV_RANK + QK_ROPE), D_MODEL ** -0.5),
        "kv_latent_norm": 1.0 + nrm(ks[17], (KV_RANK,), 0.02),
        "w_kv_b": nrm(ks[18], (KV_RANK, MLA_HEADS * (QK_NOPE + V_DIM)), KV_RANK ** -0.5),
        "w_q_a": nrm(ks[19], (N_B_LAYERS, D_MODEL, Q_RANK), D_MODEL ** -0.5),
        "q_latent_norm": 1.0 + nrm(ks[20], (N_B_LAYERS, Q_RANK), 0.02),
        "w_q_b": nrm(ks[21], (N_B_LAYERS, Q_RANK, MLA_HEADS * (QK_NOPE + QK_ROPE)), Q_RANK ** -0.5),
        "w_o": nrm(ks[22], (N_B_LAYERS, MLA_HEADS * V_DIM, D_MODEL), (MLA_HEADS * V_DIM) ** -0.5),
        "ffn_w_in": nrm(ks[23], (DEPTH, D_MODEL, 2 * D_FF), D_MODEL ** -0.5),
        "ffn_conv_w": nrm(ks[24], (DEPTH, CONV_W, D_FF), CONV_W ** -0.5),
        "ffn_conv_b": nrm(ks[25], (DEPTH, D_FF), 0.02),
        "ffn_w_out": nrm(ks[26], (DEPTH, D_FF, D_MODEL), D_FF ** -0.5),
    }


def reference(x, positions, ln_mix, ln_ffn, ln_final,
              ssm_A_re, ssm_A_im, ssm_log_dt, ssm_B_re, ssm_B_im, ssm_C_re, ssm_C_im,
              ssm_D, ssm_w_glu, ssm_b_glu,
              kv_in_norm, w_kv_a, kv_latent_norm, w_kv_b,
              w_q_a, q_latent_norm, w_q_b, w_o,
              ffn_w_in, ffn_conv_w, ffn_conv_b, ffn_w_out):
    c_kv = None
    k_rope = None
    for l in range(DEPTH):
        h = rms_norm(x, ln_mix[l])
        if l < N_A_LAYERS:
            a = l
            x = x + s5_mixer(h, ssm_A_re[a], ssm_A_im[a], ssm_log_dt[a], ssm_B_re[a], ssm_B_im[a],
                             ssm_C_re[a], ssm_C_im[a], ssm_D[a], ssm_w_glu[a], ssm_b_glu[a])
        else:
            b = l - N_A_LAYERS
            x = x + mla_layer(h, c_kv, k_rope, w_kv_b, positions,
                              w_q_a[b], q_latent_norm[b], w_q_b[b], w_o[b])
        x = x + conv_ffn(rms_norm(x, ln_ffn[l]), ffn_w_in[l], ffn_conv_w[l], ffn_conv_b[l], ffn_w_out[l])
        if l == N_A_LAYERS - 1:
            c_kv, k_rope = mla_shared_kv(x, kv_in_norm, w_kv_a, kv_latent_norm, positions)
    return rms_norm(x, ln_final)
```

```python
import math
import numpy as np
import ml_dtypes
from contextlib import ExitStack
import concourse.bass as bass
import concourse.mybir as mybir
from concourse.bass_utils import run_bass_kernel_spmd


EPOCH = 24000
ENGS = ("pe", "dve", "act", "pool", "sp")


class Op:
    __slots__ = ("eng", "fn", "deps", "idx", "is_dma", "chan", "sig", "sigcnt", "chan_cnt", "inc")

    def __init__(self, eng, fn, is_dma=False, chan=None, inc=16):
        self.inc = inc
        self.eng = eng
        self.fn = fn
        self.deps = []
        self.idx = -1
        self.is_dma = is_dma
        self.chan = chan
        self.sig = False
        self.sigcnt = 0
        self.chan_cnt = 0


class Sched:
    def __init__(self, nc):
        self.nc = nc
        self.streams = {e: [] for e in ENGS}
        self.last_w = {}
        self.readers = {}
        self.chan_last = {}
        self.chan_count = {}
        self.seen = {e: {} for e in ENGS}
        self.seen_chan = {e: {} for e in ENGS}
        self.bar_ops = []
        self.bar_id = 0
        self.eng_bar = {e: 0 for e in ENGS}

    def _add_dep(self, op, d):
        if d is None or d is op:
            return
        if d.is_dma:
            cur = self.seen_chan[op.eng].get(d.chan, 0)
            if cur >= d.chan_cnt:
                return
            self.seen_chan[op.eng][d.chan] = d.chan_cnt
            op.deps.append(d)
            d.sig = True
            return
        if d.eng == op.eng and op.eng == "pe" and not op.is_dma:
            return
        cur = self.seen[op.eng].get(d.eng, -1)
        if cur >= d.idx:
            return
        self.seen[op.eng][d.eng] = d.idx
        op.deps.append(d)
        d.sig = True

    def barrier(self):
        ops = [st[-1] for st in self.streams.values() if st]
        for e in ENGS:
            for o in reversed(self.streams[e]):
                if not o.is_dma:
                    ops.append(o)
                    break
        ops += list(self.chan_last.values())
        self.bar_ops = ops
        self.bar_id += 1

    def op(self, eng, fn, reads=(), writes=(), dma=False, chan=None, inc=16):
        o = Op(eng, fn, is_dma=dma, chan=chan, inc=inc)
        st = self.streams[eng]
        o.idx = len(st)
        if self.eng_bar[eng] != self.bar_id:
            self.eng_bar[eng] = self.bar_id
            for d in self.bar_ops:
                self._add_dep(o, d)
        if dma:
            prev = self.chan_last.get(chan)
            self.chan_count[chan] = self.chan_count.get(chan, 0) + 1
            o.chan_cnt = self.chan_count[chan]
            if prev is not None:
                self._add_dep(o, prev)
            self.chan_last[chan] = o
            o.sig = True
        for k in reads:
            self._add_dep(o, self.last_w.get(k))
        for k in writes:
            self._add_dep(o, self.last_w.get(k))
            for r in self.readers.get(k, ()):
                self._add_dep(o, r)
        for k in reads:
            self.readers.setdefault(k, []).append(o)
        for k in writes:
            self.last_w[k] = o
            self.readers[k] = []
        st.append(o)
        return o

    def emit(self, final_waits=()):
        nc = self.nc
        nsig = {}
        for e in ENGS:
            c = 0
            for o in self.streams[e]:
                if o.is_dma:
                    continue
                if o.sig:
                    c += 1
                    o.sigcnt = c
            nsig[e] = c
        from contextlib import ExitStack
        with ExitStack() as es:
            esem = {}
            for e in ENGS:
                n_ep = (nsig[e] + EPOCH - 1) // EPOCH
                esem[e] = [es.enter_context(nc.semaphore(f"s_{e}_{i}")) for i in range(max(n_ep, 1))]
            csem = {ch: es.enter_context(nc.semaphore(f"c_{ch}")) for ch in self.chan_count}
            block = es.enter_context(nc.Block())
            engobj = {"pe": "tensor", "dve": "vector", "act": "scalar", "pool": "gpsimd", "sp": "sync"}

            def make(e):
                def body(eng):
                    for o in self.streams[e]:
                        for d in o.deps:
                            if d.is_dma:
                                eng.wait_ge(csem[d.chan], d.inc * d.chan_cnt)
                            else:
                                ep, v = divmod(d.sigcnt - 1, EPOCH)
                                eng.wait_ge(esem[d.eng][ep], v + 1)
                        ins = o.fn(eng)
                        if o.is_dma:
                            ins.then_inc(csem[o.chan], o.inc)
                        elif o.sig:
                            ep, v = divmod(o.sigcnt - 1, EPOCH)
                            ins.then_inc(esem[e][ep], 1)
                    if e == "sp":
                        for d in final_waits:
                            if d.is_dma:
                                eng.wait_ge(csem[d.chan], d.inc * d.chan_cnt)
                return body

            for e in ENGS:
                if not self.streams[e] and e != "sp":
                    continue
                getattr(block, engobj[e])(make(e))

F32 = mybir.dt.float32
BF16 = mybir.dt.bfloat16
I32 = mybir.dt.int32
AF = mybir.ActivationFunctionType
ALU = mybir.AluOpType

D = 4096
KC = 32
T = 512
FF = 11008
FC = 86
HF = 43
EPS = 1e-6
NBLK = 128
NH = 64
TWO_PI = 2.0 * math.pi
SCALE = 192.0 ** -0.5
ARENA_BYTES = 203264
PAIRS = [[0, 1], [2, 3], [4, 5], [6, 7]]
DTSZ = {F32: 4, BF16: 2, I32: 4}


class Ctx:
    def __init__(self):
        self.nc = bass.Bass("TRN2", target_bir_lowering=False)
        self.es = ExitStack()
        self.S = Sched(self.nc)
        self.banks = [self.es.enter_context(self.nc.psum_tensor(f"pb{i}", [128, 512], F32)) for i in range(8)]
        self.arena = self.es.enter_context(self.nc.sbuf_tensor("arena", [128, ARENA_BYTES // 2], BF16))
        self.aoff = 0
        self.outs = []
        self.wbufs = []
        self.wi = 0
        self.nph = 0

    def din(self, name, shape, dt=F32):
        return self.nc.dram_tensor(name, list(shape), dt, kind="ExternalInput").ap()

    def dout(self, name, shape, dt=F32):
        return self.nc.dram_tensor(name, list(shape), dt, kind="ExternalOutput").ap()

    def dscr(self, name, shape, dt=F32):
        return self.nc.dram_tensor(name, list(shape), dt).ap()

    def psb(self, name, shape, dt):
        return self.es.enter_context(self.nc.sbuf_tensor("sb_" + name, list(shape), dt))

    def sb(self, name, shape, dt):
        n = 1
        for v in shape[1:]:
            n *= v
        nb = n * DTSZ[dt]
        nb = (nb + 63) // 64 * 64
        off = self.aoff
        self.aoff += nb
        assert self.aoff <= ARENA_BYTES, (name, self.aoff)
        ap = self.arena[0:shape[0], off // 2:(off + n * DTSZ[dt]) // 2]
        if dt != BF16:
            ap = ap.bitcast(dt)
        if len(shape) == 3:
            ap = ap.rearrange("p (a b) -> p a b", a=shape[1])
        return ap

    def end_phase(self):
        self.aoff = 0
        self.wbufs = []
        self.S.barrier()

    def finish(self):
        self.S.emit(final_waits=self.outs)
        self.es.close()
        return self.nc

    def mm(self, out, lhsT, rhs, start, stop, reads, writes, **kw):
        return self.S.op("pe", lambda e: e.matmul(out, lhsT, rhs, start=start, stop=stop, **kw), reads=reads, writes=writes)

    def act(self, out, in_, func, reads, writes, **kw):
        return self.S.op("act", lambda e: e.activation(out=out, in_=in_, func=func, **kw), reads=reads, writes=writes)

    def tt(self, eng, out, in0, in1, op, reads, writes):
        return self.S.op(eng, lambda e: e.tensor_tensor(out=out, in0=in0, in1=in1, op=op), reads=reads, writes=writes)

    def ts(self, eng, out, in0, s1, s2, op0, op1, reads, writes):
        if s2 is None:
            return self.S.op(eng, lambda e: e.tensor_scalar(out=out, in0=in0, scalar1=s1, scalar2=None, op0=op0), reads=reads, writes=writes)
        return self.S.op(eng, lambda e: e.tensor_scalar(out=out, in0=in0, scalar1=s1, scalar2=s2, op0=op0, op1=op1), reads=reads, writes=writes)

    def stt(self, out, in0, scalar, in1, op0, op1, reads, writes):
        return self.S.op("dve", lambda e: e.scalar_tensor_tensor(out=out, in0=in0, scalar=scalar, in1=in1, op0=op0, op1=op1), reads=reads, writes=writes)

    def copy(self, eng, out, in_, reads, writes):
        return self.S.op(eng, lambda e: e.tensor_copy(out=out, in_=in_), reads=reads, writes=writes)

    def memset(self, ap, val, writes):
        return self.S.op("dve", lambda e: e.memset(ap, val), writes=writes)

    def recip(self, out, in_, reads, writes):
        return self.S.op("dve", lambda e: e.reciprocal(out=out, in_=in_), reads=reads, writes=writes)

    def scan(self, out, d0, d1, init, reads, writes):
        return self.S.op("dve", lambda e: e.tensor_tensor_scan(out=out, data0=d0, data1=d1, initial=init, op0=ALU.mult, op1=ALU.add), reads=reads, writes=writes)

    def dma(self, q, out, in_, reads, writes, chan):
        return self.S.op(q, lambda e: e.dma_start(out=out, in_=in_), reads=reads, writes=writes, dma=True, chan=chan)

    def store(self, out, in_, reads, chan, writes=()):
        o = self.dma("sp", out, in_, reads, list(writes), chan)
        self.outs.append(o)
        return o

    def allgather(self, src, dst, reads, writes, chan):
        o = self.S.op("pool", lambda e: e.collective_compute("AllGather", ALU.bypass, replica_groups=PAIRS, ins=[src.opt()], outs=[dst.opt()]),
                      reads=reads, writes=writes, dma=True, chan=chan, inc=1)
        self.outs.append(o)
        return o

    def wpool(self, nbuf, nelem):
        self.wbufs = [self.sb(f"wb{i}", [128, nelem], BF16) for i in range(nbuf)]
        self.wi = 0

    def wload(self, src, n):
        i = self.wi
        self.wi = (i + 1) % len(self.wbufs)
        buf = self.wbufs[i]
        self.dma("pool", buf[:, 0:n], src, [], [("wb", i)], f"wb{i}")
        return buf, ("wb", i)

    def setup(self):
        self.epsc = self.psb("epsc", [128, 1], F32)
        self.memset(self.epsc[:], EPS, ["epsc"])
        self.ones_bf = self.psb("ones_bf", [128, 128], BF16)
        self.memset(self.ones_bf[:], 1.0, ["ones"])
        self.sq = [self.psb(f"sq{i}", [128, T], BF16) for i in range(2)]
        self.rstd = self.psb("rstd", [128, T], F32)
        self.rtmp = self.psb("rtmp", [128, T], F32)
        self.flag = self.psb("flag", [128, 1], F32)
        self.fnflag = self.psb("fnflag", [128, 1], F32)

    def rmsnorm(self, x, xkey, g, gkey, out, okey, nkc, n, bank, bkey, dn):
        for c in range(nkc):
            s = self.sq[c % 2]
            self.act(s[:, 0:n], x[:, c, 0:n], AF.Square, [(xkey, c)], [("sq", c % 2)])
            self.mm(bank[:, 0:n], self.ones_bf[:], s[:, 0:n], c == 0, c == nkc - 1, ["ones", ("sq", c % 2)], [bkey])
        self.act(self.rtmp[:, 0:n], bank[:, 0:n], AF.Sqrt, [bkey, "epsc"], ["rtmp"], scale=1.0 / dn, bias=self.epsc[:, 0:1])
        self.recip(self.rstd[:, 0:n], self.rtmp[:, 0:n], ["rtmp"], ["rstd"])
        for c in range(nkc):
            self.stt(out[:, c, 0:n], x[:, c, 0:n], g[:, c:c + 1], self.rstd[:, 0:n], ALU.mult, ALU.mult,
                     [(xkey, c), gkey, "rstd"], [(okey, c)])


def fm(ap):
    return ap.rearrange("(c p) t -> p c t", p=128)


def sincos(cx, ang, akey, n, cos_out, ckey, sin_out, skey, scr, scrkey, scr2, scr2key):
    MAGIC = 12582912.0
    C1 = 6.28125
    C2 = TWO_PI - C1

    def reduce():
        cx.ts("dve", scr[:, 0:n], ang, 1.0 / TWO_PI, MAGIC, ALU.mult, ALU.add, [akey], [scrkey])
        cx.ts("dve", scr[:, 0:n], scr[:, 0:n], -MAGIC, None, ALU.add, None, [scrkey], [scrkey])
        cx.stt(scr2[:, 0:n], scr[:, 0:n], -C1, ang, ALU.mult, ALU.add, [scrkey, akey], [scr2key])
        cx.stt(scr2[:, 0:n], scr[:, 0:n], -C2, scr2[:, 0:n], ALU.mult, ALU.add, [scrkey, scr2key], [scr2key])
        cx.ts("dve", scr2[:, 0:n], scr2[:, 0:n], 3.1415925, -3.1415925, ALU.min, ALU.max, [scr2key], [scr2key])
    reduce()
    cx.act(sin_out, scr2[:, 0:n], AF.Sin, [scr2key], [skey])
    cx.ts("dve", ang, ang, math.pi / 2, None, ALU.add, None, [akey], [akey])
    reduce()
    cx.act(cos_out, scr2[:, 0:n], AF.Sin, [scr2key], [ckey])


def ffn_phase(cx, NT, src, dst, halo_all, W, l, final_norm, dst_is_out, dkey):
    nt = NT // T
    B = cx.banks
    cx.wpool(4, HF * 128)
    xs = cx.sb("xs", [128, KC, T], F32)
    xn = cx.sb("xn", [128, KC, T], BF16)
    h = cx.sb("h", [128, HF, T], BF16)
    xhs = cx.sb("xhs", [128, KC, 2], F32)
    xhn = cx.sb("xhn", [128, KC, 2], BF16)
    g = cx.sb("g_sb", [128, KC], F32)
    gf = cx.sb("gf_sb", [128, KC], F32)
    cw = cx.sb("cw_sb", [128, FC * 4], F32)
    carry = cx.sb("carry", [128, FC, 2], F32)
    G = [cx.sb(f"G{i}", [128, T + 2], F32) for i in range(2)]
    acc = [cx.sb(f"acc{i}", [128, T], F32) for i in range(2)]
    sg = [cx.sb(f"sg{i}", [128, T], BF16) for i in range(2)]
    srcv, dstv = fm(src), fm(dst)
    XHS = [("xhs", c) for c in range(KC)]
    cx.dma("sp", g[:], W["ffn_g"][l], [], ["g"], "c0")
    cx.dma("sp", gf[:], W["gfin"], [], ["gf"], "c1")
    cx.dma("sp", cw[:], W["ffn_cw"][l], [], ["cw"], "c2")
    cx.dma("sp", xhs, fm(halo_all[0:D, :]), ["HALO_all"], XHS, "c3")
    for c in range(KC):
        cx.ts("dve", xhs[:, c, :], xhs[:, c, :], cx.flag[:, 0:1], None, ALU.mult, None, [("xhs", c), "flag"], [("xhs", c)])
    cx.rmsnorm(xhs, "xhs", g, "g", xhn, "xhn", KC, 2, B[7], "b7", D)
    w_in, w_out = W["ffn_w_in"], W["ffn_w_out"]
    it = 0
    for ti in range(nt):
        t0 = ti * T
        cx.dma("sp", xs, srcv[:, :, t0:t0 + T], [("XM", ti)], [("xs", c) for c in range(KC)], "x0")
        cx.rmsnorm(xs, "xs", g, "g", xn, "xn", KC, T, B[6], "b6", D)
        for half in range(2):
            for fi in range(HF):
                fc = half * HF + fi
                k = it % 2
                it += 1
                wg, wgk = cx.wload(w_in[l, fc, 0], KC * 128)
                wu, wuk = cx.wload(w_in[l, fc, 1], KC * 128)
                bg, bu = B[k], B[2 + k]
                for c in range(KC):
                    cx.mm(bg[:], wg[:, c * 128:(c + 1) * 128], xn[:, c, :], c == 0, c == KC - 1, [wgk, ("xn", c)], [("bg", k)])
                Gk = G[k]
                if ti == 0:
                    for c in range(KC):
                        cx.mm(B[7][:, 0:2], wg[:, c * 128:(c + 1) * 128], xhn[:, c, :], c == 0, c == KC - 1, [wgk, ("xhn", c)], ["b7"])
                    cx.act(Gk[:, 0:2], B[7][:, 0:2], AF.Copy, ["b7"], [("G", k)])
                else:
                    cx.act(Gk[:, 0:2], carry[:, fc, :], AF.Copy, [("carry", fc)], [("G", k)])
                for c in range(KC):
                    cx.mm(bu[:], wu[:, c * 128:(c + 1) * 128], xn[:, c, :], c == 0, c == KC - 1, [wuk, ("xn", c)], [("bu", k)])
                cx.act(Gk[:, 2:T + 2], bg[:], AF.Copy, [("bg", k)], [("G", k)])
                cx.act(carry[:, fc, :], Gk[:, T:T + 2], AF.Copy, [("G", k)], [("carry", fc)])
                a = acc[k]
                o = fc * 4
                cx.ts("dve", a, Gk[:, 0:T], cw[:, o:o + 1], cw[:, o + 3:o + 4], ALU.mult, ALU.add, [("G", k), "cw"], [("acc", k)])
                cx.stt(a, Gk[:, 1:T + 1], cw[:, o + 1:o + 2], a, ALU.mult, ALU.add, [("G", k), "cw", ("acc", k)], [("acc", k)])
                cx.stt(a, Gk[:, 2:T + 2], cw[:, o + 2:o + 3], a, ALU.mult, ALU.add, [("G", k), "cw", ("acc", k)], [("acc", k)])
                cx.act(sg[k], a, AF.Silu, [("acc", k)], [("sg", k)])
                cx.tt("dve", h[:, fi, :], sg[k], bu[:], ALU.mult, [("sg", k), ("bu", k)], [("h", fi)])
            for dc in range(KC):
                k = dc % 2
                wo, wok = cx.wload(w_out[l, dc, half], HF * 128)
                bo = B[4 + k]
                for fi in range(HF):
                    cx.mm(bo[:], wo[:, fi * 128:(fi + 1) * 128], h[:, fi, :], fi == 0, fi == HF - 1, [wok, ("h", fi)], [("bo", k)])
                cx.tt("dve", xs[:, dc, :], xs[:, dc, :], bo[:], ALU.add, [("xs", dc), ("bo", k)], [("xs", dc)])
        if final_norm:
            for c in range(KC):
                sqb = cx.sq[c % 2]
                cx.act(sqb[:, 0:T], xs[:, c, :], AF.Square, [("xs", c)], [("sq", c % 2)])
                cx.mm(B[6][:], cx.ones_bf[:], sqb[:, 0:T], c == 0, c == KC - 1, ["ones", ("sq", c % 2)], ["b6"])
            cx.act(cx.rtmp[:, 0:T], B[6][:], AF.Sqrt, ["b6", "epsc"], ["rtmp"], scale=1.0 / D, bias=cx.epsc[:, 0:1])
            cx.recip(cx.rstd[:, 0:T], cx.rtmp[:, 0:T], ["rtmp"], ["rstd"])
            for c in range(KC):
                tn = acc[c % 2]
                tnk = ("acc", c % 2)
                cx.stt(tn, xs[:, c, :], gf[:, c:c + 1], cx.rstd[:, 0:T], ALU.mult, ALU.mult, [("xs", c), "gf", "rstd"], [tnk])
                cx.tt("dve", tn, tn, xs[:, c, :], ALU.subtract, [tnk, ("xs", c)], [tnk])
                cx.stt(xs[:, c, :], tn, cx.fnflag[:, 0:1], xs[:, c, :], ALU.mult, ALU.add, [tnk, "fnflag", ("xs", c)], [("xs", c)])
        XS = [("xs", c) for c in range(KC)]
        if dst_is_out:
            cx.store(dstv[:, :, t0:t0 + T], xs, XS, "y0")
        else:
            cx.dma("sp", dstv[:, :, t0:t0 + T], xs, XS, [(dkey, ti)], "y0")
    cx.end_phase()


def write_halo(cx, xs, XS, halo_in, halo_all):
    cx.dma("sp", fm(halo_in), xs[:, :, T - 2:T], XS, ["HALO_in"], "hw")
    cx.allgather(halo_in, halo_all, ["HALO_in"], ["HALO_all"], "ag_h")


def s5_phase(cx, NT, src, dst, W, a, l, scr):
    nc = cx.nc
    S = cx.S
    B = cx.banks
    TAB, ST_in, ST_all, halo_in, halo_all = scr["TAB"], scr["ST_in"], scr["ST_all"], scr["HALO_in"], scr["HALO_all"]
    cx.wpool(4, KC * 128)
    xs = cx.sb("xs", [128, KC, T], F32)
    un = cx.sb("un", [128, KC, T], BF16)
    g = cx.sb("g_sb", [128, KC], F32)
    dsk = cx.sb("dsk", [128, KC], F32)
    bgl = cx.sb("bgl", [128, 2 * KC], F32)
    iota = cx.sb("iota", [128, T], F32)
    sm = {n: cx.sb("sm_" + n, [128, NBLK], F32) for n in
          ["ar", "ai", "dt", "th", "R", "c", "s", "lr", "li", "den", "fre", "fim", "nfre", "t1", "t2", "hre", "him"]}
    tabs = [cx.sb(f"tab{i}", [128, 4, T], F32) for i in range(2)]
    tmp = {n: cx.sb("tmp_" + n, [128, T], F32) for n in ["t1", "t2", "t3", "t4", "gir", "gii", "gr", "gi", "yp", "y2"]}
    Rt = cx.sb("Rt", [128, T], F32)
    hb = [[cx.sb(f"hb{i}{k}", [128, T], BF16) for k in range(2)] for i in range(2)]
    ini = [[cx.sb(f"ini{i}{k}", [128, 1], F32) for k in range(2)] for i in range(2)]
    tiny = [cx.sb(f"tiny{i}", [128, 1], F32) for i in range(4)]
    ones512 = cx.sb("ones512", [128, T], F32)
    srcv, dstv = fm(src), fm(dst)
    m = sm
    cx.memset(ones512, 1.0, ["ones512"])
    cx.memset(m["hre"], 0.0, ["hre"])
    cx.memset(m["him"], 0.0, ["him"])
    cx.dma("sp", g, W["mix_g"][l], [], ["g"], "c0")
    cx.dma("sp", dsk, W["s5_dsk"][a], [], ["dsk"], "c1")
    cx.dma("sp", bgl, W["s5_bgl"][a], [], ["bgl"], "c2")
    cx.dma("sp", iota, W["iota"], [], ["iota"], "c3")
    cx.dma("sp", m["ar"], W["s5_ap_re"][a], [], ["ar"], "c4")
    cx.dma("sp", m["ai"], W["s5_ap_im"][a], [], ["ai"], "c5")
    cx.dma("sp", m["dt"], W["s5_ldt"][a], [], ["dt"], "c6")
    cx.act(m["dt"], m["dt"], AF.Exp, ["dt"], ["dt"])
    cx.tt("dve", m["th"], m["ai"], m["dt"], ALU.mult, ["ai", "dt"], ["th"])
    cx.tt("dve", m["t1"], m["ar"], m["dt"], ALU.mult, ["ar", "dt"], ["t1"])
    cx.act(m["R"], m["t1"], AF.Exp, ["t1"], ["R"])
    cx.copy("dve", m["hre"], m["th"], ["th"], ["hre"])
    sincos(cx, m["hre"], "hre", NBLK, m["c"], "c", m["s"], "s", m["t2"], "t2", m["him"], "him")
    S.op("dve", lambda e: e.memset(m["hre"], 0.0), reads=["hre"], writes=["hre"])
    S.op("dve", lambda e: e.memset(m["him"], 0.0), reads=["him"], writes=["him"])
    cx.tt("dve", m["lr"], m["R"], m["c"], ALU.mult, ["R", "c"], ["lr"])
    cx.tt("dve", m["li"], m["R"], m["s"], ALU.mult, ["R", "s"], ["li"])
    cx.ts("dve", m["lr"], m["lr"], -1.0, None, ALU.add, None, ["lr"], ["lr"])
    cx.tt("dve", m["den"], m["ar"], m["ar"], ALU.mult, ["ar"], ["den"])
    cx.tt("dve", m["t1"], m["ai"], m["ai"], ALU.mult, ["ai"], ["t1"])
    cx.tt("dve", m["den"], m["den"], m["t1"], ALU.add, ["den", "t1"], ["den"])
    cx.recip(m["den"], m["den"], ["den"], ["den"])
    cx.tt("dve", m["t1"], m["lr"], m["ar"], ALU.mult, ["lr", "ar"], ["t1"])
    cx.tt("dve", m["t2"], m["li"], m["ai"], ALU.mult, ["li", "ai"], ["t2"])
    cx.tt("dve", m["t1"], m["t1"], m["t2"], ALU.add, ["t1", "t2"], ["t1"])
    cx.tt("dve", m["fre"], m["t1"], m["den"], ALU.mult, ["t1", "den"], ["fre"])
    cx.tt("dve", m["t1"], m["li"], m["ar"], ALU.mult, ["li", "ar"], ["t1"])
    cx.tt("dve", m["t2"], m["lr"], m["ai"], ALU.mult, ["lr", "ai"], ["t2"])
    cx.tt("dve", m["t1"], m["t1"], m["t2"], ALU.subtract, ["t1", "t2"], ["t1"])
    cx.tt("dve", m["fim"], m["t1"], m["den"], ALU.mult, ["t1", "den"], ["fim"])
    cx.ts("dve", m["nfre"], m["fre"], -1.0, None, ALU.mult, None, ["fre"], ["nfre"])
    for blk in range(NBLK):
        k = blk % 2
        tb = tabs[k]
        tk = ("tab", k)
        ang = tmp["t1"] if k == 0 else tmp["t2"]
        ak = ("ang", k)
        scr1 = tmp["t3"] if k == 0 else tmp["t4"]
        scr2 = tmp["gr"] if k == 0 else tmp["gi"]
        cx.ts("dve", ang, iota, m["th"][:, blk:blk + 1], None, ALU.mult, None, ["iota", "th"], [ak])
        sincos(cx, ang, ak, T, tb[:, 2, :], tk, tb[:, 3, :], tk, scr1, ("scr", k), scr2, ("scr2", k))
        aa = tmp["gir"] if k == 0 else tmp["gii"]
        cx.ts("dve", aa, tb[:, 2, :], m["fre"][:, blk:blk + 1], None, ALU.mult, None, [tk, "fre"], [("a", k)])
        cx.stt(tb[:, 0, :], tb[:, 3, :], m["fim"][:, blk:blk + 1], aa, ALU.mult, ALU.add, [tk, "fim", ("a", k)], [tk])
        cx.ts("dve", aa, tb[:, 2, :], m["fim"][:, blk:blk + 1], None, ALU.mult, None, [tk, "fim"], [("a", k)])
        cx.stt(tb[:, 1, :], tb[:, 3, :], m["nfre"][:, blk:blk + 1], aa, ALU.mult, ALU.add, [tk, "nfre", ("a", k)], [tk])
        cx.dma("sp", TAB[blk], tb.rearrange("p a t -> p (a t)"), [tk], [("TAB", blk)], f"tw{k}")

    nt = NT // T
    tiles = [(True, i) for i in range(nt)] + [(False, i) for i in range(nt)]
    it = 0
    bp_in, cp_in, wg_in = W["s5_bpad"], W["s5_cpad"], W["s5_wglu"]
    for idx, (is_pre, ti) in enumerate(tiles):
        t0 = ti * T
        if idx == nt:
            cx.dma("sp", ST_in[0:128, :], m["hre"], ["hre"], ["ST_in"], "st0")
            cx.dma("sp", ST_in[128:256, :], m["him"], ["him"], ["ST_in"], "st1")
            cx.allgather(ST_in, ST_all, ["ST_in"], ["ST_all"], "ag_s")
            cx.dma("sp", m["hre"], ST_all[0:128, :], ["ST_all"], ["hre"], "st0")
            cx.dma("sp", m["him"], ST_all[128:256, :], ["ST_all"], ["him"], "st1")
            cx.ts("dve", m["hre"], m["hre"], cx.flag[:, 0:1], None, ALU.mult, None, ["hre", "flag"], ["hre"])
            cx.ts("dve", m["him"], m["him"], cx.flag[:, 0:1], None, ALU.mult, None, ["him", "flag"], ["him"])
        XS = [("xs", c) for c in range(KC)]
        cx.dma("sp", xs, srcv[:, :, t0:t0 + T], [("XA", ti)], XS, "x0")
        cx.rmsnorm(xs, "xs", g, "g", un, "un", KC, T, B[6], "b6", D)
        for cc in range(KC):
            bp, bpk = cx.wload(bp_in[a, cc], 1024)
            if not is_pre:
                cp, cpk = cx.wload(cp_in[a, cc], 1024)
            for j in range(4):
                blk = cc * 4 + j
                k = it % 2
                it += 1
                tb = tabs[k]
                tk = ("tab", k)
                cx.dma("sp", tb.rearrange("p a t -> p (a t)"), TAB[blk], [("TAB", blk)], [tk], f"tr{k}")
                bre, bim = B[2 * k], B[2 * k + 1]
                cx.mm(bre[:], bp[:, j * 128:(j + 1) * 128], un[:, cc, :], True, True, [bpk, ("un", cc)], [("bre", k)])
                cx.mm(bim[:], bp[:, 512 + j * 128:512 + (j + 1) * 128], un[:, cc, :], True, True, [bpk, ("un", cc)], [("bim", k)])
                t = tmp
                cx.tt("dve", t["t1"], bre[:], tb[:, 0, :], ALU.mult, [("bre", k), tk], ["t1"])
                cx.tt("dve", t["t2"], bim[:], tb[:, 1, :], ALU.mult, [("bim", k), tk], ["t2"])
                cx.tt("dve", t["gir"], t["t1"], t["t2"], ALU.subtract, ["t1", "t2"], ["gir"])
                cx.tt("dve", t["t3"], bre[:], tb[:, 1, :], ALU.mult, [("bre", k), tk], ["t3"])
                cx.tt("dve", t["t4"], bim[:], tb[:, 0, :], ALU.mult, [("bim", k), tk], ["t4"])
                cx.tt("dve", t["gii"], t["t3"], t["t4"], ALU.add, ["t3", "t4"], ["gii"])
                hre_c, him_c = m["hre"][:, blk:blk + 1], m["him"][:, blk:blk + 1]
                c1, s1 = tb[:, 2, 1:2], tb[:, 3, 1:2]
                ir, ii = ini[k]
                cx.tt("dve", tiny[0], him_c, s1, ALU.mult, ["him", tk], ["tiny0"])
                cx.stt(ir, hre_c, c1, tiny[0], ALU.mult, ALU.subtract, ["hre", tk, "tiny0"], [("ir", k)])
                cx.tt("dve", tiny[1], hre_c, s1, ALU.mult, ["hre", tk], ["tiny1"])
                cx.stt(ii, him_c, c1, tiny[1], ALU.mult, ALU.add, ["him", tk, "tiny1"], [("ii", k)])
                cx.ts("dve", Rt, ones512, m["R"][:, blk:blk + 1], None, ALU.mult, None, ["ones512", "R"], ["Rt"])
                cx.scan(t["gr"], Rt, t["gir"], ir[:, 0:1], ["Rt", "gir", ("ir", k)], ["gr"])
                cx.scan(t["gi"], Rt, t["gii"], ii[:, 0:1], ["Rt", "gii", ("ii", k)], ["gi"])
                cl, sl = tb[:, 2, T - 1:T], tb[:, 3, T - 1:T]
                grl, gil = t["gr"][:, T - 1:T], t["gi"][:, T - 1:T]
                cx.tt("dve", tiny[2], gil, sl, ALU.mult, ["gi", tk], ["tiny2"])
                cx.stt(hre_c, grl, cl, tiny[2], ALU.mult, ALU.subtract, ["gr", tk, "tiny2"], ["hre"])
                cx.tt("dve", tiny[3], grl, sl, ALU.mult, ["gr", tk], ["tiny3"])
                cx.stt(him_c, gil, cl, tiny[3], ALU.mult, ALU.add, ["gi", tk, "tiny3"], ["him"])
                if is_pre:
                    continue
                hr, hn = hb[k]
                cx.tt("dve", t["t1"], t["gr"], tb[:, 2, :], ALU.mult, ["gr", tk], ["t1"])
                cx.tt("dve", t["t2"], t["gi"], tb[:, 3, :], ALU.mult, ["gi", tk], ["t2"])
                cx.tt("dve", hr, t["t1"], t["t2"], ALU.subtract, ["t1", "t2"], [("hr", k)])
                cx.tt("dve", t["t3"], t["gi"], tb[:, 2, :], ALU.mult, ["gi", tk], ["t3"])
                cx.tt("dve", t["t4"], t["gr"], tb[:, 3, :], ALU.mult, ["gr", tk], ["t4"])
                cx.stt(hn, t["t3"], -1.0, t["t4"], ALU.mult, ALU.subtract, ["t3", "t4"], [("hn", k)])
                yb = B[4 + (cc % 2)]
                ybk = ("yb", cc % 2)
                cx.mm(yb[:], cp[:, j * 128:(j + 1) * 128], hr, j == 0, False, [cpk, ("hr", k)], [ybk])
                cx.mm(yb[:], cp[:, 512 + j * 128:512 + (j + 1) * 128], hn, False, j == 3, [cpk, ("hn", k)], [ybk])
            if is_pre:
                continue
            yp, y2 = tmp["yp"], tmp["y2"]
            cx.stt(yp, un[:, cc, :], dsk[:, cc:cc + 1], yb[:], ALU.mult, ALU.add, [("un", cc), "dsk", ybk], ["yp"])
            cx.tt("dve", y2, yp, yp, ALU.mult, ["yp"], ["y2"])
            cx.ts("dve", y2, y2, 0.044715, 1.0, ALU.mult, ALU.add, ["y2"], ["y2"])
            cx.tt("dve", y2, y2, yp, ALU.mult, ["y2", "yp"], ["y2"])
            cx.act(y2, y2, AF.Sigmoid, ["y2"], ["y2"], scale=1.5957691216057308)
            cx.tt("dve", un[:, cc, :], yp, y2, ALU.mult, ["yp", "y2"], [("un", cc)])
        if is_pre:
            continue
        for dc in range(KC):
            k = dc % 2
            wa, wak = cx.wload(wg_in[a, dc, 0], KC * 128)
            wgt, wgk = cx.wload(wg_in[a, dc, 1], KC * 128)
            ba_, bg_ = B[2 * k], B[2 * k + 1]
            for c in range(KC):
                cx.mm(ba_[:], wa[:, c * 128:(c + 1) * 128], un[:, c, :], c == 0, c == KC - 1, [wak, ("un", c)], [("bre", k)])
            for c in range(KC):
                cx.mm(bg_[:], wgt[:, c * 128:(c + 1) * 128], un[:, c, :], c == 0, c == KC - 1, [wgk, ("un", c)], [("bim", k)])
            sgm = tmp["t1"] if k == 0 else tmp["t2"]
            sk = "t1" if k == 0 else "t2"
            cx.act(sgm, bg_[:], AF.Sigmoid, [("bim", k), "bgl"], [sk], bias=bgl[:, KC + dc:KC + dc + 1])
            cx.stt(sgm, ba_[:], bgl[:, dc:dc + 1], sgm, ALU.add, ALU.mult, [("bre", k), "bgl", sk], [sk])
            cx.tt("dve", xs[:, dc, :], xs[:, dc, :], sgm, ALU.add, [("xs", dc), sk], [("xs", dc)])
        cx.dma("sp", dstv[:, :, t0:t0 + T], xs, XS, [("XM", ti)], "y0")
        if ti == nt - 1:
            write_halo(cx, xs, XS, halo_in, halo_all)
    cx.end_phase()


class Rope:
    def __init__(self, cx, pos_in, rc_in, NT):
        self.cx = cx
        self.posi = cx.sb("posi", [64, NT], I32)
        self.rc = cx.sb("rc", [64, 2], F32)
        self.ang = cx.sb("r_ang", [64, T], F32)
        self.s1 = cx.sb("r_s1", [64, T], F32)
        self.s2 = cx.sb("r_s2", [64, T], F32)
        self.cosT = cx.sb("r_cos", [64, T], F32)
        self.sinS = cx.sb("r_sin", [64, T], F32)
        self.t1 = cx.sb("r_t1", [64, T], F32)
        self.t2 = cx.sb("r_t2", [64, T], F32)
        cx.dma("sp", self.posi, pos_in, [], ["posi"], "r0")
        cx.dma("sp", self.rc, rc_in, [], ["rc"], "r1")

    def tables(self, t0):
        cx = self.cx
        cx.copy("dve", self.ang, self.posi[:, t0:t0 + T], ["posi"], ["r_ang"])
        cx.ts("dve", self.ang, self.ang, self.rc[:, 0:1], None, ALU.mult, None, ["r_ang", "rc"], ["r_ang"])
        sincos(cx, self.ang, "r_ang", T, self.cosT, "r_cos", self.sinS, "r_sin", self.s1, "r_s1", self.s2, "r_s2")
        cx.ts("dve", self.sinS, self.sinS, self.rc[:, 1:2], None, ALU.mult, None, ["r_sin", "rc"], ["r_sin"])

    def apply(self, out, okeys, A, akey, SW, swkey):
        cx = self.cx
        cx.tt("dve", self.t1, A, self.cosT, ALU.mult, [akey, "r_cos"], ["r_t1"])
        cx.tt("dve", self.t2, SW, self.sinS, ALU.mult, [swkey, "r_sin"], ["r_t2"])
        cx.tt("dve", out, self.t1, self.t2, ALU.add, ["r_t1", "r_t2"], okeys)


def kv_phase(cx, NT, src, W, scr):
    B = cx.banks
    KnT_own, KrT_own, Vh_own = scr["KnT_own"], scr["KrT_own"], scr["Vh_own"]
    cx.wpool(4, KC * 128)
    xs = cx.sb("xs", [128, KC, T], F32)
    xn = cx.sb("xn", [128, KC, T], BF16)
    g = cx.sb("g_sb", [128, KC], F32)
    gkv = cx.sb("gkv_sb", [128, 4], F32)
    ckv = cx.sb("ckv", [128, 4, T], F32)
    cn = cx.sb("cn", [128, 4, T], BF16)
    kr = cx.sb("kr", [64, T], BF16)
    stage = [cx.sb(f"stage{i}", [128, T], BF16) for i in range(2)]
    rope = Rope(cx, W["pos"], W["rc"], NT)
    cx.dma("sp", g, W["kv_g"], [], ["g"], "c0")
    cx.dma("sp", gkv, W["kv_gkv"], [], ["gkv"], "c1")
    srcv = fm(src)
    Vv = Vh_own.rearrange("(h p) (kc d) -> p h kc d", p=128, d=128)
    it = 0
    for ti in range(NT // T):
        t0 = ti * T
        cx.dma("sp", xs, srcv[:, :, t0:t0 + T], [("XA", ti)], [("xs", c) for c in range(KC)], "x0")
        cx.rmsnorm(xs, "xs", g, "g", xn, "xn", KC, T, B[7], "b7", D)
        rope.tables(t0)
        for oc in range(6):
            w, wk = cx.wload(W["kv_wa"][oc], KC * 128)
            bk = B[oc % 2]
            for c in range(KC):
                cx.mm(bk[:], w[:, c * 128:(c + 1) * 128], xn[:, c, :], c == 0, c == KC - 1, [wk, ("xn", c)], [("b", oc % 2)])
            if oc < 4:
                cx.act(ckv[:, oc, :], bk[:], AF.Copy, [("b", oc % 2)], [("ckv", oc)])
        rope.apply(kr, ["kr"], B[0][0:64, :], ("b", 0), B[1][0:64, :], ("b", 1))
        cx.dma("sp", KrT_own[:, t0:t0 + T], kr, ["kr"], ["KrT_own"], "y0")
        cx.rmsnorm(ckv, "ckv", gkv, "gkv", cn, "cn", 4, T, B[7], "b7", 512)
        for h in range(NH):
            k = it % 2
            it += 1
            w, wk = cx.wload(W["kv_wuk"][h], 512)
            bk = B[2 + k]
            for c in range(4):
                cx.mm(bk[:], w[:, c * 128:(c + 1) * 128], cn[:, c, :], c == 0, c == 3, [wk, ("cn", c)], [("b2", k)])
            cx.act(stage[k], bk[:], AF.Copy, [("b2", k)], [("stage", k)])
            cx.dma("sp", KnT_own[h * 128:(h + 1) * 128, t0:t0 + T], stage[k], [("stage", k)], ["KnT_own"], f"y1{k}")
        for nb in range(16):
            w, wk = cx.wload(W["kv_wuv"][nb], 2048)
            for tq in range(4):
                k = it % 2
                it += 1
                bk = B[2 + k]
                for c in range(4):
                    cx.mm(bk[:], cn[:, c, tq * 128:(tq + 1) * 128], w[:, c * 512:(c + 1) * 512], c == 0, c == 3, [wk, ("cn", c)], [("b2", k)])
                cx.act(stage[k], bk[:], AF.Copy, [("b2", k)], [("stage", k)])
                cx.dma("sp", Vv[:, 4 * nb:4 * nb + 4, ti * 4 + tq, :], stage[k].rearrange("p (h d) -> p h d", d=128),
                       [("stage", k)], ["Vh_own"], f"y1{k}")
    cx.allgather(KrT_own, scr["KrT_all"], ["KrT_own"], ["KrT_all"], "ag_r")
    for i in range(16):
        cx.allgather(KnT_own[i * 512:(i + 1) * 512, :], scr["KnT_all"][i], ["KnT_own"], ["KnT_all"], "ag_k")
        cx.allgather(Vh_own[i * 512:(i + 1) * 512, :], scr["Vh_all"][i], ["Vh_own"], ["Vh_all"], "ag_v")
    cx.end_phase()


def mla_phase(cx, NT, src, dst, W, b, l, scr):
    S = cx.S
    B = cx.banks
    NS = 2 * NT
    NKC = NS // 128
    OKC = NT // 128
    KnT_own, KnT_all, Vh_own, Vh_all = scr["KnT_own"], scr["KnT_all"], scr["Vh_own"], scr["Vh_all"]
    cx.wpool(3, KC * 128)
    big = cx.sb("big", [128, KC * T], F32)
    xs = big.rearrange("p (c t) -> p c t", c=KC)
    obuf = big.bitcast(BF16).rearrange("p (h t) -> p h t", h=NH)
    xnb = cx.sb("xn", [128, KC * T], BF16)
    xn = xnb.rearrange("p (c t) -> p c t", c=KC)
    g = cx.sb("g_sb", [128, KC], F32)
    gq = cx.sb("gq_sb", [128, 8], F32)
    cqs = cx.sb("cqs", [128, 8, T], F32)
    cqn = cx.sb("cqn", [128, 8, T], BF16)
    krt = cx.sb("krt", [64, NS], BF16)
    kb = cx.sb("kb", [128, NKC], F32)
    masks = cx.sb("masks", [128, 4 * T], BF16)
    qn = [cx.sb(f"qn{i}", [128, T], BF16) for i in range(2)]
    qr = [cx.sb(f"qr{i}", [64, T], BF16) for i in range(2)]
    pt = [cx.sb(f"pt{i}", [128, T], BF16) for i in range(2)]
    rden = cx.sb("rden", [128, T], F32)
    xc = [cx.sb(f"xc{i}", [128, T], F32) for i in range(2)]
    rope = Rope(cx, W["pos"], W["rc"], NT)
    cx.dma("sp", g, W["mix_g"][l], [], ["g"], "c0")
    cx.dma("sp", gq, W["mla_gq"][b], [], ["gq"], "c1")
    cx.dma("sp", krt[:, 0:NT], scr["KrT_all"][0:64, :], ["KrT_all"], ["krt"], "c2")
    cx.dma("sp", krt[:, NT:NS], scr["KrT_own"], ["KrT_own"], ["krt"], "c5")
    cx.dma("sp", kb, W["kbias"], [], ["kb"], "c3")
    cx.dma("sp", masks, W["masks"], [], ["masks"], "c4")
    srcv, dstv = fm(src), fm(dst)
    XS = [("xs", c) for c in range(KC)]
    it = 0
    nt = NT // T
    for ti in range(nt):
        t0 = ti * T
        q0 = NS - NT + t0
        nkc = (q0 + T) // 128
        cx.dma("sp", xs, srcv[:, :, t0:t0 + T], [("XA", ti)], XS, "x0")
        cx.rmsnorm(xs, "xs", g, "g", xn, "xn", KC, T, B[7], "b7", D)
        rope.tables(t0)
        for oc in range(8):
            w, wk = cx.wload(W["mla_wqa"][b, oc], KC * 128)
            bk = B[4 + oc % 2]
            for c in range(KC):
                cx.mm(bk[:], w[:, c * 128:(c + 1) * 128], xn[:, c, :], c == 0, c == KC - 1, [wk, ("xn", c)], [("b4", oc % 2)])
            cx.act(cqs[:, oc, :], bk[:], AF.Copy, [("b4", oc % 2)], [("cqs", oc)])
        cx.rmsnorm(cqs, "cqs", gq, "gq", cqn, "cqn", 8, T, B[7], "b7", 1024)
        for h in range(NH):
            hs = h % 2
            w, wk = cx.wload(W["mla_wqb"][b, h], 2048)
            for (bk, bkey, c0, mcols) in ((B[4], ("b4", 0), 0, 128), (B[5], ("b4", 1), 128, 64), (B[6], "q6", 192, 64)):
                for c in range(8):
                    cx.mm(bk[0:mcols, :], w[:, c * 256 + c0:c * 256 + c0 + mcols], cqn[:, c, :], c == 0, c == 7, [wk, ("cqn", c)], [bkey])
            cx.act(qn[hs], B[4][:], AF.Copy, [("b4", 0)], [("qn", hs)])
            rope.apply(qr[hs], [("qr", hs)], B[5][0:64, :], ("b4", 1), B[6][0:64, :], "q6")
            kvk = [("xn", c) for c in range(hs * 16, hs * 16 + 16)]
            kT = xnb[:, hs * 8192:hs * 8192 + nkc * 128]
            vv = xnb[:, hs * 8192 + 4096:hs * 8192 + 4096 + nkc * 128]
            nown = (nkc - OKC) * 128
            cx.dma("sp", kT[:, 0:NT], KnT_all[h // 4, (h % 4) * 128:(h % 4 + 1) * 128, :], ["KnT_all"], kvk, f"k{hs}")
            cx.dma("sp", kT[:, NT:NT + nown], KnT_own[h * 128:(h + 1) * 128, 0:nown], ["KnT_own"], kvk, f"k{hs}b")
            cx.dma("sp", vv[:, 0:NT], Vh_all[h // 4, (h % 4) * 128:(h % 4 + 1) * 128, :], ["Vh_all"], kvk, f"v{hs}")
            cx.dma("sp", vv[:, NT:NT + nown], Vh_own[h * 128:(h + 1) * 128, 0:nown], ["Vh_own"], kvk, f"v{hs}b")
            for kc in range(nkc):
                k = it % 2
                it += 1
                sb_ = B[k]
                cx.mm(sb_[:], kT[:, kc * 128:(kc + 1) * 128], qn[hs], True, False, kvk + [("qn", hs)], [("s", k)])
                cx.mm(sb_[:], krt[:, kc * 128:(kc + 1) * 128], qr[hs], False, True, ["krt", ("qr", hs)], [("s", k)])
                cx.act(pt[k], sb_[:], AF.Exp, [("s", k), "kb"], [("pt", k)], scale=SCALE, bias=kb[:, kc:kc + 1])
                off = kc * 128 - q0
                if off >= 0:
                    mi = off // 128
                    cx.tt("dve", pt[k], pt[k], masks[:, mi * T:(mi + 1) * T], ALU.mult, [("pt", k), "masks"], [("pt", k)])
                cx.mm(B[2][:], vv[:, kc * 128:(kc + 1) * 128], pt[k], kc == 0, kc == nkc - 1, kvk + [("pt", k)], ["o"])
                cx.mm(B[3][:], cx.ones_bf[:], pt[k], kc == 0, kc == nkc - 1, ["ones", ("pt", k)], ["den"])
            cx.recip(rden, B[3][:], ["den"], ["rden"])
            cx.tt("dve", obuf[:, h, :], B[2][:], rden, ALU.mult, ["o", "rden"], [("xs", h // 2)])
        for dc in range(KC):
            k = dc % 2
            bk = B[4 + k]
            for hh in range(2):
                w, wk = cx.wload(W["mla_wo"][b, dc][:, hh * 4096:(hh + 1) * 4096], 4096)
                for h2 in range(32):
                    h = hh * 32 + h2
                    cx.mm(bk[:], w[:, h2 * 128:(h2 + 1) * 128], obuf[:, h, :], h == 0, h == NH - 1, [wk, ("xs", h // 2)], [("b4", k)])
            cx.dma("sp", xc[k], srcv[:, dc, t0:t0 + T], [("XA", ti)], [("xc", k)], f"xc{k}")
            cx.tt("dve", xc[k], xc[k], bk[:], ALU.add, [("xc", k), ("b4", k)], [("xc", k)])
            cx.dma("sp", dstv[:, dc, t0:t0 + T], xc[k], [("xc", k)], [("XM", ti)], f"yo{k}")
            if ti == nt - 1:
                cx.dma("sp", fm(scr["HALO_in"])[:, dc, :], xc[k][:, T - 2:T], [("xc", k)], ["HALO_in"], f"hw{k}")
    cx.allgather(scr["HALO_in"], scr["HALO_all"], ["HALO_in"], ["HALO_all"], "ag_h")
    cx.end_phase()


def _scratch(cx, NT):
    return {
        "XA": cx.dscr("XA", [D, NT]), "XM": cx.dscr("XM", [D, NT]),
        "TAB": cx.dscr("TAB", [NBLK, 128, 4 * T]),
        "ST_in": cx.dscr("ST_in", [256, NBLK]), "ST_all": cx.dscr("ST_all", [512, NBLK]),
        "HALO_in": cx.dscr("HALO_in", [D, 2]), "HALO_all": cx.dscr("HALO_all", [2 * D, 2]),
    }


def _ffn_inputs(cx, W):
    W["ffn_g"] = cx.din("ffn_g", [1, 128, KC])
    W["gfin"] = cx.din("gfin", [128, KC])
    W["ffn_w_in"] = cx.din("ffn_w_in", [1, FC, 2, 128, KC * 128])
    W["ffn_w_out"] = cx.din("ffn_w_out", [1, KC, 2, 128, HF * 128])
    W["ffn_cw"] = cx.din("ffn_cw", [1, 128, FC * 4])


def build_s5_layer(NT):
    cx = Ctx()
    W = {}
    W["xT"] = cx.din("xT", [D, NT])
    W["flag"] = cx.din("flag", [128, 1])
    W["iota"] = cx.din("iota", [128, T])
    W["mix_g"] = cx.din("mix_g", [1, 128, KC])
    _ffn_inputs(cx, W)
    W["s5_ap_re"] = cx.din("s5_ap_re", [1, 128, NBLK])
    W["s5_ap_im"] = cx.din("s5_ap_im", [1, 128, NBLK])
    W["s5_ldt"] = cx.din("s5_ldt", [1, 128, NBLK])
    W["s5_bpad"] = cx.din("s5_bpad", [1, KC, 128, 1024])
    W["s5_cpad"] = cx.din("s5_cpad", [1, KC, 128, 1024])
    W["s5_dsk"] = cx.din("s5_dsk", [1, 128, KC])
    W["s5_wglu"] = cx.din("s5_wglu", [1, KC, 2, 128, KC * 128])
    W["s5_bgl"] = cx.din("s5_bgl", [1, 128, 2 * KC])
    yT = cx.dout("yT", [D, NT])
    scr = _scratch(cx, NT)
    cx.setup()
    cx.dma("sp", cx.flag[:], W["flag"], [], ["flag"], "fl")
    s5_phase(cx, NT, W["xT"], scr["XM"], W, 0, 0, scr)
    ffn_phase(cx, NT, scr["XM"], yT, scr["HALO_all"], W, 0, False, True, "XA")
    return cx.finish()


def build_mla_layer(NT):
    cx = Ctx()
    W = {}
    W["xT"] = cx.din("xT", [D, NT])
    W["x1T"] = cx.din("x1T", [D, NT])
    W["pos"] = cx.din("pos", [64, NT], I32)
    W["flag"] = cx.din("flag", [128, 1])
    W["fnflag"] = cx.din("fnflag", [128, 1])
    W["kbias"] = cx.din("kbias", [128, 2 * NT // 128])
    W["masks"] = cx.din("masks", [128, 4 * T], BF16)
    W["rc"] = cx.din("rc", [64, 2])
    W["mix_g"] = cx.din("mix_g", [1, 128, KC])
    _ffn_inputs(cx, W)
    W["kv_g"] = cx.din("kv_g", [128, KC])
    W["kv_gkv"] = cx.din("kv_gkv", [128, 4])
    W["kv_wa"] = cx.din("kv_wa", [6, 128, KC * 128])
    W["kv_wuk"] = cx.din("kv_wuk", [NH, 128, 512])
    W["kv_wuv"] = cx.din("kv_wuv", [16, 128, 2048])
    W["mla_gq"] = cx.din("mla_gq", [1, 128, 8])
    W["mla_wqa"] = cx.din("mla_wqa", [1, 8, 128, KC * 128])
    W["mla_wqb"] = cx.din("mla_wqb", [1, NH, 128, 2048])
    W["mla_wo"] = cx.din("mla_wo", [1, KC, 128, NH * 128])
    yT = cx.dout("yT", [D, NT])
    scr = _scratch(cx, NT)
    scr.update({
        "KnT_own": cx.dscr("KnT_own", [NH * 128, NT], BF16), "KnT_all": cx.dscr("KnT_all", [16, 1024, NT], BF16),
        "KrT_own": cx.dscr("KrT_own", [64, NT], BF16), "KrT_all": cx.dscr("KrT_all", [128, NT], BF16),
        "Vh_own": cx.dscr("Vh_own", [NH * 128, NT], BF16), "Vh_all": cx.dscr("Vh_all", [16, 1024, NT], BF16),
    })
    cx.setup()
    cx.dma("sp", cx.flag[:], W["flag"], [], ["flag"], "fl")
    cx.dma("sp", cx.fnflag[:], W["fnflag"], [], ["fnflag"], "fl2")
    kv_phase(cx, NT, W["x1T"], W, scr)
    mla_phase(cx, NT, W["xT"], scr["XM"], W, 0, 0, scr)
    ffn_phase(cx, NT, scr["XM"], yT, scr["HALO_all"], W, 0, True, True, "XA")
    return cx.finish()


def tile_w(w, cols_list):
    K = w.shape[0]
    out = []
    for cols in cols_list:
        sub = w[:, cols]
        m = sub.shape[1]
        out.append(np.ascontiguousarray(sub.reshape(K // 128, 128, m).transpose(1, 0, 2)).reshape(128, (K // 128) * m))
    return np.stack(out)


def pvec(v, n):
    return np.ascontiguousarray(np.asarray(v, np.float32).reshape(n, 128).T)


def prep_ffn(w_in, conv_w, conv_b, w_out):
    wi = w_in.reshape(KC, 128, 2, FC, 128)
    wi = np.ascontiguousarray(wi.transpose(3, 2, 1, 0, 4)).reshape(FC, 2, 128, KC * 128)
    wo = w_out.reshape(2, HF, 128, KC, 128)
    wo = np.ascontiguousarray(wo.transpose(3, 0, 2, 1, 4)).reshape(KC, 2, 128, HF * 128)
    cw = np.concatenate([conv_w, conv_b[None, :]], axis=0)
    cw = np.ascontiguousarray(cw.reshape(4, FC, 128).transpose(2, 1, 0)).reshape(128, FC * 4)
    return wi, wo, cw


def prep_s5(A_re, A_im, log_dt, B_re, B_im, C_re, C_im, D_skip, w_glu, b_glu):
    def pp(a):
        return np.ascontiguousarray(a.reshape(NBLK, 2, 64).transpose(1, 2, 0).reshape(128, NBLK))
    bpad = np.zeros((KC, 128, 2, 4, 128), np.float32)
    cpad = np.zeros((KC, 128, 2, 4, 128), np.float32)
    for part, (Bm, Cm) in enumerate([(B_re, C_re), (B_im, C_im)]):
        Bv = Bm.reshape(KC, 4, 2, 64, 16)
        Cv = Cm.reshape(KC, 4, 2, 16, 64)
        for j in range(4):
            for gl in range(2):
                r0 = 32 * j + 16 * gl
                bpad[:, r0:r0 + 16, part, j, gl * 64:(gl + 1) * 64] = Bv[:, j, gl].transpose(0, 2, 1)
                cpad[:, gl * 64:(gl + 1) * 64, part, j, r0:r0 + 16] = Cv[:, j, gl].transpose(0, 2, 1)
    wg = w_glu.reshape(KC, 128, 2, KC, 128)
    wg = np.ascontiguousarray(wg.transpose(3, 2, 1, 0, 4)).reshape(KC, 2, 128, KC * 128)
    return dict(ap_re=pp(A_re), ap_im=pp(A_im), ldt=pp(np.repeat(log_dt[:, None], 64, axis=1)),
                bpad=bpad.reshape(KC, 128, 1024), cpad=cpad.reshape(KC, 128, 1024),
                dsk=pvec(D_skip, KC), wglu=wg, bgl=pvec(b_glu, 2 * KC))


def kernel(x, positions, ln_mix, ln_ffn, ln_final,
           ssm_A_re, ssm_A_im, ssm_log_dt, ssm_B_re, ssm_B_im, ssm_C_re, ssm_C_im,
           ssm_D, ssm_w_glu, ssm_b_glu,
           kv_in_norm, w_kv_a, kv_latent_norm, w_kv_b,
           w_q_a, q_latent_norm, w_q_b, w_o,
           ffn_w_in, ffn_conv_w, ffn_conv_b, ffn_w_out):
    f32 = np.float32
    A = lambda v: np.asarray(v, f32)
    NT = 2048
    x = A(x)
    positions = np.asarray(positions, np.int32)
    W = {}
    W["mix_g"] = np.stack([pvec(ln_mix[l], KC) for l in range(4)])
    W["ffn_g"] = np.stack([pvec(ln_ffn[l], KC) for l in range(4)])
    W["gfin"] = pvec(ln_final, KC)
    ff = [prep_ffn(A(ffn_w_in[l]), A(ffn_conv_w[l]), A(ffn_conv_b[l]), A(ffn_w_out[l])) for l in range(4)]
    W["ffn_w_in"] = np.stack([f[0] for f in ff])
    W["ffn_w_out"] = np.stack([f[1] for f in ff])
    W["ffn_cw"] = np.stack([f[2] for f in ff])
    del ff
    s5 = [prep_s5(A(ssm_A_re[a]), A(ssm_A_im[a]), A(ssm_log_dt[a]), A(ssm_B_re[a]), A(ssm_B_im[a]), A(ssm_C_re[a]),
                  A(ssm_C_im[a]), A(ssm_D[a]), A(ssm_w_glu[a]), A(ssm_b_glu[a])) for a in range(2)]
    for k_, n_ in [("ap_re", "s5_ap_re"), ("ap_im", "s5_ap_im"), ("ldt", "s5_ldt"), ("bpad", "s5_bpad"), ("cpad", "s5_cpad"),
                   ("dsk", "s5_dsk"), ("wglu", "s5_wglu"), ("bgl", "s5_bgl")]:
        W[n_] = np.stack([s5[a][k_] for a in range(2)])
    del s5
    wkva = A(w_kv_a)
    wa_ext = np.concatenate([wkva, np.zeros((D, 64), f32)], axis=1)
    cols = [np.arange(oc * 128, (oc + 1) * 128) for oc in range(4)]
    cols.append(np.concatenate([np.arange(512, 576), np.arange(576, 640)]))
    cols.append(np.concatenate([np.arange(544, 576), np.arange(512, 544), np.arange(576, 640)]))
    W["kv_wa"] = tile_w(wa_ext, cols)
    wkv = A(w_kv_b).reshape(512, NH, 256)
    w_uk = np.ascontiguousarray(wkv[:, :, :128]).reshape(512, NH * 128)
    w_uv = np.ascontiguousarray(wkv[:, :, 128:]).reshape(512, NH * 128)
    W["kv_wuk"] = tile_w(w_uk, [np.arange(h * 128, (h + 1) * 128) for h in range(NH)])
    W["kv_wuv"] = tile_w(w_uv, [np.arange(nb * 512, (nb + 1) * 512) for nb in range(16)])
    W["kv_g"] = pvec(kv_in_norm, KC)
    W["kv_gkv"] = pvec(kv_latent_norm, 4)
    qcols = []
    for h in range(NH):
        b0 = h * 192
        qcols.append(np.concatenate([np.arange(b0, b0 + 192), np.arange(b0 + 160, b0 + 192), np.arange(b0 + 128, b0 + 160)]))
    W["mla_gq"] = np.stack([pvec(q_latent_norm[b], 8) for b in range(2)])
    W["mla_wqa"] = np.stack([tile_w(A(w_q_a[b]), [np.arange(oc * 128, (oc + 1) * 128) for oc in range(8)]) for b in range(2)])
    W["mla_wqb"] = np.stack([tile_w(A(w_q_b[b]), qcols) for b in range(2)])
    W["mla_wo"] = np.stack([tile_w(A(w_o[b]), [np.arange(dc * 128, (dc + 1) * 128) for dc in range(KC)]) for b in range(2)])
    invf = (10000.0 ** (-np.arange(32, dtype=np.float32) / 32)).astype(f32)
    rc = np.zeros((64, 2), f32)
    rc[:, 0] = np.concatenate([invf, invf])
    rc[:32, 1] = -1.0
    rc[32:, 1] = 1.0
    W["rc"] = rc
    W["iota"] = np.ascontiguousarray(np.broadcast_to(np.arange(T, dtype=f32), (128, T)))
    p_ = np.arange(128)[:, None]
    q_ = np.arange(T)[None, :]
    W["masks"] = np.concatenate([((q_ - p_) >= mi * 128) for mi in range(4)], axis=1).astype(ml_dtypes.bfloat16)
    cores = [(b, h) for b in range(4) for h in range(2)]
    ids = list(range(8))
    flags = [np.full((128, 1), float(h), f32) for (b, h) in cores]
    xs = [np.ascontiguousarray(x[b, h * NT:(h + 1) * NT].T) for (b, h) in cores]
    nc_s5 = build_s5_layer(NT)
    for l in range(2):
        Wl = {"iota": W["iota"], "gfin": W["gfin"], "mix_g": W["mix_g"][l:l + 1], "ffn_g": W["ffn_g"][l:l + 1],
              "ffn_w_in": W["ffn_w_in"][l:l + 1], "ffn_w_out": W["ffn_w_out"][l:l + 1], "ffn_cw": W["ffn_cw"][l:l + 1]}
        for n_ in ["s5_ap_re", "s5_ap_im", "s5_ldt", "s5_bpad", "s5_cpad", "s5_dsk", "s5_wglu", "s5_bgl"]:
            Wl[n_] = W[n_][l:l + 1]
        maps = [dict(Wl, xT=xs[i], flag=flags[i]) for i in range(8)]
        res = run_bass_kernel_spmd(nc_s5, maps, core_ids=ids).results
        xs = [np.ascontiguousarray(r["yT"]) for r in res]
        del maps, res, Wl
    x1 = xs
    nc_mla = build_mla_layer(NT)
    for l in range(2, 4):
        b_ = l - 2
        Wl = {"gfin": W["gfin"], "mix_g": W["mix_g"][l:l + 1], "ffn_g": W["ffn_g"][l:l + 1],
              "ffn_w_in": W["ffn_w_in"][l:l + 1], "ffn_w_out": W["ffn_w_out"][l:l + 1], "ffn_cw": W["ffn_cw"][l:l + 1],
              "rc": W["rc"], "masks": W["masks"], "kv_g": W["kv_g"], "kv_gkv": W["kv_gkv"], "kv_wa": W["kv_wa"],
              "kv_wuk": W["kv_wuk"], "kv_wuv": W["kv_wuv"], "mla_gq": W["mla_gq"][b_:b_ + 1], "mla_wqa": W["mla_wqa"][b_:b_ + 1],
              "mla_wqb": W["mla_wqb"][b_:b_ + 1], "mla_wo": W["mla_wo"][b_:b_ + 1]}
        fn = np.full((128, 1), 1.0 if l == 3 else 0.0, f32)
        maps = []
        for i, (b, h) in enumerate(cores):
            m = dict(Wl, xT=xs[i], x1T=x1[i], flag=flags[i], fnflag=fn)
            m["pos"] = np.ascontiguousarray(np.broadcast_to(positions[b, h * NT:(h + 1) * NT], (64, NT)))
            bias = np.zeros((2 * NT,), f32)
            if h == 0:
                bias[:NT] = -80.0
            m["kbias"] = np.ascontiguousarray(bias.reshape(2 * NT // 128, 128).T)
            maps.append(m)
        res = run_bass_kernel_spmd(nc_mla, maps, core_ids=ids).results
        xs = [np.ascontiguousarray(r["yT"]) for r in res]
        del maps, res, Wl
    out = np.empty((4, 2 * NT, D), f32)
    for i, (b, h) in enumerate(cores):
        out[b, h * NT:(h + 1) * NT] = xs[i].T
    return out
```

```python
import math
import numpy as np
import ml_dtypes
from contextlib import ExitStack
import concourse.bass as bass
import concourse.mybir as mybir
from concourse.bass_utils import run_bass_kernel_spmd


EPOCH = 24000
ENGS = ("pe", "dve", "act", "pool", "sp")


class Op:
    __slots__ = ("eng", "fn", "deps", "idx", "is_dma", "chan", "sig", "sigcnt", "chan_cnt", "inc")

    def __init__(self, eng, fn, is_dma=False, chan=None, inc=16):
        self.inc = inc
        self.eng = eng
        self.fn = fn
        self.deps = []
        self.idx = -1
        self.is_dma = is_dma
        self.chan = chan
        self.sig = False
        self.sigcnt = 0
        self.chan_cnt = 0


class Sched:
    def __init__(self, nc):
        self.nc = nc
        self.streams = {e: [] for e in ENGS}
        self.last_w = {}
        self.readers = {}
        self.chan_last = {}
        self.chan_count = {}
        self.seen = {e: {} for e in ENGS}
        self.seen_chan = {e: {} for e in ENGS}
        self.bar_ops = []
        self.bar_id = 0
        self.eng_bar = {e: 0 for e in ENGS}

    def _add_dep(self, op, d):
        if d is None or d is op:
            return
        if d.is_dma:
            cur = self.seen_chan[op.eng].get(d.chan, 0)
            if cur >= d.chan_cnt:
                return
            self.seen_chan[op.eng][d.chan] = d.chan_cnt
            op.deps.append(d)
            d.sig = True
            return
        if d.eng == op.eng and op.eng == "pe" and not op.is_dma:
            return
        cur = self.seen[op.eng].get(d.eng, -1)
        if cur >= d.idx:
            return
        self.seen[op.eng][d.eng] = d.idx
        op.deps.append(d)
        d.sig = True

    def barrier(self):
        ops = [st[-1] for st in self.streams.values() if st]
        for e in ENGS:
            for o in reversed(self.streams[e]):
                if not o.is_dma:
                    ops.append(o)
                    break
        ops += list(self.chan_last.values())
        self.bar_ops = ops
        self.bar_id += 1

    def op(self, eng, fn, reads=(), writes=(), dma=False, chan=None, inc=16):
        o = Op(eng, fn, is_dma=dma, chan=chan, inc=inc)
        st = self.streams[eng]
        o.idx = len(st)
        if self.eng_bar[eng] != self.bar_id:
            self.eng_bar[eng] = self.bar_id
            for d in self.bar_ops:
                self._add_dep(o, d)
        if dma:
            prev = self.chan_last.get(chan)
            self.chan_count[chan] = self.chan_count.get(chan, 0) + 1
            o.chan_cnt = self.chan_count[chan]
            if prev is not None:
                self._add_dep(o, prev)
            self.chan_last[chan] = o
            o.sig = True
        for k in reads:
            self._add_dep(o, self.last_w.get(k))
        for k in writes:
            self._add_dep(o, self.last_w.get(k))
            for r in self.readers.get(k, ()):
                self._add_dep(o, r)
        for k in reads:
            self.readers.setdefault(k, []).append(o)
        for k in writes:
            self.last_w[k] = o
            self.readers[k] = []
        st.append(o)
        return o

    def emit(self, final_waits=()):
        nc = self.nc
        nsig = {}
        for e in ENGS:
            c = 0
            for o in self.streams[e]:
                if o.is_dma:
                    continue
                if o.sig:
                    c += 1
                    o.sigcnt = c
            nsig[e] = c
        from contextlib import ExitStack
        with ExitStack() as es:
            esem = {}
            for e in ENGS:
                n_ep = (nsig[e] + EPOCH - 1) // EPOCH
                esem[e] = [es.enter_context(nc.semaphore(f"s_{e}_{i}")) for i in range(max(n_ep, 1))]
            csem = {ch: es.enter_context(nc.semaphore(f"c_{ch}")) for ch in self.chan_count}
            block = es.enter_context(nc.Block())
            engobj = {"pe": "tensor", "dve": "vector", "act": "scalar", "pool": "gpsimd", "sp": "sync"}

            def make(e):
                def body(eng):
                    for o in self.streams[e]:
                        for d in o.deps:
                            if d.is_dma:
                                eng.wait_ge(csem[d.chan], d.inc * d.chan_cnt)
                            else:
                                ep, v = divmod(d.sigcnt - 1, EPOCH)
                                eng.wait_ge(esem[d.eng][ep], v + 1)
                        ins = o.fn(eng)
                        if o.is_dma:
                            ins.then_inc(csem[o.chan], o.inc)
                        elif o.sig:
                            ep, v = divmod(o.sigcnt - 1, EPOCH)
                            ins.then_inc(esem[e][ep], 1)
                    if e == "sp":
                        for d in final_waits:
                            if d.is_dma:
                                eng.wait_ge(csem[d.chan], d.inc * d.chan_cnt)
                return body

            for e in ENGS:
                if not self.streams[e] and e != "sp":
                    continue
                getattr(block, engobj[e])(make(e))

F32 = mybir.dt.float32
BF16 = mybir.dt.bfloat16
I32 = mybir.dt.int32
AF = mybir.ActivationFunctionType
ALU = mybir.AluOpType

D = 4096
KC = 32
T = 512
FF = 11008
FC = 86
HF = 43
EPS = 1e-6
NBLK = 128
NH = 64
TWO_PI = 2.0 * math.pi
SCALE = 192.0 ** -0.5
ARENA_BYTES = 203264
PAIRS = [[0, 1], [2, 3], [4, 5], [6, 7]]
DTSZ = {F32: 4, BF16: 2, I32: 4}


class Ctx:
    def __init__(self):
        self.nc = bass.Bass("TRN2", target_bir_lowering=False)
        self.es = ExitStack()
        self.S = Sched(self.nc)
        self.banks = [self.es.enter_context(self.nc.psum_tensor(f"pb{i}", [128, 512], F32)) for i in range(8)]
        self.arena = self.es.enter_context(self.nc.sbuf_tensor("arena", [128, ARENA_BYTES // 2], BF16))
        self.aoff = 0
        self.outs = []
        self.wbufs = []
        self.wi = 0
        self.nph = 0

    def din(self, name, shape, dt=F32):
        return self.nc.dram_tensor(name, list(shape), dt, kind="ExternalInput").ap()

    def dout(self, name, shape, dt=F32):
        return self.nc.dram_tensor(name, list(shape), dt, kind="ExternalOutput").ap()

    def dscr(self, name, shape, dt=F32):
        return self.nc.dram_tensor(name, list(shape), dt).ap()

    def psb(self, name, shape, dt):
        return self.es.enter_context(self.nc.sbuf_tensor("sb_" + name, list(shape), dt))

    def sb(self, name, shape, dt):
        n = 1
        for v in shape[1:]:
            n *= v
        nb = n * DTSZ[dt]
        nb = (nb + 63) // 64 * 64
        off = self.aoff
        self.aoff += nb
        assert self.aoff <= ARENA_BYTES, (name, self.aoff)
        ap = self.arena[0:shape[0], off // 2:(off + n * DTSZ[dt]) // 2]
        if dt != BF16:
            ap = ap.bitcast(dt)
        if len(shape) == 3:
            ap = ap.rearrange("p (a b) -> p a b", a=shape[1])
        return ap

    def end_phase(self):
        self.aoff = 0
        self.wbufs = []
        self.S.barrier()

    def finish(self):
        self.S.emit(final_waits=self.outs)
        self.es.close()
        return self.nc

    def mm(self, out, lhsT, rhs, start, stop, reads, writes, **kw):
        return self.S.op("pe", lambda e: e.matmul(out, lhsT, rhs, start=start, stop=stop, **kw), reads=reads, writes=writes)

    def act(self, out, in_, func, reads, writes, **kw):
        return self.S.op("act", lambda e: e.activation(out=out, in_=in_, func=func, **kw), reads=reads, writes=writes)

    def tt(self, eng, out, in0, in1, op, reads, writes):
        return self.S.op(eng, lambda e: e.tensor_tensor(out=out, in0=in0, in1=in1, op=op), reads=reads, writes=writes)

    def ts(self, eng, out, in0, s1, s2, op0, op1, reads, writes):
        if s2 is None:
            return self.S.op(eng, lambda e: e.tensor_scalar(out=out, in0=in0, scalar1=s1, scalar2=None, op0=op0), reads=reads, writes=writes)
        return self.S.op(eng, lambda e: e.tensor_scalar(out=out, in0=in0, scalar1=s1, scalar2=s2, op0=op0, op1=op1), reads=reads, writes=writes)

    def stt(self, out, in0, scalar, in1, op0, op1, reads, writes):
        return self.S.op("dve", lambda e: e.scalar_tensor_tensor(out=out, in0=in0, scalar=scalar, in1=in1, op0=op0, op1=op1), reads=reads, writes=writes)

    def copy(self, eng, out, in_, reads, writes):
        return self.S.op(eng, lambda e: e.tensor_copy(out=out, in_=in_), reads=reads, writes=writes)

    def memset(self, ap, val, writes):
        return self.S.op("dve", lambda e: e.memset(ap, val), writes=writes)

    def recip(self, out, in_, reads, writes):
        return self.S.op("dve", lambda e: e.reciprocal(out=out, in_=in_), reads=reads, writes=writes)

    def scan(self, out, d0, d1, init, reads, writes):
        return self.S.op("dve", lambda e: e.tensor_tensor_scan(out=out, data0=d0, data1=d1, initial=init, op0=ALU.mult, op1=ALU.add), reads=reads, writes=writes)

    def dma(self, q, out, in_, reads, writes, chan):
        return self.S.op(q, lambda e: e.dma_start(out=out, in_=in_), reads=reads, writes=writes, dma=True, chan=chan)

    def store(self, out, in_, reads, chan, writes=()):
        o = self.dma("sp", out, in_, reads, list(writes), chan)
        self.outs.append(o)
        return o

    def allgather(self, src, dst, reads, writes, chan):
        o = self.S.op("pool", lambda e: e.collective_compute("AllGather", ALU.bypass, replica_groups=PAIRS, ins=[src.opt()], outs=[dst.opt()]),
                      reads=reads, writes=writes, dma=True, chan=chan, inc=1)
        self.outs.append(o)
        return o

    def wpool(self, nbuf, nelem):
        self.wbufs = [self.sb(f"wb{i}", [128, nelem], BF16) for i in range(nbuf)]
        self.wi = 0

    def wload(self, src, n):
        i = self.wi
        self.wi = (i + 1) % len(self.wbufs)
        buf = self.wbufs[i]
        self.dma("pool", buf[:, 0:n], src, [], [("wb", i)], f"wb{i}")
        return buf, ("wb", i)

    def setup(self):
        self.epsc = self.psb("epsc", [128, 1], F32)
        self.memset(self.epsc[:], EPS, ["epsc"])
        self.ones_bf = self.psb("ones_bf", [128, 128], BF16)
        self.memset(self.ones_bf[:], 1.0, ["ones"])
        self.sq = [self.psb(f"sq{i}", [128, T], BF16) for i in range(2)]
        self.rstd = self.psb("rstd", [128, T], F32)
        self.rtmp = self.psb("rtmp", [128, T], F32)
        self.flag = self.psb("flag", [128, 1], F32)
        self.fnflag = self.psb("fnflag", [128, 1], F32)

    def rmsnorm(self, x, xkey, g, gkey, out, okey, nkc, n, bank, bkey, dn):
        for c in range(nkc):
            s = self.sq[c % 2]
            self.act(s[:, 0:n], x[:, c, 0:n], AF.Square, [(xkey, c)], [("sq", c % 2)])
            self.mm(bank[:, 0:n], self.ones_bf[:], s[:, 0:n], c == 0, c == nkc - 1, ["ones", ("sq", c % 2)], [bkey])
        self.act(self.rtmp[:, 0:n], bank[:, 0:n], AF.Sqrt, [bkey, "epsc"], ["rtmp"], scale=1.0 / dn, bias=self.epsc[:, 0:1])
        self.recip(self.rstd[:, 0:n], self.rtmp[:, 0:n], ["rtmp"], ["rstd"])
        for c in range(nkc):
            self.stt(out[:, c, 0:n], x[:, c, 0:n], g[:, c:c + 1], self.rstd[:, 0:n], ALU.mult, ALU.mult,
                     [(xkey, c), gkey, "rstd"], [(okey, c)])


def fm(ap):
    return ap.rearrange("(c p) t -> p c t", p=128)


def sincos(cx, ang, akey, n, cos_out, ckey, sin_out, skey, scr, scrkey, scr2, scr2key):
    MAGIC = 12582912.0
    C1 = 6.28125
    C2 = TWO_PI - C1

    def reduce():
        cx.ts("dve", scr[:, 0:n], ang, 1.0 / TWO_PI, MAGIC, ALU.mult, ALU.add, [akey], [scrkey])
        cx.ts("dve", scr[:, 0:n], scr[:, 0:n], -MAGIC, None, ALU.add, None, [scrkey], [scrkey])
        cx.stt(scr2[:, 0:n], scr[:, 0:n], -C1, ang, ALU.mult, ALU.add, [scrkey, akey], [scr2key])
        cx.stt(scr2[:, 0:n], scr[:, 0:n], -C2, scr2[:, 0:n], ALU.mult, ALU.add, [scrkey, scr2key], [scr2key])
        cx.ts("dve", scr2[:, 0:n], scr2[:, 0:n], 3.1415925, -3.1415925, ALU.min, ALU.max, [scr2key], [scr2key])
    reduce()
    cx.act(sin_out, scr2[:, 0:n], AF.Sin, [scr2key], [skey])
    cx.ts("dve", ang, ang, math.pi / 2, None, ALU.add, None, [akey], [akey])
    reduce()
    cx.act(cos_out, scr2[:, 0:n], AF.Sin, [scr2key], [ckey])


def ffn_phase(cx, NT, src, dst, halo_all, W, l, final_norm, dst_is_out, dkey):
    nt = NT // T
    B = cx.banks
    cx.wpool(4, HF * 128)
    xs = cx.sb("xs", [128, KC, T], F32)
    xn = cx.sb("xn", [128, KC, T], BF16)
    h = cx.sb("h", [128, HF, T], BF16)
    xhs = cx.sb("xhs", [128, KC, 2], F32)
    xhn = cx.sb("xhn", [128, KC, 2], BF16)
    g = cx.sb("g_sb", [128, KC], F32)
    gf = cx.sb("gf_sb", [128, KC], F32)
    cw = cx.sb("cw_sb", [128, FC * 4], F32)
    carry = cx.sb("carry", [128, FC, 2], F32)
    G = [cx.sb(f"G{i}", [128, T + 2], F32) for i in range(2)]
    acc = [cx.sb(f"acc{i}", [128, T], F32) for i in range(2)]
    sg = [cx.sb(f"sg{i}", [128, T], BF16) for i in range(2)]
    srcv, dstv = fm(src), fm(dst)
    XHS = [("xhs", c) for c in range(KC)]
    cx.dma("sp", g[:], W["ffn_g"][l], [], ["g"], "c0")
    cx.dma("sp", gf[:], W["gfin"], [], ["gf"], "c1")
    cx.dma("sp", cw[:], W["ffn_cw"][l], [], ["cw"], "c2")
    cx.dma("sp", xhs, fm(halo_all[0:D, :]), ["HALO_all"], XHS, "c3")
    for c in range(KC):
        cx.ts("dve", xhs[:, c, :], xhs[:, c, :], cx.flag[:, 0:1], None, ALU.mult, None, [("xhs", c), "flag"], [("xhs", c)])
    cx.rmsnorm(xhs, "xhs", g, "g", xhn, "xhn", KC, 2, B[7], "b7", D)
    w_in, w_out = W["ffn_w_in"], W["ffn_w_out"]
    it = 0
    for ti in range(nt):
        t0 = ti * T
        cx.dma("sp", xs, srcv[:, :, t0:t0 + T], [("XM", ti)], [("xs", c) for c in range(KC)], "x0")
        cx.rmsnorm(xs, "xs", g, "g", xn, "xn", KC, T, B[6], "b6", D)
        for half in range(2):
            for fi in range(HF):
                fc = half * HF + fi
                k = it % 2
                it += 1
                wg, wgk = cx.wload(w_in[l, fc, 0], KC * 128)
                wu, wuk = cx.wload(w_in[l, fc, 1], KC * 128)
                bg, bu = B[k], B[2 + k]
                for c in range(KC):
                    cx.mm(bg[:], wg[:, c * 128:(c + 1) * 128], xn[:, c, :], c == 0, c == KC - 1, [wgk, ("xn", c)], [("bg", k)])
                Gk = G[k]
                if ti == 0:
                    for c in range(KC):
                        cx.mm(B[7][:, 0:2], wg[:, c * 128:(c + 1) * 128], xhn[:, c, :], c == 0, c == KC - 1, [wgk, ("xhn", c)], ["b7"])
                    cx.act(Gk[:, 0:2], B[7][:, 0:2], AF.Copy, ["b7"], [("G", k)])
                else:
                    cx.act(Gk[:, 0:2], carry[:, fc, :], AF.Copy, [("carry", fc)], [("G", k)])
                for c in range(KC):
                    cx.mm(bu[:], wu[:, c * 128:(c + 1) * 128], xn[:, c, :], c == 0, c == KC - 1, [wuk, ("xn", c)], [("bu", k)])
                cx.act(Gk[:, 2:T + 2], bg[:], AF.Copy, [("bg", k)], [("G", k)])
                cx.act(carry[:, fc, :], Gk[:, T:T + 2], AF.Copy, [("G", k)], [("carry", fc)])
                a = acc[k]
                o = fc * 4
                cx.ts("dve", a, Gk[:, 0:T], cw[:, o:o + 1], cw[:, o + 3:o + 4], ALU.mult, ALU.add, [("G", k), "cw"], [("acc", k)])
                cx.stt(a, Gk[:, 1:T + 1], cw[:, o + 1:o + 2], a, ALU.mult, ALU.add, [("G", k), "cw", ("acc", k)], [("acc", k)])
                cx.stt(a, Gk[:, 2:T + 2], cw[:, o + 2:o + 3], a, ALU.mult, ALU.add, [("G", k), "cw", ("acc", k)], [("acc", k)])
                cx.act(sg[k], a, AF.Silu, [("acc", k)], [("sg", k)])
                cx.tt("dve", h[:, fi, :], sg[k], bu[:], ALU.mult, [("sg", k), ("bu", k)], [("h", fi)])
            for dc in range(KC):
                k = dc % 2
                wo, wok = cx.wload(w_out[l, dc, half], HF * 128)
                bo = B[4 + k]
                for fi in range(HF):
                    cx.mm(bo[:], wo[:, fi * 128:(fi + 1) * 128], h[:, fi, :], fi == 0, fi == HF - 1, [wok, ("h", fi)], [("bo", k)])
                cx.tt("dve", xs[:, dc, :], xs[:, dc, :], bo[:], ALU.add, [("xs", dc), ("bo", k)], [("xs", dc)])
        if final_norm:
            for c in range(KC):
                sqb = cx.sq[c % 2]
                cx.act(sqb[:, 0:T], xs[:, c, :], AF.Square, [("xs", c)], [("sq", c % 2)])
                cx.mm(B[6][:], cx.ones_bf[:], sqb[:, 0:T], c == 0, c == KC - 1, ["ones", ("sq", c % 2)], ["b6"])
            cx.act(cx.rtmp[:, 0:T], B[6][:], AF.Sqrt, ["b6", "epsc"], ["rtmp"], scale=1.0 / D, bias=cx.epsc[:, 0:1])
            cx.recip(cx.rstd[:, 0:T], cx.rtmp[:, 0:T], ["rtmp"], ["rstd"])
            for c in range(KC):
                tn = acc[c % 2]
                tnk = ("acc", c % 2)
                cx.stt(tn, xs[:, c, :], gf[:, c:c + 1], cx.rstd[:, 0:T], ALU.mult, ALU.mult, [("xs", c), "gf", "rstd"], [tnk])
                cx.tt("dve", tn, tn, xs[:, c, :], ALU.subtract, [tnk, ("xs", c)], [tnk])
                cx.stt(xs[:, c, :], tn, cx.fnflag[:, 0:1], xs[:, c, :], ALU.mult, ALU.add, [tnk, "fnflag", ("xs", c)], [("xs", c)])
        XS = [("xs", c) for c in range(KC)]
        if dst_is_out:
            cx.store(dstv[:, :, t0:t0 + T], xs, XS, "y0")
        else:
            cx.dma("sp", dstv[:, :, t0:t0 + T], xs, XS, [(dkey, ti)], "y0")
    cx.end_phase()


def write_halo(cx, xs, XS, halo_in, halo_all):
    cx.dma("sp", fm(halo_in), xs[:, :, T - 2:T], XS, ["HALO_in"], "hw")
    cx.allgather(halo_in, halo_all, ["HALO_in"], ["HALO_all"], "ag_h")


def s5_phase(cx, NT, src, dst, W, a, l, scr):
    nc = cx.nc
    S = cx.S
    B = cx.banks
    TAB, ST_in, ST_all, halo_in, halo_all = scr["TAB"], scr["ST_in"], scr["ST_all"], scr["HALO_in"], scr["HALO_all"]
    cx.wpool(4, KC * 128)
    xs = cx.sb("xs", [128, KC, T], F32)
    un = cx.sb("un", [128, KC, T], BF16)
    g = cx.sb("g_sb", [128, KC], F32)
    dsk = cx.sb("dsk", [128, KC], F32)
    bgl = cx.sb("bgl", [128, 2 * KC], F32)
    iota = cx.sb("iota", [128, T], F32)
    sm = {n: cx.sb("sm_" + n, [128, NBLK], F32) for n in
          ["ar", "ai", "dt", "th", "R", "c", "s", "lr", "li", "den", "fre", "fim", "nfre", "t1", "t2", "hre", "him"]}
    tabs = [cx.sb(f"tab{i}", [128, 4, T], F32) for i in range(2)]
    tmp = {n: cx.sb("tmp_" + n, [128, T], F32) for n in ["t1", "t2", "t3", "t4", "gir", "gii", "gr", "gi", "yp", "y2"]}
    Rt = cx.sb("Rt", [128, T], F32)
    hb = [[cx.sb(f"hb{i}{k}", [128, T], BF16) for k in range(2)] for i in range(2)]
    ini = [[cx.sb(f"ini{i}{k}", [128, 1], F32) for k in range(2)] for i in range(2)]
    tiny = [cx.sb(f"tiny{i}", [128, 1], F32) for i in range(4)]
    ones512 = cx.sb("ones512", [128, T], F32)
    srcv, dstv = fm(src), fm(dst)
    m = sm
    cx.memset(ones512, 1.0, ["ones512"])
    cx.memset(m["hre"], 0.0, ["hre"])
    cx.memset(m["him"], 0.0, ["him"])
    cx.dma("sp", g, W["mix_g"][l], [], ["g"], "c0")
    cx.dma("sp", dsk, W["s5_dsk"][a], [], ["dsk"], "c1")
    cx.dma("sp", bgl, W["s5_bgl"][a], [], ["bgl"], "c2")
    cx.dma("sp", iota, W["iota"], [], ["iota"], "c3")
    cx.dma("sp", m["ar"], W["s5_ap_re"][a], [], ["ar"], "c4")
    cx.dma("sp", m["ai"], W["s5_ap_im"][a], [], ["ai"], "c5")
    cx.dma("sp", m["dt"], W["s5_ldt"][a], [], ["dt"], "c6")
    cx.act(m["dt"], m["dt"], AF.Exp, ["dt"], ["dt"])
    cx.tt("dve", m["th"], m["ai"], m["dt"], ALU.mult, ["ai", "dt"], ["th"])
    cx.tt("dve", m["t1"], m["ar"], m["dt"], ALU.mult, ["ar", "dt"], ["t1"])
    cx.act(m["R"], m["t1"], AF.Exp, ["t1"], ["R"])
    cx.copy("dve", m["hre"], m["th"], ["th"], ["hre"])
    sincos(cx, m["hre"], "hre", NBLK, m["c"], "c", m["s"], "s", m["t2"], "t2", m["him"], "him")
    S.op("dve", lambda e: e.memset(m["hre"], 0.0), reads=["hre"], writes=["hre"])
    S.op("dve", lambda e: e.memset(m["him"], 0.0), reads=["him"], writes=["him"])
    cx.tt("dve", m["lr"], m["R"], m["c"], ALU.mult, ["R", "c"], ["lr"])
    cx.tt("dve", m["li"], m["R"], m["s"], ALU.mult, ["R", "s"], ["li"])
    cx.ts("dve", m["lr"], m["lr"], -1.0, None, ALU.add, None, ["lr"], ["lr"])
    cx.tt("dve", m["den"], m["ar"], m["ar"], ALU.mult, ["ar"], ["den"])
    cx.tt("dve", m["t1"], m["ai"], m["ai"], ALU.mult, ["ai"], ["t1"])
    cx.tt("dve", m["den"], m["den"], m["t1"], ALU.add, ["den", "t1"], ["den"])
    cx.recip(m["den"], m["den"], ["den"], ["den"])
    cx.tt("dve", m["t1"], m["lr"], m["ar"], ALU.mult, ["lr", "ar"], ["t1"])
    cx.tt("dve", m["t2"], m["li"], m["ai"], ALU.mult, ["li", "ai"], ["t2"])
    cx.tt("dve", m["t1"], m["t1"], m["t2"], ALU.add, ["t1", "t2"], ["t1"])
    cx.tt("dve", m["fre"], m["t1"], m["den"], ALU.mult, ["t1", "den"], ["fre"])
    cx.tt("dve", m["t1"], m["li"], m["ar"], ALU.mult, ["li", "ar"], ["t1"])
    cx.tt("dve", m["t2"], m["lr"], m["ai"], ALU.mult, ["lr", "ai"], ["t2"])
    cx.tt("dve", m["t1"], m["t1"], m["t2"], ALU.subtract, ["t1", "t2"], ["t1"])
    cx.tt("dve", m["fim"], m["t1"], m["den"], ALU.mult, ["t1", "den"], ["fim"])
    cx.ts("dve", m["nfre"], m["fre"], -1.0, None, ALU.mult, None, ["fre"], ["nfre"])
    for blk in range(NBLK):
        k = blk % 2
        tb = tabs[k]
        tk = ("tab", k)
        ang = tmp["t1"] if k == 0 else tmp["t2"]
        ak = ("ang", k)
        scr1 = tmp["t3"] if k == 0 else tmp["t4"]
        scr2 = tmp["gr"] if k == 0 else tmp["gi"]
        cx.ts("dve", ang, iota, m["th"][:, blk:blk + 1], None, ALU.mult, None, ["iota", "th"], [ak])
        sincos(cx, ang, ak, T, tb[:, 2, :], tk, tb[:, 3, :], tk, scr1, ("scr", k), scr2, ("scr2", k))
        aa = tmp["gir"] if k == 0 else tmp["gii"]
        cx.ts("dve", aa, tb[:, 2, :], m["fre"][:, blk:blk + 1], None, ALU.mult, None, [tk, "fre"], [("a", k)])
        cx.stt(tb[:, 0, :], tb[:, 3, :], m["fim"][:, blk:blk + 1], aa, ALU.mult, ALU.add, [tk, "fim", ("a", k)], [tk])
        cx.ts("dve", aa, tb[:, 2, :], m["fim"][:, blk:blk + 1], None, ALU.mult, None, [tk, "fim"], [("a", k)])
        cx.stt(tb[:, 1, :], tb[:, 3, :], m["nfre"][:, blk:blk + 1], aa, ALU.mult, ALU.add, [tk, "nfre", ("a", k)], [tk])
        cx.dma("sp", TAB[blk], tb.rearrange("p a t -> p (a t)"), [tk], [("TAB", blk)], f"tw{k}")

    nt = NT // T
    tiles = [(True, i) for i in range(nt)] + [(False, i) for i in range(nt)]
    it = 0
    bp_in, cp_in, wg_in = W["s5_bpad"], W["s5_cpad"], W["s5_wglu"]
    for idx, (is_pre, ti) in enumerate(tiles):
        t0 = ti * T
        if idx == nt:
            cx.dma("sp", ST_in[0:128, :], m["hre"], ["hre"], ["ST_in"], "st0")
            cx.dma("sp", ST_in[128:256, :], m["him"], ["him"], ["ST_in"], "st1")
            cx.allgather(ST_in, ST_all, ["ST_in"], ["ST_all"], "ag_s")
            cx.dma("sp", m["hre"], ST_all[0:128, :], ["ST_all"], ["hre"], "st0")
            cx.dma("sp", m["him"], ST_all[128:256, :], ["ST_all"], ["him"], "st1")
            cx.ts("dve", m["hre"], m["hre"], cx.flag[:, 0:1], None, ALU.mult, None, ["hre", "flag"], ["hre"])
            cx.ts("dve", m["him"], m["him"], cx.flag[:, 0:1], None, ALU.mult, None, ["him", "flag"], ["him"])
        XS = [("xs", c) for c in range(KC)]
        cx.dma("sp", xs, srcv[:, :, t0:t0 + T], [("XA", ti)], XS, "x0")
        cx.rmsnorm(xs, "xs", g, "g", un, "un", KC, T, B[6], "b6", D)
        for cc in range(KC):
            bp, bpk = cx.wload(bp_in[a, cc], 1024)
            if not is_pre:
                cp, cpk = cx.wload(cp_in[a, cc], 1024)
            for j in range(4):
                blk = cc * 4 + j
                k = it % 2
                it += 1
                tb = tabs[k]
                tk = ("tab", k)
                cx.dma("sp", tb.rearrange("p a t -> p (a t)"), TAB[blk], [("TAB", blk)], [tk], f"tr{k}")
                bre, bim = B[2 * k], B[2 * k + 1]
                cx.mm(bre[:], bp[:, j * 128:(j + 1) * 128], un[:, cc, :], True, True, [bpk, ("un", cc)], [("bre", k)])
                cx.mm(bim[:], bp[:, 512 + j * 128:512 + (j + 1) * 128], un[:, cc, :], True, True, [bpk, ("un", cc)], [("bim", k)])
                t = tmp
                cx.tt("dve", t["t1"], bre[:], tb[:, 0, :], ALU.mult, [("bre", k), tk], ["t1"])
                cx.tt("dve", t["t2"], bim[:], tb[:, 1, :], ALU.mult, [("bim", k), tk], ["t2"])
                cx.tt("dve", t["gir"], t["t1"], t["t2"], ALU.subtract, ["t1", "t2"], ["gir"])
                cx.tt("dve", t["t3"], bre[:], tb[:, 1, :], ALU.mult, [("bre", k), tk], ["t3"])
                cx.tt("dve", t["t4"], bim[:], tb[:, 0, :], ALU.mult, [("bim", k), tk], ["t4"])
                cx.tt("dve", t["gii"], t["t3"], t["t4"], ALU.add, ["t3", "t4"], ["gii"])
                hre_c, him_c = m["hre"][:, blk:blk + 1], m["him"][:, blk:blk + 1]
                c1, s1 = tb[:, 2, 1:2], tb[:, 3, 1:2]
                ir, ii = ini[k]
                cx.tt("dve", tiny[0], him_c, s1, ALU.mult, ["him", tk], ["tiny0"])
                cx.stt(ir, hre_c, c1, tiny[0], ALU.mult, ALU.subtract, ["hre", tk, "tiny0"], [("ir", k)])
                cx.tt("dve", tiny[1], hre_c, s1, ALU.mult, ["hre", tk], ["tiny1"])
                cx.stt(ii, him_c, c1, tiny[1], ALU.mult, ALU.add, ["him", tk, "tiny1"], [("ii", k)])
                cx.ts("dve", Rt, ones512, m["R"][:, blk:blk + 1], None, ALU.mult, None, ["ones512", "R"], ["Rt"])
                cx.scan(t["gr"], Rt, t["gir"], ir[:, 0:1], ["Rt", "gir", ("ir", k)], ["gr"])
                cx.scan(t["gi"], Rt, t["gii"], ii[:, 0:1], ["Rt", "gii", ("ii", k)], ["gi"])
                cl, sl = tb[:, 2, T - 1:T], tb[:, 3, T - 1:T]
                grl, gil = t["gr"][:, T - 1:T], t["gi"][:, T - 1:T]
                cx.tt("dve", tiny[2], gil, sl, ALU.mult, ["gi", tk], ["tiny2"])
                cx.stt(hre_c, grl, cl, tiny[2], ALU.mult, ALU.subtract, ["gr", tk, "tiny2"], ["hre"])
                cx.tt("dve", tiny[3], grl, sl, ALU.mult, ["gr", tk], ["tiny3"])
                cx.stt(him_c, gil, cl, tiny[3], ALU.mult, ALU.add, ["gi", tk, "tiny3"], ["him"])
                if is_pre:
                    continue
                hr, hn = hb[k]
                cx.tt("dve", t["t1"], t["gr"], tb[:, 2, :], ALU.mult, ["gr", tk], ["t1"])
                cx.tt("dve", t["t2"], t["gi"], tb[:, 3, :], ALU.mult, ["gi", tk], ["t2"])
                cx.tt("dve", hr, t["t1"], t["t2"], ALU.subtract, ["t1", "t2"], [("hr", k)])
                cx.tt("dve", t["t3"], t["gi"], tb[:, 2, :], ALU.mult, ["gi", tk], ["t3"])
                cx.tt("dve", t["t4"], t["gr"], tb[:, 3, :], ALU.mult, ["gr", tk], ["t4"])
                cx.stt(hn, t["t3"], -1.0, t["t4"], ALU.mult, ALU.subtract, ["t3", "t4"], [("hn", k)])
                yb = B[4 + (cc % 2)]
                ybk = ("yb", cc % 2)
                cx.mm(yb[:], cp[:, j * 128:(j + 1) * 128], hr, j == 0, False, [cpk, ("hr", k)], [ybk])
                cx.mm(yb[:], cp[:, 512 + j * 128:512 + (j + 1) * 128], hn, False, j == 3, [cpk, ("hn", k)], [ybk])
            if is_pre:
                continue
            yp, y2 = tmp["yp"], tmp["y2"]
            cx.stt(yp, un[:, cc, :], dsk[:, cc:cc + 1], yb[:], ALU.mult, ALU.add, [("un", cc), "dsk", ybk], ["yp"])
            cx.tt("dve", y2, yp, yp, ALU.mult, ["yp"], ["y2"])
            cx.ts("dve", y2, y2, 0.044715, 1.0, ALU.mult, ALU.add, ["y2"], ["y2"])
            cx.tt("dve", y2, y2, yp, ALU.mult, ["y2", "yp"], ["y2"])
            cx.act(y2, y2, AF.Sigmoid, ["y2"], ["y2"], scale=1.5957691216057308)
            cx.tt("dve", un[:, cc, :], yp, y2, ALU.mult, ["yp", "y2"], [("un", cc)])
        if is_pre:
            continue
        for dc in range(KC):
            k = dc % 2
            wa, wak = cx.wload(wg_in[a, dc, 0], KC * 128)
            wgt, wgk = cx.wload(wg_in[a, dc, 1], KC * 128)
            ba_, bg_ = B[2 * k], B[2 * k + 1]
            for c in range(KC):
                cx.mm(ba_[:], wa[:, c * 128:(c + 1) * 128], un[:, c, :], c == 0, c == KC - 1, [wak, ("un", c)], [("bre", k)])
            for c in range(KC):
                cx.mm(bg_[:], wgt[:, c * 128:(c + 1) * 128], un[:, c, :], c == 0, c == KC - 1, [wgk, ("un", c)], [("bim", k)])
            sgm = tmp["t1"] if k == 0 else tmp["t2"]
            sk = "t1" if k == 0 else "t2"
            cx.act(sgm, bg_[:], AF.Sigmoid, [("bim", k), "bgl"], [sk], bias=bgl[:, KC + dc:KC + dc + 1])
            cx.stt(sgm, ba_[:], bgl[:, dc:dc + 1], sgm, ALU.add, ALU.mult, [("bre", k), "bgl", sk], [sk])
            cx.tt("dve", xs[:, dc, :], xs[:, dc, :], sgm, ALU.add, [("xs", dc), sk], [("xs", dc)])
        cx.dma("sp", dstv[:, :, t0:t0 + T], xs, XS, [("XM", ti)], "y0")
        if ti == nt - 1:
            write_halo(cx, xs, XS, halo_in, halo_all)
    cx.end_phase()


class Rope:
    def __init__(self, cx, pos_in, rc_in, NT):
        self.cx = cx
        self.posi = cx.sb("posi", [64, NT], I32)
        self.rc = cx.sb("rc", [64, 2], F32)
        self.ang = cx.sb("r_ang", [64, T], F32)
        self.s1 = cx.sb("r_s1", [64, T], F32)
        self.s2 = cx.sb("r_s2", [64, T], F32)
        self.cosT = cx.sb("r_cos", [64, T], F32)
        self.sinS = cx.sb("r_sin", [64, T], F32)
        self.t1 = cx.sb("r_t1", [64, T], F32)
        self.t2 = cx.sb("r_t2", [64, T], F32)
        cx.dma("sp", self.posi, pos_in, [], ["posi"], "r0")
        cx.dma("sp", self.rc, rc_in, [], ["rc"], "r1")

    def tables(self, t0):
        cx = self.cx
        cx.copy("dve", self.ang, self.posi[:, t0:t0 + T], ["posi"], ["r_ang"])
        cx.ts("dve", self.ang, self.ang, self.rc[:, 0:1], None, ALU.mult, None, ["r_ang", "rc"], ["r_ang"])
        sincos(cx, self.ang, "r_ang", T, self.cosT, "r_cos", self.sinS, "r_sin", self.s1, "r_s1", self.s2, "r_s2")
        cx.ts("dve", self.sinS, self.sinS, self.rc[:, 1:2], None, ALU.mult, None, ["r_sin", "rc"], ["r_sin"])

    def apply(self, out, okeys, A, akey, SW, swkey):
        cx = self.cx
        cx.tt("dve", self.t1, A, self.cosT, ALU.mult, [akey, "r_cos"], ["r_t1"])
        cx.tt("dve", self.t2, SW, self.sinS, ALU.mult, [swkey, "r_sin"], ["r_t2"])
        cx.tt("dve", out, self.t1, self.t2, ALU.add, ["r_t1", "r_t2"], okeys)


def kv_phase(cx, NT, src, W, scr):
    B = cx.banks
    KnT_own, KrT_own, Vh_own = scr["KnT_own"], scr["KrT_own"], scr["Vh_own"]
    cx.wpool(4, KC * 128)
    xs = cx.sb("xs", [128, KC, T], F32)
    xn = cx.sb("xn", [128, KC, T], BF16)
    g = cx.sb("g_sb", [128, KC], F32)
    gkv = cx.sb("gkv_sb", [128, 4], F32)
    ckv = cx.sb("ckv", [128, 4, T], F32)
    cn = cx.sb("cn", [128, 4, T], BF16)
    kr = cx.sb("kr", [64, T], BF16)
    stage = [cx.sb(f"stage{i}", [128, T], BF16) for i in range(2)]
    rope = Rope(cx, W["pos"], W["rc"], NT)
    cx.dma("sp", g, W["kv_g"], [], ["g"], "c0")
    cx.dma("sp", gkv, W["kv_gkv"], [], ["gkv"], "c1")
    srcv = fm(src)
    Vv = Vh_own.rearrange("(h p) (kc d) -> p h kc d", p=128, d=128)
    it = 0
    for ti in range(NT // T):
        t0 = ti * T
        cx.dma("sp", xs, srcv[:, :, t0:t0 + T], [("XA", ti)], [("xs", c) for c in range(KC)], "x0")
        cx.rmsnorm(xs, "xs", g, "g", xn, "xn", KC, T, B[7], "b7", D)
        rope.tables(t0)
        for oc in range(6):
            w, wk = cx.wload(W["kv_wa"][oc], KC * 128)
            bk = B[oc % 2]
            for c in range(KC):
                cx.mm(bk[:], w[:, c * 128:(c + 1) * 128], xn[:, c, :], c == 0, c == KC - 1, [wk, ("xn", c)], [("b", oc % 2)])
            if oc < 4:
                cx.act(ckv[:, oc, :], bk[:], AF.Copy, [("b", oc % 2)], [("ckv", oc)])
        rope.apply(kr, ["kr"], B[0][0:64, :], ("b", 0), B[1][0:64, :], ("b", 1))
        cx.dma("sp", KrT_own[:, t0:t0 + T], kr, ["kr"], ["KrT_own"], "y0")
        cx.rmsnorm(ckv, "ckv", gkv, "gkv", cn, "cn", 4, T, B[7], "b7", 512)
        for h in range(NH):
            k = it % 2
            it += 1
            w, wk = cx.wload(W["kv_wuk"][h], 512)
            bk = B[2 + k]
            for c in range(4):
                cx.mm(bk[:], w[:, c * 128:(c + 1) * 128], cn[:, c, :], c == 0, c == 3, [wk, ("cn", c)], [("b2", k)])
            cx.act(stage[k], bk[:], AF.Copy, [("b2", k)], [("stage", k)])
            cx.dma("sp", KnT_own[h * 128:(h + 1) * 128, t0:t0 + T], stage[k], [("stage", k)], ["KnT_own"], f"y1{k}")
        for nb in range(16):
            w, wk = cx.wload(W["kv_wuv"][nb], 2048)
            for tq in range(4):
                k = it % 2
                it += 1
                bk = B[2 + k]
                for c in range(4):
                    cx.mm(bk[:], cn[:, c, tq * 128:(tq + 1) * 128], w[:, c * 512:(c + 1) * 512], c == 0, c == 3, [wk, ("cn", c)], [("b2", k)])
                cx.act(stage[k], bk[:], AF.Copy, [("b2", k)], [("stage", k)])
                cx.dma("sp", Vv[:, 4 * nb:4 * nb + 4, ti * 4 + tq, :], stage[k].rearrange("p (h d) -> p h d", d=128),
                       [("stage", k)], ["Vh_own"], f"y1{k}")
    cx.allgather(KrT_own, scr["KrT_all"], ["KrT_own"], ["KrT_all"], "ag_r")
    for i in range(16):
        cx.allgather(KnT_own[i * 512:(i + 1) * 512, :], scr["KnT_all"][i], ["KnT_own"], ["KnT_all"], "ag_k")
        cx.allgather(Vh_own[i * 512:(i + 1) * 512, :], scr["Vh_all"][i], ["Vh_own"], ["Vh_all"], "ag_v")
    cx.end_phase()


def mla_phase(cx, NT, src, dst, W, b, l, scr):
    S = cx.S
    B = cx.banks
    NS = 2 * NT
    NKC = NS // 128
    OKC = NT // 128
    KnT_own, KnT_all, Vh_own, Vh_all = scr["KnT_own"], scr["KnT_all"], scr["Vh_own"], scr["Vh_all"]
    cx.wpool(3, KC * 128)
    big = cx.sb("big", [128, KC * T], F32)
    xs = big.rearrange("p (c t) -> p c t", c=KC)
    obuf = big.bitcast(BF16).rearrange("p (h t) -> p h t", h=NH)
    xnb = cx.sb("xn", [128, KC * T], BF16)
    xn = xnb.rearrange("p (c t) -> p c t", c=KC)
    g = cx.sb("g_sb", [128, KC], F32)
    gq = cx.sb("gq_sb", [128, 8], F32)
    cqs = cx.sb("cqs", [128, 8, T], F32)
    cqn = cx.sb("cqn", [128, 8, T], BF16)
    krt = cx.sb("krt", [64, NS], BF16)
    kb = cx.sb("kb", [128, NKC], F32)
    masks = cx.sb("masks", [128, 4 * T], BF16)
    qn = [cx.sb(f"qn{i}", [128, T], BF16) for i in range(2)]
    qr = [cx.sb(f"qr{i}", [64, T], BF16) for i in range(2)]
    pt = [cx.sb(f"pt{i}", [128, T], BF16) for i in range(2)]
    rden = cx.sb("rden", [128, T], F32)
    xc = [cx.sb(f"xc{i}", [128, T], F32) for i in range(2)]
    rope = Rope(cx, W["pos"], W["rc"], NT)
    cx.dma("sp", g, W["mix_g"][l], [], ["g"], "c0")
    cx.dma("sp", gq, W["mla_gq"][b], [], ["gq"], "c1")
    cx.dma("sp", krt[:, 0:NT], scr["KrT_all"][0:64, :], ["KrT_all"], ["krt"], "c2")
    cx.dma("sp", krt[:, NT:NS], scr["KrT_own"], ["KrT_own"], ["krt"], "c5")
    cx.dma("sp", kb, W["kbias"], [], ["kb"], "c3")
    cx.dma("sp", masks, W["masks"], [], ["masks"], "c4")
    srcv, dstv = fm(src), fm(dst)
    XS = [("xs", c) for c in range(KC)]
    it = 0
    nt = NT // T
    for ti in range(nt):
        t0 = ti * T
        q0 = NS - NT + t0
        nkc = (q0 + T) // 128
        cx.dma("sp", xs, srcv[:, :, t0:t0 + T], [("XA", ti)], XS, "x0")
        cx.rmsnorm(xs, "xs", g, "g", xn, "xn", KC, T, B[7], "b7", D)
        rope.tables(t0)
        for oc in range(8):
            w, wk = cx.wload(W["mla_wqa"][b, oc], KC * 128)
            bk = B[4 + oc % 2]
            for c in range(KC):
                cx.mm(bk[:], w[:, c * 128:(c + 1) * 128], xn[:, c, :], c == 0, c == KC - 1, [wk, ("xn", c)], [("b4", oc % 2)])
            cx.act(cqs[:, oc, :], bk[:], AF.Copy, [("b4", oc % 2)], [("cqs", oc)])
        cx.rmsnorm(cqs, "cqs", gq, "gq", cqn, "cqn", 8, T, B[7], "b7", 1024)
        for h in range(NH):
            hs = h % 2
            w, wk = cx.wload(W["mla_wqb"][b, h], 2048)
            for (bk, bkey, c0, mcols) in ((B[4], ("b4", 0), 0, 128), (B[5], ("b4", 1), 128, 64), (B[6], "q6", 192, 64)):
                for c in range(8):
                    cx.mm(bk[0:mcols, :], w[:, c * 256 + c0:c * 256 + c0 + mcols], cqn[:, c, :], c == 0, c == 7, [wk, ("cqn", c)], [bkey])
            cx.act(qn[hs], B[4][:], AF.Copy, [("b4", 0)], [("qn", hs)])
            rope.apply(qr[hs], [("qr", hs)], B[5][0:64, :], ("b4", 1), B[6][0:64, :], "q6")
            kvk = [("xn", c) for c in range(hs * 16, hs * 16 + 16)]
            kT = xnb[:, hs * 8192:hs * 8192 + nkc * 128]
            vv = xnb[:, hs * 8192 + 4096:hs * 8192 + 4096 + nkc * 128]
            nown = (nkc - OKC) * 128
            cx.dma("sp", kT[:, 0:NT], KnT_all[h // 4, (h % 4) * 128:(h % 4 + 1) * 128, :], ["KnT_all"], kvk, f"k{hs}")
            cx.dma("sp", kT[:, NT:NT + nown], KnT_own[h * 128:(h + 1) * 128, 0:nown], ["KnT_own"], kvk, f"k{hs}b")
            cx.dma("sp", vv[:, 0:NT], Vh_all[h // 4, (h % 4) * 128:(h % 4 + 1) * 128, :], ["Vh_all"], kvk, f"v{hs}")
            cx.dma("sp", vv[:, NT:NT + nown], Vh_own[h * 128:(h + 1) * 128, 0:nown], ["Vh_own"], kvk, f"v{hs}b")
            def scores(kc_, k_):
                cx.mm(B[k_][:], kT[:, kc_ * 128:(kc_ + 1) * 128], qn[hs], True, False, kvk + [("qn", hs)], [("s", k_)])
                cx.mm(B[k_][:], krt[:, kc_ * 128:(kc_ + 1) * 128], qr[hs], False, True, ["krt", ("qr", hs)], [("s", k_)])

            scores(0, it % 2)
            for kc in range(nkc):
                k = it % 2
                it += 1
                sb_ = B[k]
                if kc + 1 < nkc:
                    scores(kc + 1, it % 2)
                cx.act(pt[k], sb_[:], AF.Exp, [("s", k), "kb"], [("pt", k)], scale=SCALE, bias=kb[:, kc:kc + 1])
                off = kc * 128 - q0
                if off >= 0:
                    mi = off // 128
                    cx.tt("dve", pt[k], pt[k], masks[:, mi * T:(mi + 1) * T], ALU.mult, [("pt", k), "masks"], [("pt", k)])
                cx.mm(B[2][:], vv[:, kc * 128:(kc + 1) * 128], pt[k], kc == 0, kc == nkc - 1, kvk + [("pt", k)], ["o"])
                cx.mm(B[3][:], cx.ones_bf[:], pt[k], kc == 0, kc == nkc - 1, ["ones", ("pt", k)], ["den"])
            cx.recip(rden, B[3][:], ["den"], ["rden"])
            cx.tt("dve", obuf[:, h, :], B[2][:], rden, ALU.mult, ["o", "rden"], [("xs", h // 2)])
        for dc in range(KC):
            k = dc % 2
            bk = B[4 + k]
            for hh in range(2):
                w, wk = cx.wload(W["mla_wo"][b, dc][:, hh * 4096:(hh + 1) * 4096], 4096)
                for h2 in range(32):
                    h = hh * 32 + h2
                    cx.mm(bk[:], w[:, h2 * 128:(h2 + 1) * 128], obuf[:, h, :], h == 0, h == NH - 1, [wk, ("xs", h // 2)], [("b4", k)])
            cx.dma("sp", xc[k], srcv[:, dc, t0:t0 + T], [("XA", ti)], [("xc", k)], f"xc{k}")
            cx.tt("dve", xc[k], xc[k], bk[:], ALU.add, [("xc", k), ("b4", k)], [("xc", k)])
            cx.dma("sp", dstv[:, dc, t0:t0 + T], xc[k], [("xc", k)], [("XM", ti)], f"yo{k}")
            if ti == nt - 1:
                cx.dma("sp", fm(scr["HALO_in"])[:, dc, :], xc[k][:, T - 2:T], [("xc", k)], ["HALO_in"], f"hw{k}")
    cx.allgather(scr["HALO_in"], scr["HALO_all"], ["HALO_in"], ["HALO_all"], "ag_h")
    cx.end_phase()


def _scratch(cx, NT):
    return {
        "XA": cx.dscr("XA", [D, NT]), "XM": cx.dscr("XM", [D, NT]),
        "TAB": cx.dscr("TAB", [NBLK, 128, 4 * T]),
        "ST_in": cx.dscr("ST_in", [256, NBLK]), "ST_all": cx.dscr("ST_all", [512, NBLK]),
        "HALO_in": cx.dscr("HALO_in", [D, 2]), "HALO_all": cx.dscr("HALO_all", [2 * D, 2]),
    }


def _ffn_inputs(cx, W):
    W["ffn_g"] = cx.din("ffn_g", [1, 128, KC])
    W["gfin"] = cx.din("gfin", [128, KC])
    W["ffn_w_in"] = cx.din("ffn_w_in", [1, FC, 2, 128, KC * 128])
    W["ffn_w_out"] = cx.din("ffn_w_out", [1, KC, 2, 128, HF * 128])
    W["ffn_cw"] = cx.din("ffn_cw", [1, 128, FC * 4])


def build_s5_layer(NT):
    cx = Ctx()
    W = {}
    W["xT"] = cx.din("xT", [D, NT])
    W["flag"] = cx.din("flag", [128, 1])
    W["iota"] = cx.din("iota", [128, T])
    W["mix_g"] = cx.din("mix_g", [1, 128, KC])
    _ffn_inputs(cx, W)
    W["s5_ap_re"] = cx.din("s5_ap_re", [1, 128, NBLK])
    W["s5_ap_im"] = cx.din("s5_ap_im", [1, 128, NBLK])
    W["s5_ldt"] = cx.din("s5_ldt", [1, 128, NBLK])
    W["s5_bpad"] = cx.din("s5_bpad", [1, KC, 128, 1024])
    W["s5_cpad"] = cx.din("s5_cpad", [1, KC, 128, 1024])
    W["s5_dsk"] = cx.din("s5_dsk", [1, 128, KC])
    W["s5_wglu"] = cx.din("s5_wglu", [1, KC, 2, 128, KC * 128])
    W["s5_bgl"] = cx.din("s5_bgl", [1, 128, 2 * KC])
    yT = cx.dout("yT", [D, NT])
    scr = _scratch(cx, NT)
    cx.setup()
    cx.dma("sp", cx.flag[:], W["flag"], [], ["flag"], "fl")
    s5_phase(cx, NT, W["xT"], scr["XM"], W, 0, 0, scr)
    ffn_phase(cx, NT, scr["XM"], yT, scr["HALO_all"], W, 0, False, True, "XA")
    return cx.finish()


def build_mla_layer(NT):
    cx = Ctx()
    W = {}
    W["xT"] = cx.din("xT", [D, NT])
    W["x1T"] = cx.din("x1T", [D, NT])
    W["pos"] = cx.din("pos", [64, NT], I32)
    W["flag"] = cx.din("flag", [128, 1])
    W["fnflag"] = cx.din("fnflag", [128, 1])
    W["kbias"] = cx.din("kbias", [128, 2 * NT // 128])
    W["masks"] = cx.din("masks", [128, 4 * T], BF16)
    W["rc"] = cx.din("rc", [64, 2])
    W["mix_g"] = cx.din("mix_g", [1, 128, KC])
    _ffn_inputs(cx, W)
    W["kv_g"] = cx.din("kv_g", [128, KC])
    W["kv_gkv"] = cx.din("kv_gkv", [128, 4])
    W["kv_wa"] = cx.din("kv_wa", [6, 128, KC * 128])
    W["kv_wuk"] = cx.din("kv_wuk", [NH, 128, 512])
    W["kv_wuv"] = cx.din("kv_wuv", [16, 128, 2048])
    W["mla_gq"] = cx.din("mla_gq", [1, 128, 8])
    W["mla_wqa"] = cx.din("mla_wqa", [1, 8, 128, KC * 128])
    W["mla_wqb"] = cx.din("mla_wqb", [1, NH, 128, 2048])
    W["mla_wo"] = cx.din("mla_wo", [1, KC, 128, NH * 128])
    yT = cx.dout("yT", [D, NT])
    scr = _scratch(cx, NT)
    scr.update({
        "KnT_own": cx.dscr("KnT_own", [NH * 128, NT], BF16), "KnT_all": cx.dscr("KnT_all", [16, 1024, NT], BF16),
        "KrT_own": cx.dscr("KrT_own", [64, NT], BF16), "KrT_all": cx.dscr("KrT_all", [128, NT], BF16),
        "Vh_own": cx.dscr("Vh_own", [NH * 128, NT], BF16), "Vh_all": cx.dscr("Vh_all", [16, 1024, NT], BF16),
    })
    cx.setup()
    cx.dma("sp", cx.flag[:], W["flag"], [], ["flag"], "fl")
    cx.dma("sp", cx.fnflag[:], W["fnflag"], [], ["fnflag"], "fl2")
    kv_phase(cx, NT, W["x1T"], W, scr)
    mla_phase(cx, NT, W["xT"], scr["XM"], W, 0, 0, scr)
    ffn_phase(cx, NT, scr["XM"], yT, scr["HALO_all"], W, 0, True, True, "XA")
    return cx.finish()


def tile_w(w, cols_list):
    K = w.shape[0]
    out = []
    for cols in cols_list:
        sub = w[:, cols]
        m = sub.shape[1]
        out.append(np.ascontiguousarray(sub.reshape(K // 128, 128, m).transpose(1, 0, 2)).reshape(128, (K // 128) * m))
    return np.stack(out)


def pvec(v, n):
    return np.ascontiguousarray(np.asarray(v, np.float32).reshape(n, 128).T)


def prep_ffn(w_in, conv_w, conv_b, w_out):
    wi = w_in.reshape(KC, 128, 2, FC, 128)
    wi = np.ascontiguousarray(wi.transpose(3, 2, 1, 0, 4)).reshape(FC, 2, 128, KC * 128)
    wo = w_out.reshape(2, HF, 128, KC, 128)
    wo = np.ascontiguousarray(wo.transpose(3, 0, 2, 1, 4)).reshape(KC, 2, 128, HF * 128)
    cw = np.concatenate([conv_w, conv_b[None, :]], axis=0)
    cw = np.ascontiguousarray(cw.reshape(4, FC, 128).transpose(2, 1, 0)).reshape(128, FC * 4)
    return wi, wo, cw


def prep_s5(A_re, A_im, log_dt, B_re, B_im, C_re, C_im, D_skip, w_glu, b_glu):
    def pp(a):
        return np.ascontiguousarray(a.reshape(NBLK, 2, 64).transpose(1, 2, 0).reshape(128, NBLK))
    bpad = np.zeros((KC, 128, 2, 4, 128), np.float32)
    cpad = np.zeros((KC, 128, 2, 4, 128), np.float32)
    for part, (Bm, Cm) in enumerate([(B_re, C_re), (B_im, C_im)]):
        Bv = Bm.reshape(KC, 4, 2, 64, 16)
        Cv = Cm.reshape(KC, 4, 2, 16, 64)
        for j in range(4):
            for gl in range(2):
                r0 = 32 * j + 16 * gl
                bpad[:, r0:r0 + 16, part, j, gl * 64:(gl + 1) * 64] = Bv[:, j, gl].transpose(0, 2, 1)
                cpad[:, gl * 64:(gl + 1) * 64, part, j, r0:r0 + 16] = Cv[:, j, gl].transpose(0, 2, 1)
    wg = w_glu.reshape(KC, 128, 2, KC, 128)
    wg = np.ascontiguousarray(wg.transpose(3, 2, 1, 0, 4)).reshape(KC, 2, 128, KC * 128)
    return dict(ap_re=pp(A_re), ap_im=pp(A_im), ldt=pp(np.repeat(log_dt[:, None], 64, axis=1)),
                bpad=bpad.reshape(KC, 128, 1024), cpad=cpad.reshape(KC, 128, 1024),
                dsk=pvec(D_skip, KC), wglu=wg, bgl=pvec(b_glu, 2 * KC))


def kernel(x, positions, ln_mix, ln_ffn, ln_final,
           ssm_A_re, ssm_A_im, ssm_log_dt, ssm_B_re, ssm_B_im, ssm_C_re, ssm_C_im,
           ssm_D, ssm_w_glu, ssm_b_glu,
           kv_in_norm, w_kv_a, kv_latent_norm, w_kv_b,
           w_q_a, q_latent_norm, w_q_b, w_o,
           ffn_w_in, ffn_conv_w, ffn_conv_b, ffn_w_out):
    f32 = np.float32
    A = lambda v: np.asarray(v, f32)
    NT = 2048
    x = A(x)
    positions = np.asarray(positions, np.int32)
    W = {}
    W["mix_g"] = np.stack([pvec(ln_mix[l], KC) for l in range(4)])
    W["ffn_g"] = np.stack([pvec(ln_ffn[l], KC) for l in range(4)])
    W["gfin"] = pvec(ln_final, KC)
    ff = [prep_ffn(A(ffn_w_in[l]), A(ffn_conv_w[l]), A(ffn_conv_b[l]), A(ffn_w_out[l])) for l in range(4)]
    W["ffn_w_in"] = np.stack([f[0] for f in ff])
    W["ffn_w_out"] = np.stack([f[1] for f in ff])
    W["ffn_cw"] = np.stack([f[2] for f in ff])
    del ff
    s5 = [prep_s5(A(ssm_A_re[a]), A(ssm_A_im[a]), A(ssm_log_dt[a]), A(ssm_B_re[a]), A(ssm_B_im[a]), A(ssm_C_re[a]),
                  A(ssm_C_im[a]), A(ssm_D[a]), A(ssm_w_glu[a]), A(ssm_b_glu[a])) for a in range(2)]
    for k_, n_ in [("ap_re", "s5_ap_re"), ("ap_im", "s5_ap_im"), ("ldt", "s5_ldt"), ("bpad", "s5_bpad"), ("cpad", "s5_cpad"),
                   ("dsk", "s5_dsk"), ("wglu", "s5_wglu"), ("bgl", "s5_bgl")]:
        W[n_] = np.stack([s5[a][k_] for a in range(2)])
    del s5
    wkva = A(w_kv_a)
    wa_ext = np.concatenate([wkva, np.zeros((D, 64), f32)], axis=1)
    cols = [np.arange(oc * 128, (oc + 1) * 128) for oc in range(4)]
    cols.append(np.concatenate([np.arange(512, 576), np.arange(576, 640)]))
    cols.append(np.concatenate([np.arange(544, 576), np.arange(512, 544), np.arange(576, 640)]))
    W["kv_wa"] = tile_w(wa_ext, cols)
    wkv = A(w_kv_b).reshape(512, NH, 256)
    w_uk = np.ascontiguousarray(wkv[:, :, :128]).reshape(512, NH * 128)
    w_uv = np.ascontiguousarray(wkv[:, :, 128:]).reshape(512, NH * 128)
    W["kv_wuk"] = tile_w(w_uk, [np.arange(h * 128, (h + 1) * 128) for h in range(NH)])
    W["kv_wuv"] = tile_w(w_uv, [np.arange(nb * 512, (nb + 1) * 512) for nb in range(16)])
    W["kv_g"] = pvec(kv_in_norm, KC)
    W["kv_gkv"] = pvec(kv_latent_norm, 4)
    qcols = []
    for h in range(NH):
        b0 = h * 192
        qcols.append(np.concatenate([np.arange(b0, b0 + 192), np.arange(b0 + 160, b0 + 192), np.arange(b0 + 128, b0 + 160)]))
    W["mla_gq"] = np.stack([pvec(q_latent_norm[b], 8) for b in range(2)])
    W["mla_wqa"] = np.stack([tile_w(A(w_q_a[b]), [np.arange(oc * 128, (oc + 1) * 128) for oc in range(8)]) for b in range(2)])
    W["mla_wqb"] = np.stack([tile_w(A(w_q_b[b]), qcols) for b in range(2)])
    W["mla_wo"] = np.stack([tile_w(A(w_o[b]), [np.arange(dc * 128, (dc + 1) * 128) for dc in range(KC)]) for b in range(2)])
    invf = (10000.0 ** (-np.arange(32, dtype=np.float32) / 32)).astype(f32)
    rc = np.zeros((64, 2), f32)
    rc[:, 0] = np.concatenate([invf, invf])
    rc[:32, 1] = -1.0
    rc[32:, 1] = 1.0
    W["rc"] = rc
    W["iota"] = np.ascontiguousarray(np.broadcast_to(np.arange(T, dtype=f32), (128, T)))
    p_ = np.arange(128)[:, None]
    q_ = np.arange(T)[None, :]
    W["masks"] = np.concatenate([((q_ - p_) >= mi * 128) for mi in range(4)], axis=1).astype(ml_dtypes.bfloat16)
    cores = [(b, h) for b in range(4) for h in range(2)]
    ids = list(range(8))
    flags = [np.full((128, 1), float(h), f32) for (b, h) in cores]
    xs = [np.ascontiguousarray(x[b, h * NT:(h + 1) * NT].T) for (b, h) in cores]
    nc_s5 = build_s5_layer(NT)
    for l in range(2):
        Wl = {"iota": W["iota"], "gfin": W["gfin"], "mix_g": W["mix_g"][l:l + 1], "ffn_g": W["ffn_g"][l:l + 1],
              "ffn_w_in": W["ffn_w_in"][l:l + 1], "ffn_w_out": W["ffn_w_out"][l:l + 1], "ffn_cw": W["ffn_cw"][l:l + 1]}
        for n_ in ["s5_ap_re", "s5_ap_im", "s5_ldt", "s5_bpad", "s5_cpad", "s5_dsk", "s5_wglu", "s5_bgl"]:
            Wl[n_] = W[n_][l:l + 1]
        maps = [dict(Wl, xT=xs[i], flag=flags[i]) for i in range(8)]
        res = run_bass_kernel_spmd(nc_s5, maps, core_ids=ids).results
        xs = [np.ascontiguousarray(r["yT"]) for r in res]
        del maps, res, Wl
    x1 = xs
    nc_mla = build_mla_layer(NT)
    for l in range(2, 4):
        b_ = l - 2
        Wl = {"gfin": W["gfin"], "mix_g": W["mix_g"][l:l + 1], "ffn_g": W["ffn_g"][l:l + 1],
              "ffn_w_in": W["ffn_w_in"][l:l + 1], "ffn_w_out": W["ffn_w_out"][l:l + 1], "ffn_cw": W["ffn_cw"][l:l + 1],
              "rc": W["rc"], "masks": W["masks"], "kv_g": W["kv_g"], "kv_gkv": W["kv_gkv"], "kv_wa": W["kv_wa"],
              "kv_wuk": W["kv_wuk"], "kv_wuv": W["kv_wuv"], "mla_gq": W["mla_gq"][b_:b_ + 1], "mla_wqa": W["mla_wqa"][b_:b_ + 1],
              "mla_wqb": W["mla_wqb"][b_:b_ + 1], "mla_wo": W["mla_wo"][b_:b_ + 1]}
        fn = np.full((128, 1), 1.0 if l == 3 else 0.0, f32)
        maps = []
        for i, (b, h) in enumerate(cores):
            m = dict(Wl, xT=xs[i], x1T=x1[i], flag=flags[i], fnflag=fn)
            m["pos"] = np.ascontiguousarray(np.broadcast_to(positions[b, h * NT:(h + 1) * NT], (64, NT)))
            bias = np.zeros((2 * NT,), f32)
            if h == 0:
                bias[:NT] = -80.0
            m["kbias"] = np.ascontiguousarray(bias.reshape(2 * NT // 128, 128).T)
            maps.append(m)
        res = run_bass_kernel_spmd(nc_mla, maps, core_ids=ids).results
        xs = [np.ascontiguousarray(r["yT"]) for r in res]
        del maps, res, Wl
    out = np.empty((4, 2 * NT, D), f32)
    for i, (b, h) in enumerate(cores):
        out[b, h * NT:(h + 1) * NT] = xs[i].T
    return out
```

```python
import math
import numpy as np
import ml_dtypes
from contextlib import ExitStack
import concourse.bass as bass
import concourse.mybir as mybir
from concourse.bass_utils import run_bass_kernel_spmd


EPOCH = 24000
ENGS = ("pe", "dve", "act", "pool", "sp")


class Op:
    __slots__ = ("eng", "fn", "deps", "idx", "is_dma", "chan", "sig", "sigcnt", "chan_cnt", "inc")

    def __init__(self, eng, fn, is_dma=False, chan=None, inc=16):
        self.inc = inc
        self.eng = eng
        self.fn = fn
        self.deps = []
        self.idx = -1
        self.is_dma = is_dma
        self.chan = chan
        self.sig = False
        self.sigcnt = 0
        self.chan_cnt = 0


class Sched:
    def __init__(self, nc):
        self.nc = nc
        self.streams = {e: [] for e in ENGS}
        self.last_w = {}
        self.readers = {}
        self.chan_last = {}
        self.chan_count = {}
        self.seen = {e: {} for e in ENGS}
        self.seen_chan = {e: {} for e in ENGS}
        self.bar_ops = []
        self.bar_id = 0
        self.eng_bar = {e: 0 for e in ENGS}

    def _add_dep(self, op, d):
        if d is None or d is op:
            return
        if d.is_dma:
            cur = self.seen_chan[op.eng].get(d.chan, 0)
            if cur >= d.chan_cnt:
                return
            self.seen_chan[op.eng][d.chan] = d.chan_cnt
            op.deps.append(d)
            d.sig = True
            return
        if d.eng == op.eng and op.eng == "pe" and not op.is_dma:
            return
        cur = self.seen[op.eng].get(d.eng, -1)
        if cur >= d.idx:
            return
        self.seen[op.eng][d.eng] = d.idx
        op.deps.append(d)
        d.sig = True

    def barrier(self):
        ops = [st[-1] for st in self.streams.values() if st]
        for e in ENGS:
            for o in reversed(self.streams[e]):
                if not o.is_dma:
                    ops.append(o)
                    break
        ops += list(self.chan_last.values())
        self.bar_ops = ops
        self.bar_id += 1

    def op(self, eng, fn, reads=(), writes=(), dma=False, chan=None, inc=16):
        o = Op(eng, fn, is_dma=dma, chan=chan, inc=inc)
        st = self.streams[eng]
        o.idx = len(st)
        if self.eng_bar[eng] != self.bar_id:
            self.eng_bar[eng] = self.bar_id
            for d in self.bar_ops:
                self._add_dep(o, d)
        if dma:
            prev = self.chan_last.get(chan)
            self.chan_count[chan] = self.chan_count.get(chan, 0) + 1
            o.chan_cnt = self.chan_count[chan]
            if prev is not None:
                self._add_dep(o, prev)
            self.chan_last[chan] = o
            o.sig = True
        for k in reads:
            self._add_dep(o, self.last_w.get(k))
        for k in writes:
            self._add_dep(o, self.last_w.get(k))
            for r in self.readers.get(k, ()):
                self._add_dep(o, r)
        for k in reads:
            self.readers.setdefault(k, []).append(o)
        for k in writes:
            self.last_w[k] = o
            self.readers[k] = []
        st.append(o)
        return o

    def emit(self, final_waits=()):
        nc = self.nc
        nsig = {}
        for e in ENGS:
            c = 0
            for o in self.streams[e]:
                if o.is_dma:
                    continue
                if o.sig:
                    c += 1
                    o.sigcnt = c
            nsig[e] = c
        from contextlib import ExitStack
        with ExitStack() as es:
            esem = {}
            for e in ENGS:
                n_ep = (nsig[e] + EPOCH - 1) // EPOCH
                esem[e] = [es.enter_context(nc.semaphore(f"s_{e}_{i}")) for i in range(max(n_ep, 1))]
            csem = {ch: es.enter_context(nc.semaphore(f"c_{ch}")) for ch in self.chan_count}
            block = es.enter_context(nc.Block())
            engobj = {"pe": "tensor", "dve": "vector", "act": "scalar", "pool": "gpsimd", "sp": "sync"}

            def make(e):
                def body(eng):
                    for o in self.streams[e]:
                        for d in o.deps:
                            if d.is_dma:
                                eng.wait_ge(csem[d.chan], d.inc * d.chan_cnt)
                            else:
                                ep, v = divmod(d.sigcnt - 1, EPOCH)
                                eng.wait_ge(esem[d.eng][ep], v + 1)
                        ins = o.fn(eng)
                        if o.is_dma:
                            ins.then_inc(csem[o.chan], o.inc)
                        elif o.sig:
                            ep, v = divmod(o.sigcnt - 1, EPOCH)
                            ins.then_inc(esem[e][ep], 1)
                    if e == "sp":
                        for d in final_waits:
                            if d.is_dma:
                                eng.wait_ge(csem[d.chan], d.inc * d.chan_cnt)
                return body

            for e in ENGS:
                if not self.streams[e] and e != "sp":
                    continue
                getattr(block, engobj[e])(make(e))

F32 = mybir.dt.float32
BF16 = mybir.dt.bfloat16
I32 = mybir.dt.int32
AF = mybir.ActivationFunctionType
ALU = mybir.AluOpType

D = 4096
KC = 32
T = 512
FF = 11008
FC = 86
HF = 43
EPS = 1e-6
NBLK = 128
NH = 64
TWO_PI = 2.0 * math.pi
SCALE = 192.0 ** -0.5
ARENA_BYTES = 203264
PAIRS = [[0, 1], [2, 3], [4, 5], [6, 7]]
DTSZ = {F32: 4, BF16: 2, I32: 4}


class Ctx:
    def __init__(self):
        self.nc = bass.Bass("TRN2", target_bir_lowering=False)
        self.es = ExitStack()
        self.S = Sched(self.nc)
        self.banks = [self.es.enter_context(self.nc.psum_tensor(f"pb{i}", [128, 512], F32)) for i in range(8)]
        self.arena = self.es.enter_context(self.nc.sbuf_tensor("arena", [128, ARENA_BYTES // 2], BF16))
        self.aoff = 0
        self.outs = []
        self.wbufs = []
        self.wi = 0
        self.nph = 0

    def din(self, name, shape, dt=F32):
        return self.nc.dram_tensor(name, list(shape), dt, kind="ExternalInput").ap()

    def dout(self, name, shape, dt=F32):
        return self.nc.dram_tensor(name, list(shape), dt, kind="ExternalOutput").ap()

    def dscr(self, name, shape, dt=F32):
        return self.nc.dram_tensor(name, list(shape), dt).ap()

    def psb(self, name, shape, dt):
        return self.es.enter_context(self.nc.sbuf_tensor("sb_" + name, list(shape), dt))

    def sb(self, name, shape, dt):
        n = 1
        for v in shape[1:]:
            n *= v
        nb = n * DTSZ[dt]
        nb = (nb + 63) // 64 * 64
        off = self.aoff
        self.aoff += nb
        assert self.aoff <= ARENA_BYTES, (name, self.aoff)
        ap = self.arena[0:shape[0], off // 2:(off + n * DTSZ[dt]) // 2]
        if dt != BF16:
            ap = ap.bitcast(dt)
        if len(shape) == 3:
            ap = ap.rearrange("p (a b) -> p a b", a=shape[1])
        return ap

    def end_phase(self):
        self.aoff = 0
        self.wbufs = []
        self.S.barrier()

    def finish(self):
        self.S.emit(final_waits=self.outs)
        self.es.close()
        return self.nc

    def mm(self, out, lhsT, rhs, start, stop, reads, writes, **kw):
        return self.S.op("pe", lambda e: e.matmul(out, lhsT, rhs, start=start, stop=stop, **kw), reads=reads, writes=writes)

    def act(self, out, in_, func, reads, writes, **kw):
        return self.S.op("act", lambda e: e.activation(out=out, in_=in_, func=func, **kw), reads=reads, writes=writes)

    def tt(self, eng, out, in0, in1, op, reads, writes):
        return self.S.op(eng, lambda e: e.tensor_tensor(out=out, in0=in0, in1=in1, op=op), reads=reads, writes=writes)

    def ts(self, eng, out, in0, s1, s2, op0, op1, reads, writes):
        if s2 is None:
            return self.S.op(eng, lambda e: e.tensor_scalar(out=out, in0=in0, scalar1=s1, scalar2=None, op0=op0), reads=reads, writes=writes)
        return self.S.op(eng, lambda e: e.tensor_scalar(out=out, in0=in0, scalar1=s1, scalar2=s2, op0=op0, op1=op1), reads=reads, writes=writes)

    def stt(self, out, in0, scalar, in1, op0, op1, reads, writes):
        return self.S.op("dve", lambda e: e.scalar_tensor_tensor(out=out, in0=in0, scalar=scalar, in1=in1, op0=op0, op1=op1), reads=reads, writes=writes)

    def copy(self, eng, out, in_, reads, writes):
        return self.S.op(eng, lambda e: e.tensor_copy(out=out, in_=in_), reads=reads, writes=writes)

    def memset(self, ap, val, writes):
        return self.S.op("dve", lambda e: e.memset(ap, val), writes=writes)

    def recip(self, out, in_, reads, writes):
        return self.S.op("dve", lambda e: e.reciprocal(out=out, in_=in_), reads=reads, writes=writes)

    def scan(self, out, d0, d1, init, reads, writes):
        return self.S.op("dve", lambda e: e.tensor_tensor_scan(out=out, data0=d0, data1=d1, initial=init, op0=ALU.mult, op1=ALU.add), reads=reads, writes=writes)

    def dma(self, q, out, in_, reads, writes, chan):
        return self.S.op(q, lambda e: e.dma_start(out=out, in_=in_), reads=reads, writes=writes, dma=True, chan=chan)

    def store(self, out, in_, reads, chan, writes=()):
        o = self.dma("sp", out, in_, reads, list(writes), chan)
        self.outs.append(o)
        return o

    def allgather(self, src, dst, reads, writes, chan):
        o = self.S.op("pool", lambda e: e.collective_compute("AllGather", ALU.bypass, replica_groups=PAIRS, ins=[src.opt()], outs=[dst.opt()]),
                      reads=reads, writes=writes, dma=True, chan=chan, inc=1)
        self.outs.append(o)
        return o

    def wpool(self, nbuf, nelem):
        self.wbufs = [self.sb(f"wb{i}", [128, nelem], BF16) for i in range(nbuf)]
        self.wi = 0

    def wload(self, src, n):
        i = self.wi
        self.wi = (i + 1) % len(self.wbufs)
        buf = self.wbufs[i]
        self.dma("pool", buf[:, 0:n], src, [], [("wb", i)], f"wb{i}")
        return buf, ("wb", i)

    def setup(self):
        self.epsc = self.psb("epsc", [128, 1], F32)
        self.memset(self.epsc[:], EPS, ["epsc"])
        self.ones_bf = self.psb("ones_bf", [128, 128], BF16)
        self.memset(self.ones_bf[:], 1.0, ["ones"])
        self.sq = [self.psb(f"sq{i}", [128, T], BF16) for i in range(2)]
        self.rstd = self.psb("rstd", [128, T], F32)
        self.rtmp = self.psb("rtmp", [128, T], F32)
        self.flag = self.psb("flag", [128, 1], F32)
        self.fnflag = self.psb("fnflag", [128, 1], F32)

    def rmsnorm(self, x, xkey, g, gkey, out, okey, nkc, n, bank, bkey, dn):
        for c in range(nkc):
            s = self.sq[c % 2]
            self.act(s[:, 0:n], x[:, c, 0:n], AF.Square, [(xkey, c)], [("sq", c % 2)])
            self.mm(bank[:, 0:n], self.ones_bf[:], s[:, 0:n], c == 0, c == nkc - 1, ["ones", ("sq", c % 2)], [bkey])
        self.act(self.rtmp[:, 0:n], bank[:, 0:n], AF.Sqrt, [bkey, "epsc"], ["rtmp"], scale=1.0 / dn, bias=self.epsc[:, 0:1])
        self.recip(self.rstd[:, 0:n], self.rtmp[:, 0:n], ["rtmp"], ["rstd"])
        for c in range(nkc):
            self.stt(out[:, c, 0:n], x[:, c, 0:n], g[:, c:c + 1], self.rstd[:, 0:n], ALU.mult, ALU.mult,
                     [(xkey, c), gkey, "rstd"], [(okey, c)])


def fm(ap):
    return ap.rearrange("(c p) t -> p c t", p=128)


def sincos(cx, ang, akey, n, cos_out, ckey, sin_out, skey, scr, scrkey, scr2, scr2key):
    MAGIC = 12582912.0
    C1 = 6.28125
    C2 = TWO_PI - C1

    def reduce():
        cx.ts("dve", scr[:, 0:n], ang, 1.0 / TWO_PI, MAGIC, ALU.mult, ALU.add, [akey], [scrkey])
        cx.ts("dve", scr[:, 0:n], scr[:, 0:n], -MAGIC, None, ALU.add, None, [scrkey], [scrkey])
        cx.stt(scr2[:, 0:n], scr[:, 0:n], -C1, ang, ALU.mult, ALU.add, [scrkey, akey], [scr2key])
        cx.stt(scr2[:, 0:n], scr[:, 0:n], -C2, scr2[:, 0:n], ALU.mult, ALU.add, [scrkey, scr2key], [scr2key])
        cx.ts("dve", scr2[:, 0:n], scr2[:, 0:n], 3.1415925, -3.1415925, ALU.min, ALU.max, [scr2key], [scr2key])
    reduce()
    cx.act(sin_out, scr2[:, 0:n], AF.Sin, [scr2key], [skey])
    cx.ts("dve", ang, ang, math.pi / 2, None, ALU.add, None, [akey], [akey])
    reduce()
    cx.act(cos_out, scr2[:, 0:n], AF.Sin, [scr2key], [ckey])


def ffn_phase(cx, NT, src, dst, halo_all, W, l, final_norm, dst_is_out, dkey):
    nt = NT // T
    B = cx.banks
    cx.wpool(4, HF * 128)
    xs = cx.sb("xs", [128, KC, T], F32)
    xn = cx.sb("xn", [128, KC, T], BF16)
    h = cx.sb("h", [128, HF, T], BF16)
    xhs = cx.sb("xhs", [128, KC, 2], F32)
    xhn = cx.sb("xhn", [128, KC, 2], BF16)
    g = cx.sb("g_sb", [128, KC], F32)
    gf = cx.sb("gf_sb", [128, KC], F32)
    cw = cx.sb("cw_sb", [128, FC * 4], F32)
    carry = cx.sb("carry", [128, FC, 2], F32)
    G = [cx.sb(f"G{i}", [128, T + 2], F32) for i in range(2)]
    acc = [cx.sb(f"acc{i}", [128, T], F32) for i in range(2)]
    sg = [cx.sb(f"sg{i}", [128, T], BF16) for i in range(2)]
    srcv, dstv = fm(src), fm(dst)
    XHS = [("xhs", c) for c in range(KC)]
    cx.dma("sp", g[:], W["ffn_g"][l], [], ["g"], "c0")
    cx.dma("sp", gf[:], W["gfin"], [], ["gf"], "c1")
    cx.dma("sp", cw[:], W["ffn_cw"][l], [], ["cw"], "c2")
    cx.dma("sp", xhs, fm(halo_all[0:D, :]), ["HALO_all"], XHS, "c3")
    for c in range(KC):
        cx.ts("dve", xhs[:, c, :], xhs[:, c, :], cx.flag[:, 0:1], None, ALU.mult, None, [("xhs", c), "flag"], [("xhs", c)])
    cx.rmsnorm(xhs, "xhs", g, "g", xhn, "xhn", KC, 2, B[7], "b7", D)
    w_in, w_out = W["ffn_w_in"], W["ffn_w_out"]
    it = 0
    for ti in range(nt):
        t0 = ti * T
        cx.dma("sp", xs, srcv[:, :, t0:t0 + T], [("XM", ti)], [("xs", c) for c in range(KC)], "x0")
        cx.rmsnorm(xs, "xs", g, "g", xn, "xn", KC, T, B[6], "b6", D)
        for half in range(2):
            for fi in range(HF):
                fc = half * HF + fi
                k = it % 2
                it += 1
                wg, wgk = cx.wload(w_in[l, fc, 0], KC * 128)
                wu, wuk = cx.wload(w_in[l, fc, 1], KC * 128)
                bg, bu = B[k], B[2 + k]
                for c in range(KC):
                    cx.mm(bg[:], wg[:, c * 128:(c + 1) * 128], xn[:, c, :], c == 0, c == KC - 1, [wgk, ("xn", c)], [("bg", k)])
                Gk = G[k]
                if ti == 0:
                    for c in range(KC):
                        cx.mm(B[7][:, 0:2], wg[:, c * 128:(c + 1) * 128], xhn[:, c, :], c == 0, c == KC - 1, [wgk, ("xhn", c)], ["b7"])
                    cx.act(Gk[:, 0:2], B[7][:, 0:2], AF.Copy, ["b7"], [("G", k)])
                else:
                    cx.act(Gk[:, 0:2], carry[:, fc, :], AF.Copy, [("carry", fc)], [("G", k)])
                for c in range(KC):
                    cx.mm(bu[:], wu[:, c * 128:(c + 1) * 128], xn[:, c, :], c == 0, c == KC - 1, [wuk, ("xn", c)], [("bu", k)])
                cx.act(Gk[:, 2:T + 2], bg[:], AF.Copy, [("bg", k)], [("G", k)])
                cx.act(carry[:, fc, :], Gk[:, T:T + 2], AF.Copy, [("G", k)], [("carry", fc)])
                a = acc[k]
                o = fc * 4
                cx.ts("dve", a, Gk[:, 0:T], cw[:, o:o + 1], cw[:, o + 3:o + 4], ALU.mult, ALU.add, [("G", k), "cw"], [("acc", k)])
                cx.stt(a, Gk[:, 1:T + 1], cw[:, o + 1:o + 2], a, ALU.mult, ALU.add, [("G", k), "cw", ("acc", k)], [("acc", k)])
                cx.stt(a, Gk[:, 2:T + 2], cw[:, o + 2:o + 3], a, ALU.mult, ALU.add, [("G", k), "cw", ("acc", k)], [("acc", k)])
                cx.act(sg[k], a, AF.Silu, [("acc", k)], [("sg", k)])
                cx.tt("dve", h[:, fi, :], sg[k], bu[:], ALU.mult, [("sg", k), ("bu", k)], [("h", fi)])
            for dc in range(KC):
                k = dc % 2
                wo, wok = cx.wload(w_out[l, dc, half], HF * 128)
                bo = B[4 + k]
                for fi in range(HF):
                    cx.mm(bo[:], wo[:, fi * 128:(fi + 1) * 128], h[:, fi, :], fi == 0, fi == HF - 1, [wok, ("h", fi)], [("bo", k)])
                cx.tt("dve", xs[:, dc, :], xs[:, dc, :], bo[:], ALU.add, [("xs", dc), ("bo", k)], [("xs", dc)])
        if final_norm:
            for c in range(KC):
                sqb = cx.sq[c % 2]
                cx.act(sqb[:, 0:T], xs[:, c, :], AF.Square, [("xs", c)], [("sq", c % 2)])
                cx.mm(B[6][:], cx.ones_bf[:], sqb[:, 0:T], c == 0, c == KC - 1, ["ones", ("sq", c % 2)], ["b6"])
            cx.act(cx.rtmp[:, 0:T], B[6][:], AF.Sqrt, ["b6", "epsc"], ["rtmp"], scale=1.0 / D, bias=cx.epsc[:, 0:1])
            cx.recip(cx.rstd[:, 0:T], cx.rtmp[:, 0:T], ["rtmp"], ["rstd"])
            for c in range(KC):
                tn = acc[c % 2]
                tnk = ("acc", c % 2)
                cx.stt(tn, xs[:, c, :], gf[:, c:c + 1], cx.rstd[:, 0:T], ALU.mult, ALU.mult, [("xs", c), "gf", "rstd"], [tnk])
                cx.tt("dve", tn, tn, xs[:, c, :], ALU.subtract, [tnk, ("xs", c)], [tnk])
                cx.stt(xs[:, c, :], tn, cx.fnflag[:, 0:1], xs[:, c, :], ALU.mult, ALU.add, [tnk, "fnflag", ("xs", c)], [("xs", c)])
        XS = [("xs", c) for c in range(KC)]
        if dst_is_out:
            cx.store(dstv[:, :, t0:t0 + T], xs, XS, "y0")
        else:
            cx.dma("sp", dstv[:, :, t0:t0 + T], xs, XS, [(dkey, ti)], "y0")
    cx.end_phase()


def write_halo(cx, xs, XS, halo_in, halo_all):
    cx.dma("sp", fm(halo_in), xs[:, :, T - 2:T], XS, ["HALO_in"], "hw")
    cx.allgather(halo_in, halo_all, ["HALO_in"], ["HALO_all"], "ag_h")


def s5_phase(cx, NT, src, dst, W, a, l, scr):
    nc = cx.nc
    S = cx.S
    B = cx.banks
    TAB, ST_in, ST_all, halo_in, halo_all = scr["TAB"], scr["ST_in"], scr["ST_all"], scr["HALO_in"], scr["HALO_all"]
    cx.wpool(4, KC * 128)
    xs = cx.sb("xs", [128, KC, T], F32)
    un = cx.sb("un", [128, KC, T], BF16)
    g = cx.sb("g_sb", [128, KC], F32)
    dsk = cx.sb("dsk", [128, KC], F32)
    bgl = cx.sb("bgl", [128, 2 * KC], F32)
    iota = cx.sb("iota", [128, T], F32)
    sm = {n: cx.sb("sm_" + n, [128, NBLK], F32) for n in
          ["ar", "ai", "dt", "th", "R", "c", "s", "lr", "li", "den", "fre", "fim", "nfre", "t1", "t2", "hre", "him"]}
    tabs = [cx.sb(f"tab{i}", [128, 4, T], F32) for i in range(2)]
    tmp = {n: cx.sb("tmp_" + n, [128, T], F32) for n in ["t1", "t2", "t3", "t4", "gir", "gii", "gr", "gi", "yp", "y2"]}
    Rt = cx.sb("Rt", [128, T], F32)
    hb = [[cx.sb(f"hb{i}{k}", [128, T], BF16) for k in range(2)] for i in range(2)]
    ini = [[cx.sb(f"ini{i}{k}", [128, 1], F32) for k in range(2)] for i in range(2)]
    tiny = [cx.sb(f"tiny{i}", [128, 1], F32) for i in range(4)]
    ones512 = cx.sb("ones512", [128, T], F32)
    srcv, dstv = fm(src), fm(dst)
    m = sm
    cx.memset(ones512, 1.0, ["ones512"])
    cx.memset(m["hre"], 0.0, ["hre"])
    cx.memset(m["him"], 0.0, ["him"])
    cx.dma("sp", g, W["mix_g"][l], [], ["g"], "c0")
    cx.dma("sp", dsk, W["s5_dsk"][a], [], ["dsk"], "c1")
    cx.dma("sp", bgl, W["s5_bgl"][a], [], ["bgl"], "c2")
    cx.dma("sp", iota, W["iota"], [], ["iota"], "c3")
    cx.dma("sp", m["ar"], W["s5_ap_re"][a], [], ["ar"], "c4")
    cx.dma("sp", m["ai"], W["s5_ap_im"][a], [], ["ai"], "c5")
    cx.dma("sp", m["dt"], W["s5_ldt"][a], [], ["dt"], "c6")
    cx.act(m["dt"], m["dt"], AF.Exp, ["dt"], ["dt"])
    cx.tt("dve", m["th"], m["ai"], m["dt"], ALU.mult, ["ai", "dt"], ["th"])
    cx.tt("dve", m["t1"], m["ar"], m["dt"], ALU.mult, ["ar", "dt"], ["t1"])
    cx.act(m["R"], m["t1"], AF.Exp, ["t1"], ["R"])
    cx.copy("dve", m["hre"], m["th"], ["th"], ["hre"])
    sincos(cx, m["hre"], "hre", NBLK, m["c"], "c", m["s"], "s", m["t2"], "t2", m["him"], "him")
    S.op("dve", lambda e: e.memset(m["hre"], 0.0), reads=["hre"], writes=["hre"])
    S.op("dve", lambda e: e.memset(m["him"], 0.0), reads=["him"], writes=["him"])
    cx.tt("dve", m["lr"], m["R"], m["c"], ALU.mult, ["R", "c"], ["lr"])
    cx.tt("dve", m["li"], m["R"], m["s"], ALU.mult, ["R", "s"], ["li"])
    cx.ts("dve", m["lr"], m["lr"], -1.0, None, ALU.add, None, ["lr"], ["lr"])
    cx.tt("dve", m["den"], m["ar"], m["ar"], ALU.mult, ["ar"], ["den"])
    cx.tt("dve", m["t1"], m["ai"], m["ai"], ALU.mult, ["ai"], ["t1"])
    cx.tt("dve", m["den"], m["den"], m["t1"], ALU.add, ["den", "t1"], ["den"])
    cx.recip(m["den"], m["den"], ["den"], ["den"])
    cx.tt("dve", m["t1"], m["lr"], m["ar"], ALU.mult, ["lr", "ar"], ["t1"])
    cx.tt("dve", m["t2"], m["li"], m["ai"], ALU.mult, ["li", "ai"], ["t2"])
    cx.tt("dve", m["t1"], m["t1"], m["t2"], ALU.add, ["t1", "t2"], ["t1"])
    cx.tt("dve", m["fre"], m["t1"], m["den"], ALU.mult, ["t1", "den"], ["fre"])
    cx.tt("dve", m["t1"], m["li"], m["ar"], ALU.mult, ["li", "ar"], ["t1"])
    cx.tt("dve", m["t2"], m["lr"], m["ai"], ALU.mult, ["lr", "ai"], ["t2"])
    cx.tt("dve", m["t1"], m["t1"], m["t2"], ALU.subtract, ["t1", "t2"], ["t1"])
    cx.tt("dve", m["fim"], m["t1"], m["den"], ALU.mult, ["t1", "den"], ["fim"])
    cx.ts("dve", m["nfre"], m["fre"], -1.0, None, ALU.mult, None, ["fre"], ["nfre"])
    for blk in range(NBLK):
        k = blk % 2
        tb = tabs[k]
        tk = ("tab", k)
        ang = tmp["t1"] if k == 0 else tmp["t2"]
        ak = ("ang", k)
        scr1 = tmp["t3"] if k == 0 else tmp["t4"]
        scr2 = tmp["gr"] if k == 0 else tmp["gi"]
        cx.ts("dve", ang, iota, m["th"][:, blk:blk + 1], None, ALU.mult, None, ["iota", "th"], [ak])
        sincos(cx, ang, ak, T, tb[:, 2, :], tk, tb[:, 3, :], tk, scr1, ("scr", k), scr2, ("scr2", k))
        aa = tmp["gir"] if k == 0 else tmp["gii"]
        cx.ts("dve", aa, tb[:, 2, :], m["fre"][:, blk:blk + 1], None, ALU.mult, None, [tk, "fre"], [("a", k)])
        cx.stt(tb[:, 0, :], tb[:, 3, :], m["fim"][:, blk:blk + 1], aa, ALU.mult, ALU.add, [tk, "fim", ("a", k)], [tk])
        cx.ts("dve", aa, tb[:, 2, :], m["fim"][:, blk:blk + 1], None, ALU.mult, None, [tk, "fim"], [("a", k)])
        cx.stt(tb[:, 1, :], tb[:, 3, :], m["nfre"][:, blk:blk + 1], aa, ALU.mult, ALU.add, [tk, "nfre", ("a", k)], [tk])
        cx.dma("sp", TAB[blk], tb.rearrange("p a t -> p (a t)"), [tk], [("TAB", blk)], f"tw{k}")

    nt = NT // T
    tiles = [(True, i) for i in range(nt)] + [(False, i) for i in range(nt)]
    it = 0
    bp_in, cp_in, wg_in = W["s5_bpad"], W["s5_cpad"], W["s5_wglu"]
    for idx, (is_pre, ti) in enumerate(tiles):
        t0 = ti * T
        if idx == nt:
            cx.dma("sp", ST_in[0:128, :], m["hre"], ["hre"], ["ST_in"], "st0")
            cx.dma("sp", ST_in[128:256, :], m["him"], ["him"], ["ST_in"], "st1")
            cx.allgather(ST_in, ST_all, ["ST_in"], ["ST_all"], "ag_s")
            cx.dma("sp", m["hre"], ST_all[0:128, :], ["ST_all"], ["hre"], "st0")
            cx.dma("sp", m["him"], ST_all[128:256, :], ["ST_all"], ["him"], "st1")
            cx.ts("dve", m["hre"], m["hre"], cx.flag[:, 0:1], None, ALU.mult, None, ["hre", "flag"], ["hre"])
            cx.ts("dve", m["him"], m["him"], cx.flag[:, 0:1], None, ALU.mult, None, ["him", "flag"], ["him"])
        XS = [("xs", c) for c in range(KC)]
        cx.dma("sp", xs, srcv[:, :, t0:t0 + T], [("XA", ti)], XS, "x0")
        cx.rmsnorm(xs, "xs", g, "g", un, "un", KC, T, B[6], "b6", D)
        for cc in range(KC):
            bp, bpk = cx.wload(bp_in[a, cc], 1024)
            if not is_pre:
                cp, cpk = cx.wload(cp_in[a, cc], 1024)
            pre_emitted = False
            for j in range(4):
                blk = cc * 4 + j
                k = it % 2
                it += 1
                tb = tabs[k]
                tk = ("tab", k)
                cx.dma("sp", tb.rearrange("p a t -> p (a t)"), TAB[blk], [("TAB", blk)], [tk], f"tr{k}")
                bre, bim = B[2 * k], B[2 * k + 1]
                if not pre_emitted:
                    cx.mm(bre[:], bp[:, j * 128:(j + 1) * 128], un[:, cc, :], True, True, [bpk, ("un", cc)], [("bre", k)])
                    cx.mm(bim[:], bp[:, 512 + j * 128:512 + (j + 1) * 128], un[:, cc, :], True, True, [bpk, ("un", cc)], [("bim", k)])
                pre_emitted = False
                if j + 1 < 4:
                    k2 = it % 2
                    cx.mm(B[2 * k2][:], bp[:, (j + 1) * 128:(j + 2) * 128], un[:, cc, :], True, True, [bpk, ("un", cc)], [("bre", k2)])
                    cx.mm(B[2 * k2 + 1][:], bp[:, 512 + (j + 1) * 128:512 + (j + 2) * 128], un[:, cc, :], True, True, [bpk, ("un", cc)], [("bim", k2)])
                    pre_emitted = True
                t = tmp
                cx.tt("dve", t["t1"], bre[:], tb[:, 0, :], ALU.mult, [("bre", k), tk], ["t1"])
                cx.tt("dve", t["t2"], bim[:], tb[:, 1, :], ALU.mult, [("bim", k), tk], ["t2"])
                cx.tt("dve", t["gir"], t["t1"], t["t2"], ALU.subtract, ["t1", "t2"], ["gir"])
                cx.tt("dve", t["t3"], bre[:], tb[:, 1, :], ALU.mult, [("bre", k), tk], ["t3"])
                cx.tt("dve", t["t4"], bim[:], tb[:, 0, :], ALU.mult, [("bim", k), tk], ["t4"])
                cx.tt("dve", t["gii"], t["t3"], t["t4"], ALU.add, ["t3", "t4"], ["gii"])
                hre_c, him_c = m["hre"][:, blk:blk + 1], m["him"][:, blk:blk + 1]
                c1, s1 = tb[:, 2, 1:2], tb[:, 3, 1:2]
                ir, ii = ini[k]
                cx.tt("dve", tiny[0], him_c, s1, ALU.mult, ["him", tk], ["tiny0"])
                cx.stt(ir, hre_c, c1, tiny[0], ALU.mult, ALU.subtract, ["hre", tk, "tiny0"], [("ir", k)])
                cx.tt("dve", tiny[1], hre_c, s1, ALU.mult, ["hre", tk], ["tiny1"])
                cx.stt(ii, him_c, c1, tiny[1], ALU.mult, ALU.add, ["him", tk, "tiny1"], [("ii", k)])
                cx.ts("dve", Rt, ones512, m["R"][:, blk:blk + 1], None, ALU.mult, None, ["ones512", "R"], ["Rt"])
                cx.scan(t["gr"], Rt, t["gir"], ir[:, 0:1], ["Rt", "gir", ("ir", k)], ["gr"])
                cx.scan(t["gi"], Rt, t["gii"], ii[:, 0:1], ["Rt", "gii", ("ii", k)], ["gi"])
                cl, sl = tb[:, 2, T - 1:T], tb[:, 3, T - 1:T]
                grl, gil = t["gr"][:, T - 1:T], t["gi"][:, T - 1:T]
                cx.tt("dve", tiny[2], gil, sl, ALU.mult, ["gi", tk], ["tiny2"])
                cx.stt(hre_c, grl, cl, tiny[2], ALU.mult, ALU.subtract, ["gr", tk, "tiny2"], ["hre"])
                cx.tt("dve", tiny[3], grl, sl, ALU.mult, ["gr", tk], ["tiny3"])
                cx.stt(him_c, gil, cl, tiny[3], ALU.mult, ALU.add, ["gi", tk, "tiny3"], ["him"])
                if is_pre:
                    continue
                hr, hn = hb[k]
                cx.tt("dve", t["t1"], t["gr"], tb[:, 2, :], ALU.mult, ["gr", tk], ["t1"])
                cx.tt("dve", t["t2"], t["gi"], tb[:, 3, :], ALU.mult, ["gi", tk], ["t2"])
                cx.tt("dve", hr, t["t1"], t["t2"], ALU.subtract, ["t1", "t2"], [("hr", k)])
                cx.tt("dve", t["t3"], t["gi"], tb[:, 2, :], ALU.mult, ["gi", tk], ["t3"])
                cx.tt("dve", t["t4"], t["gr"], tb[:, 3, :], ALU.mult, ["gr", tk], ["t4"])
                cx.stt(hn, t["t3"], -1.0, t["t4"], ALU.mult, ALU.subtract, ["t3", "t4"], [("hn", k)])
                yb = B[4 + (cc % 2)]
                ybk = ("yb", cc % 2)
                cx.mm(yb[:], cp[:, j * 128:(j + 1) * 128], hr, j == 0, False, [cpk, ("hr", k)], [ybk])
                cx.mm(yb[:], cp[:, 512 + j * 128:512 + (j + 1) * 128], hn, False, j == 3, [cpk, ("hn", k)], [ybk])
            if is_pre:
                continue
            yp, y2 = tmp["yp"], tmp["y2"]
            cx.stt(yp, un[:, cc, :], dsk[:, cc:cc + 1], yb[:], ALU.mult, ALU.add, [("un", cc), "dsk", ybk], ["yp"])
            cx.tt("dve", y2, yp, yp, ALU.mult, ["yp"], ["y2"])
            cx.ts("dve", y2, y2, 0.044715, 1.0, ALU.mult, ALU.add, ["y2"], ["y2"])
            cx.tt("dve", y2, y2, yp, ALU.mult, ["y2", "yp"], ["y2"])
            cx.act(y2, y2, AF.Sigmoid, ["y2"], ["y2"], scale=1.5957691216057308)
            cx.tt("dve", un[:, cc, :], yp, y2, ALU.mult, ["yp", "y2"], [("un", cc)])
        if is_pre:
            continue
        for dc in range(KC):
            k = dc % 2
            wa, wak = cx.wload(wg_in[a, dc, 0], KC * 128)
            wgt, wgk = cx.wload(wg_in[a, dc, 1], KC * 128)
            ba_, bg_ = B[2 * k], B[2 * k + 1]
            for c in range(KC):
                cx.mm(ba_[:], wa[:, c * 128:(c + 1) * 128], un[:, c, :], c == 0, c == KC - 1, [wak, ("un", c)], [("bre", k)])
            for c in range(KC):
                cx.mm(bg_[:], wgt[:, c * 128:(c + 1) * 128], un[:, c, :], c == 0, c == KC - 1, [wgk, ("un", c)], [("bim", k)])
            sgm = tmp["t1"] if k == 0 else tmp["t2"]
            sk = "t1" if k == 0 else "t2"
            cx.act(sgm, bg_[:], AF.Sigmoid, [("bim", k), "bgl"], [sk], bias=bgl[:, KC + dc:KC + dc + 1])
            cx.stt(sgm, ba_[:], bgl[:, dc:dc + 1], sgm, ALU.add, ALU.mult, [("bre", k), "bgl", sk], [sk])
            cx.tt("dve", xs[:, dc, :], xs[:, dc, :], sgm, ALU.add, [("xs", dc), sk], [("xs", dc)])
        cx.dma("sp", dstv[:, :, t0:t0 + T], xs, XS, [("XM", ti)], "y0")
        if ti == nt - 1:
            write_halo(cx, xs, XS, halo_in, halo_all)
    cx.end_phase()


class Rope:
    def __init__(self, cx, pos_in, rc_in, NT):
        self.cx = cx
        self.posi = cx.sb("posi", [64, NT], I32)
        self.rc = cx.sb("rc", [64, 2], F32)
        self.ang = cx.sb("r_ang", [64, T], F32)
        self.s1 = cx.sb("r_s1", [64, T], F32)
        self.s2 = cx.sb("r_s2", [64, T], F32)
        self.cosT = cx.sb("r_cos", [64, T], F32)
        self.sinS = cx.sb("r_sin", [64, T], F32)
        self.t1 = cx.sb("r_t1", [64, T], F32)
        self.t2 = cx.sb("r_t2", [64, T], F32)
        cx.dma("sp", self.posi, pos_in, [], ["posi"], "r0")
        cx.dma("sp", self.rc, rc_in, [], ["rc"], "r1")

    def tables(self, t0):
        cx = self.cx
        cx.copy("dve", self.ang, self.posi[:, t0:t0 + T], ["posi"], ["r_ang"])
        cx.ts("dve", self.ang, self.ang, self.rc[:, 0:1], None, ALU.mult, None, ["r_ang", "rc"], ["r_ang"])
        sincos(cx, self.ang, "r_ang", T, self.cosT, "r_cos", self.sinS, "r_sin", self.s1, "r_s1", self.s2, "r_s2")
        cx.ts("dve", self.sinS, self.sinS, self.rc[:, 1:2], None, ALU.mult, None, ["r_sin", "rc"], ["r_sin"])

    def apply(self, out, okeys, A, akey, SW, swkey):
        cx = self.cx
        cx.tt("dve", self.t1, A, self.cosT, ALU.mult, [akey, "r_cos"], ["r_t1"])
        cx.tt("dve", self.t2, SW, self.sinS, ALU.mult, [swkey, "r_sin"], ["r_t2"])
        cx.tt("dve", out, self.t1, self.t2, ALU.add, ["r_t1", "r_t2"], okeys)


def kv_phase(cx, NT, src, W, scr):
    B = cx.banks
    KnT_own, KrT_own, Vh_own = scr["KnT_own"], scr["KrT_own"], scr["Vh_own"]
    cx.wpool(4, KC * 128)
    xs = cx.sb("xs", [128, KC, T], F32)
    xn = cx.sb("xn", [128, KC, T], BF16)
    g = cx.sb("g_sb", [128, KC], F32)
    gkv = cx.sb("gkv_sb", [128, 4], F32)
    ckv = cx.sb("ckv", [128, 4, T], F32)
    cn = cx.sb("cn", [128, 4, T], BF16)
    kr = cx.sb("kr", [64, T], BF16)
    stage = [cx.sb(f"stage{i}", [128, T], BF16) for i in range(2)]
    rope = Rope(cx, W["pos"], W["rc"], NT)
    cx.dma("sp", g, W["kv_g"], [], ["g"], "c0")
    cx.dma("sp", gkv, W["kv_gkv"], [], ["gkv"], "c1")
    srcv = fm(src)
    Vv = Vh_own.rearrange("(h p) (kc d) -> p h kc d", p=128, d=128)
    it = 0
    for ti in range(NT // T):
        t0 = ti * T
        cx.dma("sp", xs, srcv[:, :, t0:t0 + T], [("XA", ti)], [("xs", c) for c in range(KC)], "x0")
        cx.rmsnorm(xs, "xs", g, "g", xn, "xn", KC, T, B[7], "b7", D)
        rope.tables(t0)
        for oc in range(6):
            w, wk = cx.wload(W["kv_wa"][oc], KC * 128)
            bk = B[oc % 2]
            for c in range(KC):
                cx.mm(bk[:], w[:, c * 128:(c + 1) * 128], xn[:, c, :], c == 0, c == KC - 1, [wk, ("xn", c)], [("b", oc % 2)])
            if oc < 4:
                cx.act(ckv[:, oc, :], bk[:], AF.Copy, [("b", oc % 2)], [("ckv", oc)])
        rope.apply(kr, ["kr"], B[0][0:64, :], ("b", 0), B[1][0:64, :], ("b", 1))
        cx.dma("sp", KrT_own[:, t0:t0 + T], kr, ["kr"], ["KrT_own"], "y0")
        cx.rmsnorm(ckv, "ckv", gkv, "gkv", cn, "cn", 4, T, B[7], "b7", 512)
        for h in range(NH):
            k = it % 2
            it += 1
            w, wk = cx.wload(W["kv_wuk"][h], 512)
            bk = B[2 + k]
            for c in range(4):
                cx.mm(bk[:], w[:, c * 128:(c + 1) * 128], cn[:, c, :], c == 0, c == 3, [wk, ("cn", c)], [("b2", k)])
            cx.act(stage[k], bk[:], AF.Copy, [("b2", k)], [("stage", k)])
            cx.dma("sp", KnT_own[h * 128:(h + 1) * 128, t0:t0 + T], stage[k], [("stage", k)], ["KnT_own"], f"y1{k}")
        for nb in range(16):
            w, wk = cx.wload(W["kv_wuv"][nb], 2048)
            for tq in range(4):
                k = it % 2
                it += 1
                bk = B[2 + k]
                for c in range(4):
                    cx.mm(bk[:], cn[:, c, tq * 128:(tq + 1) * 128], w[:, c * 512:(c + 1) * 512], c == 0, c == 3, [wk, ("cn", c)], [("b2", k)])
                cx.act(stage[k], bk[:], AF.Copy, [("b2", k)], [("stage", k)])
                cx.dma("sp", Vv[:, 4 * nb:4 * nb + 4, ti * 4 + tq, :], stage[k].rearrange("p (h d) -> p h d", d=128),
                       [("stage", k)], ["Vh_own"], f"y1{k}")
    cx.allgather(KrT_own, scr["KrT_all"], ["KrT_own"], ["KrT_all"], "ag_r")
    for i in range(16):
        cx.allgather(KnT_own[i * 512:(i + 1) * 512, :], scr["KnT_all"][i], ["KnT_own"], ["KnT_all"], "ag_k")
        cx.allgather(Vh_own[i * 512:(i + 1) * 512, :], scr["Vh_all"][i], ["Vh_own"], ["Vh_all"], "ag_v")
    cx.end_phase()


def mla_phase(cx, NT, src, dst, W, b, l, scr):
    S = cx.S
    B = cx.banks
    NS = 2 * NT
    NKC = NS // 128
    OKC = NT // 128
    KnT_own, KnT_all, Vh_own, Vh_all = scr["KnT_own"], scr["KnT_all"], scr["Vh_own"], scr["Vh_all"]
    cx.wpool(3, KC * 128)
    big = cx.sb("big", [128, KC * T], F32)
    xs = big.rearrange("p (c t) -> p c t", c=KC)
    obuf = big.bitcast(BF16).rearrange("p (h t) -> p h t", h=NH)
    xnb = cx.sb("xn", [128, KC * T], BF16)
    xn = xnb.rearrange("p (c t) -> p c t", c=KC)
    g = cx.sb("g_sb", [128, KC], F32)
    gq = cx.sb("gq_sb", [128, 8], F32)
    cqs = cx.sb("cqs", [128, 8, T], F32)
    cqn = cx.sb("cqn", [128, 8, T], BF16)
    krt = cx.sb("krt", [64, NS], BF16)
    kb = cx.sb("kb", [128, NKC], F32)
    masks = cx.sb("masks", [128, 4 * T], BF16)
    qn = [cx.sb(f"qn{i}", [128, T], BF16) for i in range(2)]
    qr = [cx.sb(f"qr{i}", [64, T], BF16) for i in range(2)]
    pt = [cx.sb(f"pt{i}", [128, T], BF16) for i in range(2)]
    rden = cx.sb("rden", [128, T], F32)
    xc = [cx.sb(f"xc{i}", [128, T], F32) for i in range(2)]
    rope = Rope(cx, W["pos"], W["rc"], NT)
    cx.dma("sp", g, W["mix_g"][l], [], ["g"], "c0")
    cx.dma("sp", gq, W["mla_gq"][b], [], ["gq"], "c1")
    cx.dma("sp", krt[:, 0:NT], scr["KrT_all"][0:64, :], ["KrT_all"], ["krt"], "c2")
    cx.dma("sp", krt[:, NT:NS], scr["KrT_own"], ["KrT_own"], ["krt"], "c5")
    cx.dma("sp", kb, W["kbias"], [], ["kb"], "c3")
    cx.dma("sp", masks, W["masks"], [], ["masks"], "c4")
    srcv, dstv = fm(src), fm(dst)
    XS = [("xs", c) for c in range(KC)]
    it = 0
    nt = NT // T
    for ti in range(nt):
        t0 = ti * T
        q0 = NS - NT + t0
        nkc = (q0 + T) // 128
        cx.dma("sp", xs, srcv[:, :, t0:t0 + T], [("XA", ti)], XS, "x0")
        cx.rmsnorm(xs, "xs", g, "g", xn, "xn", KC, T, B[7], "b7", D)
        rope.tables(t0)
        for oc in range(8):
            w, wk = cx.wload(W["mla_wqa"][b, oc], KC * 128)
            bk = B[4 + oc % 2]
            for c in range(KC):
                cx.mm(bk[:], w[:, c * 128:(c + 1) * 128], xn[:, c, :], c == 0, c == KC - 1, [wk, ("xn", c)], [("b4", oc % 2)])
            cx.act(cqs[:, oc, :], bk[:], AF.Copy, [("b4", oc % 2)], [("cqs", oc)])
        cx.rmsnorm(cqs, "cqs", gq, "gq", cqn, "cqn", 8, T, B[7], "b7", 1024)
        for h in range(NH):
            hs = h % 2
            w, wk = cx.wload(W["mla_wqb"][b, h], 2048)
            for (bk, bkey, c0, mcols) in ((B[4], ("b4", 0), 0, 128), (B[5], ("b4", 1), 128, 64), (B[6], "q6", 192, 64)):
                for c in range(8):
                    cx.mm(bk[0:mcols, :], w[:, c * 256 + c0:c * 256 + c0 + mcols], cqn[:, c, :], c == 0, c == 7, [wk, ("cqn", c)], [bkey])
            cx.act(qn[hs], B[4][:], AF.Copy, [("b4", 0)], [("qn", hs)])
            rope.apply(qr[hs], [("qr", hs)], B[5][0:64, :], ("b4", 1), B[6][0:64, :], "q6")
            kvk = [("xn", c) for c in range(hs * 16, hs * 16 + 16)]
            kT = xnb[:, hs * 8192:hs * 8192 + nkc * 128]
            vv = xnb[:, hs * 8192 + 4096:hs * 8192 + 4096 + nkc * 128]
            nown = (nkc - OKC) * 128
            cx.dma("sp", kT[:, 0:NT], KnT_all[h // 4, (h % 4) * 128:(h % 4 + 1) * 128, :], ["KnT_all"], kvk, f"k{hs}")
            cx.dma("sp", kT[:, NT:NT + nown], KnT_own[h * 128:(h + 1) * 128, 0:nown], ["KnT_own"], kvk, f"k{hs}b")
            cx.dma("sp", vv[:, 0:NT], Vh_all[h // 4, (h % 4) * 128:(h % 4 + 1) * 128, :], ["Vh_all"], kvk, f"v{hs}")
            cx.dma("sp", vv[:, NT:NT + nown], Vh_own[h * 128:(h + 1) * 128, 0:nown], ["Vh_own"], kvk, f"v{hs}b")
            def scores(kc_, k_):
                cx.mm(B[k_][:], kT[:, kc_ * 128:(kc_ + 1) * 128], qn[hs], True, False, kvk + [("qn", hs)], [("s", k_)])
                cx.mm(B[k_][:], krt[:, kc_ * 128:(kc_ + 1) * 128], qr[hs], False, True, ["krt", ("qr", hs)], [("s", k_)])

            scores(0, it % 2)
            for kc in range(nkc):
                k = it % 2
                it += 1
                sb_ = B[k]
                if kc + 1 < nkc:
                    scores(kc + 1, it % 2)
                cx.act(pt[k], sb_[:], AF.Exp, [("s", k), "kb"], [("pt", k)], scale=SCALE, bias=kb[:, kc:kc + 1])
                off = kc * 128 - q0
                if off >= 0:
                    mi = off // 128
                    cx.tt("dve", pt[k], pt[k], masks[:, mi * T:(mi + 1) * T], ALU.mult, [("pt", k), "masks"], [("pt", k)])
                cx.mm(B[2][:], vv[:, kc * 128:(kc + 1) * 128], pt[k], kc == 0, kc == nkc - 1, kvk + [("pt", k)], ["o"])
                cx.mm(B[3][:], cx.ones_bf[:], pt[k], kc == 0, kc == nkc - 1, ["ones", ("pt", k)], ["den"])
            cx.recip(rden, B[3][:], ["den"], ["rden"])
            cx.tt("dve", obuf[:, h, :], B[2][:], rden, ALU.mult, ["o", "rden"], [("xs", h // 2)])
        for dc in range(KC):
            k = dc % 2
            bk = B[4 + k]
            for hh in range(2):
                w, wk = cx.wload(W["mla_wo"][b, dc][:, hh * 4096:(hh + 1) * 4096], 4096)
                for h2 in range(32):
                    h = hh * 32 + h2
                    cx.mm(bk[:], w[:, h2 * 128:(h2 + 1) * 128], obuf[:, h, :], h == 0, h == NH - 1, [wk, ("xs", h // 2)], [("b4", k)])
            cx.dma("sp", xc[k], srcv[:, dc, t0:t0 + T], [("XA", ti)], [("xc", k)], f"xc{k}")
            cx.tt("dve", xc[k], xc[k], bk[:], ALU.add, [("xc", k), ("b4", k)], [("xc", k)])
            cx.dma("sp", dstv[:, dc, t0:t0 + T], xc[k], [("xc", k)], [("XM", ti)], f"yo{k}")
            if ti == nt - 1:
                cx.dma("sp", fm(scr["HALO_in"])[:, dc, :], xc[k][:, T - 2:T], [("xc", k)], ["HALO_in"], f"hw{k}")
    cx.allgather(scr["HALO_in"], scr["HALO_all"], ["HALO_in"], ["HALO_all"], "ag_h")
    cx.end_phase()


def _scratch(cx, NT):
    return {
        "XA": cx.dscr("XA", [D, NT]), "XM": cx.dscr("XM", [D, NT]),
        "TAB": cx.dscr("TAB", [NBLK, 128, 4 * T]),
        "ST_in": cx.dscr("ST_in", [256, NBLK]), "ST_all": cx.dscr("ST_all", [512, NBLK]),
        "HALO_in": cx.dscr("HALO_in", [D, 2]), "HALO_all": cx.dscr("HALO_all", [2 * D, 2]),
    }


def _ffn_inputs(cx, W):
    W["ffn_g"] = cx.din("ffn_g", [1, 128, KC])
    W["gfin"] = cx.din("gfin", [128, KC])
    W["ffn_w_in"] = cx.din("ffn_w_in", [1, FC, 2, 128, KC * 128])
    W["ffn_w_out"] = cx.din("ffn_w_out", [1, KC, 2, 128, HF * 128])
    W["ffn_cw"] = cx.din("ffn_cw", [1, 128, FC * 4])


def build_s5_layer(NT):
    cx = Ctx()
    W = {}
    W["xT"] = cx.din("xT", [D, NT])
    W["flag"] = cx.din("flag", [128, 1])
    W["iota"] = cx.din("iota", [128, T])
    W["mix_g"] = cx.din("mix_g", [1, 128, KC])
    _ffn_inputs(cx, W)
    W["s5_ap_re"] = cx.din("s5_ap_re", [1, 128, NBLK])
    W["s5_ap_im"] = cx.din("s5_ap_im", [1, 128, NBLK])
    W["s5_ldt"] = cx.din("s5_ldt", [1, 128, NBLK])
    W["s5_bpad"] = cx.din("s5_bpad", [1, KC, 128, 1024])
    W["s5_cpad"] = cx.din("s5_cpad", [1, KC, 128, 1024])
    W["s5_dsk"] = cx.din("s5_dsk", [1, 128, KC])
    W["s5_wglu"] = cx.din("s5_wglu", [1, KC, 2, 128, KC * 128])
    W["s5_bgl"] = cx.din("s5_bgl", [1, 128, 2 * KC])
    yT = cx.dout("yT", [D, NT])
    scr = _scratch(cx, NT)
    cx.setup()
    cx.dma("sp", cx.flag[:], W["flag"], [], ["flag"], "fl")
    s5_phase(cx, NT, W["xT"], scr["XM"], W, 0, 0, scr)
    ffn_phase(cx, NT, scr["XM"], yT, scr["HALO_all"], W, 0, False, True, "XA")
    return cx.finish()


def build_mla_layer(NT):
    cx = Ctx()
    W = {}
    W["xT"] = cx.din("xT", [D, NT])
    W["x1T"] = cx.din("x1T", [D, NT])
    W["pos"] = cx.din("pos", [64, NT], I32)
    W["flag"] = cx.din("flag", [128, 1])
    W["fnflag"] = cx.din("fnflag", [128, 1])
    W["kbias"] = cx.din("kbias", [128, 2 * NT // 128])
    W["masks"] = cx.din("masks", [128, 4 * T], BF16)
    W["rc"] = cx.din("rc", [64, 2])
    W["mix_g"] = cx.din("mix_g", [1, 128, KC])
    _ffn_inputs(cx, W)
    W["kv_g"] = cx.din("kv_g", [128, KC])
    W["kv_gkv"] = cx.din("kv_gkv", [128, 4])
    W["kv_wa"] = cx.din("kv_wa", [6, 128, KC * 128])
    W["kv_wuk"] = cx.din("kv_wuk", [NH, 128, 512])
    W["kv_wuv"] = cx.din("kv_wuv", [16, 128, 2048])
    W["mla_gq"] = cx.din("mla_gq", [1, 128, 8])
    W["mla_wqa"] = cx.din("mla_wqa", [1, 8, 128, KC * 128])
    W["mla_wqb"] = cx.din("mla_wqb", [1, NH, 128, 2048])
    W["mla_wo"] = cx.din("mla_wo", [1, KC, 128, NH * 128])
    yT = cx.dout("yT", [D, NT])
    scr = _scratch(cx, NT)
    scr.update({
        "KnT_own": cx.dscr("KnT_own", [NH * 128, NT], BF16), "KnT_all": cx.dscr("KnT_all", [16, 1024, NT], BF16),
        "KrT_own": cx.dscr("KrT_own", [64, NT], BF16), "KrT_all": cx.dscr("KrT_all", [128, NT], BF16),
        "Vh_own": cx.dscr("Vh_own", [NH * 128, NT], BF16), "Vh_all": cx.dscr("Vh_all", [16, 1024, NT], BF16),
    })
    cx.setup()
    cx.dma("sp", cx.flag[:], W["flag"], [], ["flag"], "fl")
    cx.dma("sp", cx.fnflag[:], W["fnflag"], [], ["fnflag"], "fl2")
    kv_phase(cx, NT, W["x1T"], W, scr)
    mla_phase(cx, NT, W["xT"], scr["XM"], W, 0, 0, scr)
    ffn_phase(cx, NT, scr["XM"], yT, scr["HALO_all"], W, 0, True, True, "XA")
    return cx.finish()


def tile_w(w, cols_list):
    K = w.shape[0]
    out = []
    for cols in cols_list:
        sub = w[:, cols]
        m = sub.shape[1]
        out.append(np.ascontiguousarray(sub.reshape(K // 128, 128, m).transpose(1, 0, 2)).reshape(128, (K // 128) * m))
    return np.stack(out)


def pvec(v, n):
    return np.ascontiguousarray(np.asarray(v, np.float32).reshape(n, 128).T)


def prep_ffn(w_in, conv_w, conv_b, w_out):
    wi = w_in.reshape(KC, 128, 2, FC, 128)
    wi = np.ascontiguousarray(wi.transpose(3, 2, 1, 0, 4)).reshape(FC, 2, 128, KC * 128)
    wo = w_out.reshape(2, HF, 128, KC, 128)
    wo = np.ascontiguousarray(wo.transpose(3, 0, 2, 1, 4)).reshape(KC, 2, 128, HF * 128)
    cw = np.concatenate([conv_w, conv_b[None, :]], axis=0)
    cw = np.ascontiguousarray(cw.reshape(4, FC, 128).transpose(2, 1, 0)).reshape(128, FC * 4)
    return wi, wo, cw


def prep_s5(A_re, A_im, log_dt, B_re, B_im, C_re, C_im, D_skip, w_glu, b_glu):
    def pp(a):
        return np.ascontiguousarray(a.reshape(NBLK, 2, 64).transpose(1, 2, 0).reshape(128, NBLK))
    bpad = np.zeros((KC, 128, 2, 4, 128), np.float32)
    cpad = np.zeros((KC, 128, 2, 4, 128), np.float32)
    for part, (Bm, Cm) in enumerate([(B_re, C_re), (B_im, C_im)]):
        Bv = Bm.reshape(KC, 4, 2, 64, 16)
        Cv = Cm.reshape(KC, 4, 2, 16, 64)
        for j in range(4):
            for gl in range(2):
                r0 = 32 * j + 16 * gl
                bpad[:, r0:r0 + 16, part, j, gl * 64:(gl + 1) * 64] = Bv[:, j, gl].transpose(0, 2, 1)
                cpad[:, gl * 64:(gl + 1) * 64, part, j, r0:r0 + 16] = Cv[:, j, gl].transpose(0, 2, 1)
    wg = w_glu.reshape(KC, 128, 2, KC, 128)
    wg = np.ascontiguousarray(wg.transpose(3, 2, 1, 0, 4)).reshape(KC, 2, 128, KC * 128)
    return dict(ap_re=pp(A_re), ap_im=pp(A_im), ldt=pp(np.repeat(log_dt[:, None], 64, axis=1)),
                bpad=bpad.reshape(KC, 128, 1024), cpad=cpad.reshape(KC, 128, 1024),
                dsk=pvec(D_skip, KC), wglu=wg, bgl=pvec(b_glu, 2 * KC))


def kernel(x, positions, ln_mix, ln_ffn, ln_final,
           ssm_A_re, ssm_A_im, ssm_log_dt, ssm_B_re, ssm_B_im, ssm_C_re, ssm_C_im,
           ssm_D, ssm_w_glu, ssm_b_glu,
           kv_in_norm, w_kv_a, kv_latent_norm, w_kv_b,
           w_q_a, q_latent_norm, w_q_b, w_o,
           ffn_w_in, ffn_conv_w, ffn_conv_b, ffn_w_out):
    f32 = np.float32
    A = lambda v: np.asarray(v, f32)
    NT = 2048
    x = A(x)
    positions = np.asarray(positions, np.int32)
    W = {}
    W["mix_g"] = np.stack([pvec(ln_mix[l], KC) for l in range(4)])
    W["ffn_g"] = np.stack([pvec(ln_ffn[l], KC) for l in range(4)])
    W["gfin"] = pvec(ln_final, KC)
    ff = [prep_ffn(A(ffn_w_in[l]), A(ffn_conv_w[l]), A(ffn_conv_b[l]), A(ffn_w_out[l])) for l in range(4)]
    W["ffn_w_in"] = np.stack([f[0] for f in ff])
    W["ffn_w_out"] = np.stack([f[1] for f in ff])
    W["ffn_cw"] = np.stack([f[2] for f in ff])
    del ff
    s5 = [prep_s5(A(ssm_A_re[a]), A(ssm_A_im[a]), A(ssm_log_dt[a]), A(ssm_B_re[a]), A(ssm_B_im[a]), A(ssm_C_re[a]),
                  A(ssm_C_im[a]), A(ssm_D[a]), A(ssm_w_glu[a]), A(ssm_b_glu[a])) for a in range(2)]
    for k_, n_ in [("ap_re", "s5_ap_re"), ("ap_im", "s5_ap_im"), ("ldt", "s5_ldt"), ("bpad", "s5_bpad"), ("cpad", "s5_cpad"),
                   ("dsk", "s5_dsk"), ("wglu", "s5_wglu"), ("bgl", "s5_bgl")]:
        W[n_] = np.stack([s5[a][k_] for a in range(2)])
    del s5
    wkva = A(w_kv_a)
    wa_ext = np.concatenate([wkva, np.zeros((D, 64), f32)], axis=1)
    cols = [np.arange(oc * 128, (oc + 1) * 128) for oc in range(4)]
    cols.append(np.concatenate([np.arange(512, 576), np.arange(576, 640)]))
    cols.append(np.concatenate([np.arange(544, 576), np.arange(512, 544), np.arange(576, 640)]))
    W["kv_wa"] = tile_w(wa_ext, cols)
    wkv = A(w_kv_b).reshape(512, NH, 256)
    w_uk = np.ascontiguousarray(wkv[:, :, :128]).reshape(512, NH * 128)
    w_uv = np.ascontiguousarray(wkv[:, :, 128:]).reshape(512, NH * 128)
    W["kv_wuk"] = tile_w(w_uk, [np.arange(h * 128, (h + 1) * 128) for h in range(NH)])
    W["kv_wuv"] = tile_w(w_uv, [np.arange(nb * 512, (nb + 1) * 512) for nb in range(16)])
    W["kv_g"] = pvec(kv_in_norm, KC)
    W["kv_gkv"] = pvec(kv_latent_norm, 4)
    qcols = []
    for h in range(NH):
        b0 = h * 192
        qcols.append(np.concatenate([np.arange(b0, b0 + 192), np.arange(b0 + 160, b0 + 192), np.arange(b0 + 128, b0 + 160)]))
    W["mla_gq"] = np.stack([pvec(q_latent_norm[b], 8) for b in range(2)])
    W["mla_wqa"] = np.stack([tile_w(A(w_q_a[b]), [np.arange(oc * 128, (oc + 1) * 128) for oc in range(8)]) for b in range(2)])
    W["mla_wqb"] = np.stack([tile_w(A(w_q_b[b]), qcols) for b in range(2)])
    W["mla_wo"] = np.stack([tile_w(A(w_o[b]), [np.arange(dc * 128, (dc + 1) * 128) for dc in range(KC)]) for b in range(2)])
    invf = (10000.0 ** (-np.arange(32, dtype=np.float32) / 32)).astype(f32)
    rc = np.zeros((64, 2), f32)
    rc[:, 0] = np.concatenate([invf, invf])
    rc[:32, 1] = -1.0
    rc[32:, 1] = 1.0
    W["rc"] = rc
    W["iota"] = np.ascontiguousarray(np.broadcast_to(np.arange(T, dtype=f32), (128, T)))
    p_ = np.arange(128)[:, None]
    q_ = np.arange(T)[None, :]
    W["masks"] = np.concatenate([((q_ - p_) >= mi * 128) for mi in range(4)], axis=1).astype(ml_dtypes.bfloat16)
    cores = [(b, h) for b in range(4) for h in range(2)]
    ids = list(range(8))
    flags = [np.full((128, 1), float(h), f32) for (b, h) in cores]
    xs = [np.ascontiguousarray(x[b, h * NT:(h + 1) * NT].T) for (b, h) in cores]
    nc_s5 = build_s5_layer(NT)
    for l in range(2):
        Wl = {"iota": W["iota"], "gfin": W["gfin"], "mix_g": W["mix_g"][l:l + 1], "ffn_g": W["ffn_g"][l:l + 1],
              "ffn_w_in": W["ffn_w_in"][l:l + 1], "ffn_w_out": W["ffn_w_out"][l:l + 1], "ffn_cw": W["ffn_cw"][l:l + 1]}
        for n_ in ["s5_ap_re", "s5_ap_im", "s5_ldt", "s5_bpad", "s5_cpad", "s5_dsk", "s5_wglu", "s5_bgl"]:
            Wl[n_] = W[n_][l:l + 1]
        maps = [dict(Wl, xT=xs[i], flag=flags[i]) for i in range(8)]
        res = run_bass_kernel_spmd(nc_s5, maps, core_ids=ids).results
        xs = [np.ascontiguousarray(r["yT"]) for r in res]
        del maps, res, Wl
    x1 = xs
    nc_mla = build_mla_layer(NT)
    for l in range(2, 4):
        b_ = l - 2
        Wl = {"gfin": W["gfin"], "mix_g": W["mix_g"][l:l + 1], "ffn_g": W["ffn_g"][l:l + 1],
              "ffn_w_in": W["ffn_w_in"][l:l + 1], "ffn_w_out": W["ffn_w_out"][l:l + 1], "ffn_cw": W["ffn_cw"][l:l + 1],
              "rc": W["rc"], "masks": W["masks"], "kv_g": W["kv_g"], "kv_gkv": W["kv_gkv"], "kv_wa": W["kv_wa"],
              "kv_wuk": W["kv_wuk"], "kv_wuv": W["kv_wuv"], "mla_gq": W["mla_gq"][b_:b_ + 1], "mla_wqa": W["mla_wqa"][b_:b_ + 1],
              "mla_wqb": W["mla_wqb"][b_:b_ + 1], "mla_wo": W["mla_wo"][b_:b_ + 1]}
        fn = np.full((128, 1), 1.0 if l == 3 else 0.0, f32)
        maps = []
        for i, (b, h) in enumerate(cores):
            m = dict(Wl, xT=xs[i], x1T=x1[i], flag=flags[i], fnflag=fn)
            m["pos"] = np.ascontiguousarray(np.broadcast_to(positions[b, h * NT:(h + 1) * NT], (64, NT)))
            bias = np.zeros((2 * NT,), f32)
            if h == 0:
                bias[:NT] = -80.0
            m["kbias"] = np.ascontiguousarray(bias.reshape(2 * NT // 128, 128).T)
            maps.append(m)
        res = run_bass_kernel_spmd(nc_mla, maps, core_ids=ids).results
        xs = [np.ascontiguousarray(r["yT"]) for r in res]
        del maps, res, Wl
    out = np.empty((4, 2 * NT, D), f32)
    for i, (b, h) in enumerate(cores):
        out[b, h * NT:(h + 1) * NT] = xs[i].T
    return out
```
